# Optimizing a Trainium2 kernel written in Bass

```python
import jax, jax.numpy as jnp
from jax import lax
import numpy as np


D_MODEL = 2048
BATCH = 2
SEQ = 4096
DEPTH = 1

CTX_LEN = 256
GRID_W = 64
D_CONV = D_MODEL // 2
CONV_WIDTH = 31
CONV_PAD = (CONV_WIDTH - 1) // 2
D_HGRN = D_MODEL // 2
HGRN_HEAD_DIM = 128
HGRN_HEADS = D_HGRN // HGRN_HEAD_DIM
HGRN_CHUNK = 64
D_FF = 4 * D_MODEL
N_MOD = 6
EPS = 1e-6
D_IN = 5 * D_HGRN + 2 * D_CONV + 2 * D_MODEL

kernel_name = 'hybrid_conformer_hgrn2_diffusion_block'


def rms_norm(x, g):
    xf = x.astype(jnp.float32)
    y = xf * lax.rsqrt(jnp.mean(xf * xf, axis=-1, keepdims=True) + EPS)
    return (y * g.astype(jnp.float32)).astype(x.dtype)


def layer_norm(x, g, b):
    xf = x.astype(jnp.float32)
    mu = jnp.mean(xf, axis=-1, keepdims=True)
    var = jnp.mean(jnp.square(xf - mu), axis=-1, keepdims=True)
    y = (xf - mu) * lax.rsqrt(var + EPS)
    return (y * g.astype(jnp.float32) + b.astype(jnp.float32)).astype(x.dtype)


def modulate(h, shift, scale):
    return h * (1 + scale) + shift


def split_proj(p):
    cuts = [D_HGRN, 2 * D_HGRN, 3 * D_HGRN, 4 * D_HGRN, 5 * D_HGRN,
            5 * D_HGRN + D_CONV, 5 * D_HGRN + 2 * D_CONV, 5 * D_HGRN + 2 * D_CONV + D_MODEL]
    return jnp.split(p, cuts, axis=-1)


def dwconv1d(u, w, b):
    ch = u.shape[-1]
    y = lax.conv_general_dilated(u, w[:, None, :].astype(u.dtype), window_strides=(1,),
                                 padding=[(CONV_PAD, CONV_PAD)],
                                 dimension_numbers=('NWC', 'WIO', 'NWC'),
                                 feature_group_count=ch)
    return y + b


def axial_dwconv(u, w, b, rows):
    bsz, seq, ch = u.shape
    half = ch // 2
    uh = u[..., :half].reshape(bsz * rows, GRID_W, half)
    yh = dwconv1d(uh, w[:, :half], b[:half]).reshape(bsz, seq, half)
    uv = u[..., half:].reshape(bsz, rows, GRID_W, ch - half).transpose(0, 2, 1, 3)
    uv = uv.reshape(bsz * GRID_W, rows, ch - half)
    yv = dwconv1d(uv, w[:, half:], b[half:]).reshape(bsz, GRID_W, rows, ch - half)
    yv = yv.transpose(0, 2, 1, 3).reshape(bsz, seq, ch - half)
    return jnp.concatenate([yh, yv], axis=-1)


def conformer_branch(val, gate, w_dw, b_dw, ln_g, ln_b, w_pw, rows):
    u = val * jax.nn.sigmoid(gate)
    u = dwconv1d(u, w_dw, b_dw) if rows is None else axial_dwconv(u, w_dw, b_dw, rows)
    u = jax.nn.silu(layer_norm(u, ln_g, ln_b))
    return u @ w_pw


def to_heads(t):
    return t.reshape(t.shape[:-1] + (HGRN_HEADS, HGRN_HEAD_DIM))


def hgrn_forget(f_raw, lb):
    f = lb + (1.0 - lb) * jax.nn.sigmoid(f_raw.astype(jnp.float32))
    return to_heads(jnp.log(f)), to_heads(1.0 - f)


def q_prep(p_q):
    return to_heads(jax.nn.silu(p_q.astype(jnp.float32))) * (HGRN_HEAD_DIM ** -0.5)


def hgrn_chunk_scan(q, k, v, logf, s0):
    bsz, seq = q.shape[:2]
    n_chunks = seq // HGRN_CHUNK

    def to_chunks(t):
        return t.reshape(bsz, n_chunks, HGRN_CHUNK, HGRN_HEADS, t.shape[-1]).transpose(1, 0, 3, 2, 4)

    tri = jnp.tril(jnp.ones((HGRN_CHUNK, HGRN_CHUNK), dtype=bool))

    def step(s, xs):
        qc, kc, vc, gc = xs
        a = jnp.cumsum(gc, axis=2)
        a_last = a[:, :, -1:, :]
        inter = jnp.einsum('bhtk,bhkv->bhtv', qc * jnp.exp(a), s)
        diff = a[:, :, :, None, :] - a[:, :, None, :, :]
        decay = jnp.exp(jnp.where(tri[:, :, None], diff, -jnp.inf))
        scores = jnp.einsum('bhtk,bhsk,bhtsk->bhts', qc, kc, decay)
        intra = jnp.einsum('bhts,bhsv->bhtv', scores, vc)
        s_new = jnp.exp(a_last[:, :, 0, :])[..., None] * s + \
            jnp.einsum('bhsk,bhsv->bhkv', kc * jnp.exp(a_last - a), vc)
        return s_new, inter + intra

    s_fin, out = lax.scan(step, s0, (to_chunks(q), to_chunks(k), to_chunks(v), to_chunks(logf)))
    out = out.transpose(1, 0, 3, 2, 4).reshape(bsz, seq, HGRN_HEADS, HGRN_HEAD_DIM)
    return out, s_fin


def hgrn_final_state(k, v, logf):
    a = jnp.cumsum(logf, axis=1)
    return jnp.einsum('blhk,blhv->bhkv', k * jnp.exp(a[:, -1:] - a), v)


def hgrn_bidir(q, v, gf, gb, s0f, s0b):
    o_f, s_f = hgrn_chunk_scan(q, gf[1], v, gf[0], s0f)
    fl = lambda t: jnp.flip(t, axis=1)
    o_b, s_b = hgrn_chunk_scan(fl(q), fl(gb[1]), fl(v), fl(gb[0]), s0b)
    return o_f + fl(o_b), s_f, s_b


def hgrn_readout(o, g, norm_g, w_o):
    o = rms_norm(o, norm_g.reshape(HGRN_HEADS, HGRN_HEAD_DIM))
    o = o.reshape(o.shape[:2] + (D_HGRN,)).astype(g.dtype) * jax.nn.silu(g)
    return o @ w_o


def gated_merge(y_conv, y_hgrn, p_gc, p_gh, w_o):
    return (jax.nn.sigmoid(p_gc) * y_conv + jax.nn.sigmoid(p_gh) * y_hgrn) @ w_o


def sq_relu_mlp(h, w1, w2):
    return jnp.square(jax.nn.relu(h @ w1)) @ w2


def setup_inputs(seed: int = 0) -> dict:
    key = jax.random.key(seed)
    ks = jax.random.split(key, 24)
    f32 = jnp.float32

    def nrm(k, shape, scale):
        return jax.random.normal(k, shape, f32) * scale

    def gain(k, shape):
        return 1.0 + 0.05 * jax.random.normal(k, shape, f32)

    return {
        'x': nrm(ks[0], (BATCH, SEQ, D_MODEL), 1.0),
        'c': nrm(ks[1], (BATCH, D_MODEL), 1.0),
        'ctx': nrm(ks[2], (BATCH, CTX_LEN, D_MODEL), 1.0),
        'c_ctx': nrm(ks[3], (D_MODEL,), 1.0),
        'w_mod': nrm(ks[4], (DEPTH, D_MODEL, N_MOD * D_MODEL), 0.5 * D_MODEL ** -0.5),
        'b_mod': nrm(ks[5], (DEPTH, N_MOD * D_MODEL), 0.02),
        'norm_pre_mix': gain(ks[6], (DEPTH, D_MODEL)),
        'norm_post_mix': gain(ks[7], (DEPTH, D_MODEL)),
        'norm_pre_mlp': gain(ks[8], (DEPTH, D_MODEL)),
        'norm_post_mlp': gain(ks[9], (DEPTH, D_MODEL)),
        'w_in': nrm(ks[10], (DEPTH, D_MODEL, D_IN), D_MODEL ** -0.5),
        'conv_dw_w': nrm(ks[11], (DEPTH, CONV_WIDTH, D_CONV), CONV_WIDTH ** -0.5),
        'conv_dw_b': nrm(ks[12], (DEPTH, D_CONV), 0.02),
        'conv_ln_g': gain(ks[13], (DEPTH, D_CONV)),
        'conv_ln_b': nrm(ks[14], (DEPTH, D_CONV), 0.02),
        'conv_pw_w': nrm(ks[15], (DEPTH, D_CONV, D_MODEL), D_CONV ** -0.5),
        'hgrn_lb_logits': nrm(ks[16], (2, DEPTH + 1, D_HGRN), 0.5),
        'hgrn_norm_g': gain(ks[17], (DEPTH, D_HGRN)),
        'hgrn_out_w': nrm(ks[18], (DEPTH, D_HGRN, D_MODEL), D_HGRN ** -0.5),
        'w_out': nrm(ks[19], (DEPTH, D_MODEL, D_MODEL), D_MODEL ** -0.5),
        'mlp_w1': nrm(ks[20], (DEPTH, D_MODEL, D_FF), D_MODEL ** -0.5),
        'mlp_w2': nrm(ks[21], (DEPTH, D_FF, D_MODEL), D_FF ** -0.5),
    }


def reference(x, c, ctx, c_ctx, w_mod, b_mod, norm_pre_mix, norm_post_mix, norm_pre_mlp, norm_post_mlp,
              w_in, conv_dw_w, conv_dw_b, conv_ln_g, conv_ln_b, conv_pw_w, hgrn_lb_logits, hgrn_norm_g,
              hgrn_out_w, w_out, mlp_w1, mlp_w2):
    f32 = jnp.float32
    rows = x.shape[1] // GRID_W
    lb_all = jnp.cumsum(jax.nn.softmax(hgrn_lb_logits.astype(f32), axis=1), axis=1)
    xc = ctx
    for l in range(DEPTH):
        last = l == DEPTH - 1
        mod = jax.nn.silu(c) @ w_mod[l] + b_mod[l]
        sh_m, sc_m, ga_m, sh_f, sc_f, ga_f = [m[:, None, :] for m in jnp.split(mod, N_MOD, axis=-1)]
        mod_c = jax.nn.silu(c_ctx) @ w_mod[l] + b_mod[l]
        shc_m, scc_m, gac_m, shc_f, scc_f, gac_f = jnp.split(mod_c, N_MOD, axis=-1)
        lb_f, lb_b = lb_all[0, l], lb_all[1, l]

        hc = modulate(rms_norm(xc, norm_pre_mix[l]), shc_m, scc_m)
        if last:
            pc_i, pc_ff, pc_fb = jnp.split(hc @ w_in[l][:, :3 * D_HGRN], 3, axis=-1)
            vc = to_heads(pc_i.astype(f32))
            lfc_f, kc_f = hgrn_forget(pc_ff, lb_f)
            lfc_b, kc_b = hgrn_forget(pc_fb, lb_b)
            s_f = hgrn_final_state(kc_f, vc, lfc_f)
            s_b = hgrn_final_state(jnp.flip(kc_b, 1), jnp.flip(vc, 1), jnp.flip(lfc_b, 1))
        else:
            pc = split_proj(hc @ w_in[l])
            vc = to_heads(pc[0].astype(f32))
            zeros = jnp.zeros((xc.shape[0], HGRN_HEADS, HGRN_HEAD_DIM, HGRN_HEAD_DIM), f32)
            oc, s_f, s_b = hgrn_bidir(q_prep(pc[3]), vc, hgrn_forget(pc[1], lb_f),
                                      hgrn_forget(pc[2], lb_b), zeros, zeros)
            yc_h = hgrn_readout(oc, pc[4], hgrn_norm_g[l], hgrn_out_w[l])
            yc_c = conformer_branch(pc[5], pc[6], conv_dw_w[l], conv_dw_b[l], conv_ln_g[l], conv_ln_b[l],
                                    conv_pw_w[l], None)
            yc = gated_merge(yc_c, yc_h, pc[7], pc[8], w_out[l])
            xc_next = xc + gac_m * rms_norm(yc, norm_post_mix[l])
            hc2 = modulate(rms_norm(xc_next, norm_pre_mlp[l]), shc_f, scc_f)
            xc_next = xc_next + gac_f * rms_norm(sq_relu_mlp(hc2, mlp_w1[l], mlp_w2[l]), norm_post_mlp[l])

        h = modulate(rms_norm(x, norm_pre_mix[l]), sh_m, sc_m)
        p_i, p_ff, p_fb, p_q, p_g, p_cv, p_cg, p_gc, p_gh = split_proj(h @ w_in[l])
        o, _, _ = hgrn_bidir(q_prep(p_q), to_heads(p_i.astype(f32)), hgrn_forget(p_ff, lb_f),
                             hgrn_forget(p_fb, lb_b), s_f, s_b)
        y_h = hgrn_readout(o, p_g, hgrn_norm_g[l], hgrn_out_w[l])
        y_c = conformer_branch(p_cv, p_cg, conv_dw_w[l], conv_dw_b[l], conv_ln_g[l], conv_ln_b[l],
                               conv_pw_w[l], rows)
        y = gated_merge(y_c, y_h, p_gc, p_gh, w_out[l])
        x = x + ga_m * rms_norm(y, norm_post_mix[l])
        h2 = modulate(rms_norm(x, norm_pre_mlp[l]), sh_f, sc_f)
        x = x + ga_f * rms_norm(sq_relu_mlp(h2, mlp_w1[l], mlp_w2[l]), norm_post_mlp[l])

        if not last:
            xc = xc_next
    return x
```

```python
import contextlib
import os
import numpy as np
import concourse.bass as bass
import concourse.mybir as mybir
from concourse.bass_utils import run_bass_kernel_spmd

F32 = mybir.dt.float32
BF16 = mybir.dt.bfloat16
AF = mybir.ActivationFunctionType
ALU = mybir.AluOpType

D = 2048
T = 1024
DH = 1024
DC = 1024
DFF = 8192
EPS = 1e-6
NHALO = 1920
C_I, C_FF, C_FB, C_Q, C_G, C_CV, C_CG, C_GC, C_GH = 0, 1024, 2048, 3072, 4096, 5120, 6144, 7168, 9216

PL = {}
_off = 0
for _n, _w in [("npm", 16), ("npo", 16), ("npl", 16), ("npo2", 16), ("bmod", 96), ("cw", 8 * 31),
               ("cb", 8), ("lng", 8), ("lnb", 8), ("hng", 8), ("l0", 40), ("l1", 40),
               ("slk", 3), ("sla", 3), ("slb", 3), ("hm", 2), ("cp", 32)]:
    PL[_n] = (_off, _w)
    _off += _w
NP = _off


class Sched:
    COMPUTE = ("pe", "act", "dve", "pool")
    NSP = 6

    def __init__(self, nc, sems, dma_streams):
        self.nc = nc
        self.sems = sems
        self.issuers = ("pe", "act", "dve", "pool", "sp")
        self.dma_streams = tuple(dma_streams)
        self.streams = self.COMPUTE + self.dma_streams
        self.ops = {e: [] for e in self.issuers}
        self.cnt = {s: 0 for s in self.streams}
        self.seen = {e: {s: 0 for s in self.streams} for e in self.issuers}
        self.state = {}
        self.phase_base = {s: 0 for s in self.streams}
        self.sp_rr = 0

    def _st(self, b):
        s = self.state.get(b)
        if s is None:
            s = {"w": None, "r": {}}
            self.state[b] = s
        return s

    def op(self, eng, fn, reads=(), writes=(), dma=False, dstream=None):
        deps = {}
        if dma:
            if dstream is None:
                dstream = "sp%d" % (self.sp_rr % self.NSP)
                self.sp_rr += 1
            stream = dstream
            inc = 16
            if self.cnt[stream] > 0:
                deps[stream] = self.cnt[stream]
        else:
            stream = eng
            inc = 1
        for b in reads:
            w = self._st(b)["w"]
            if w is not None:
                deps[w[0]] = max(deps.get(w[0], 0), w[1])
        for b in writes:
            st = self._st(b)
            if st["w"] is not None:
                w = st["w"]
                deps[w[0]] = max(deps.get(w[0], 0), w[1])
            for s2, c in st["r"].items():
                deps[s2] = max(deps.get(s2, 0), c)
        waits = []
        for s2, c in deps.items():
            if s2 == "pe" and eng == "pe":
                continue
            if c > self.seen[eng][s2]:
                waits.append((s2, c))
                self.seen[eng][s2] = c
        self.cnt[stream] += inc
        my = self.cnt[stream]
        self.ops[eng].append((waits, fn, stream))
        for b in reads:
            st = self._st(b)
            st["r"][stream] = max(st["r"].get(stream, 0), my)
        for b in writes:
            st = self._st(b)
            st["w"] = (stream, my)
            st["r"] = {}
        return my

    def emit_phase(self, final=False):
        nc = self.nc
        sems = self.sems
        base = dict(self.phase_base)
        finals = [(s, c) for s, c in self.cnt.items() if c > 0]
        ops = self.ops
        dmas = set(self.dma_streams)

        def make(engname):
            def body(eng):
                for s2, c in base.items():
                    if c > 0:
                        eng.wait_ge(sems[s2], c)
                for waits, fn, stream in ops[engname]:
                    for s2, c in waits:
                        eng.wait_ge(sems[s2], c)
                    inst = fn(eng)
                    inst.then_inc(sems[stream], 16 if stream in dmas else 1)
                if final and engname == "sp":
                    for s2, c in finals:
                        eng.wait_ge(sems[s2], c)
            return body

        with nc.Block() as block:
            block.tensor(make("pe"))
            block.scalar(make("act"))
            block.vector(make("dve"))
            block.gpsimd(make("pool"))
            block.sync(make("sp"))
        self.ops = {e: [] for e in self.issuers}
        self.phase_base = dict(self.cnt)
        for e in self.issuers:
            for s2 in self.streams:
                self.seen[e][s2] = self.cnt[s2]
        self.state = {}


def fm(ap):
    return ap.rearrange("(c p) t -> p c t", p=128)


class _Stop(Exception):
    pass


def build(debug=False, upto=None):
    try:
        return _build(debug, upto)
    except _Stop as e:
        return e.args[0]


def _build(debug, upto):
    nc = bass.Bass("TRN2", target_bir_lowering=False)

    def din(name, shape, dt=F32):
        return nc.dram_tensor(name, shape, dt, kind="ExternalInput").ap()

    xo = din("xo", [D, T])
    xh = din("xh", [D, NHALO])
    xs = din("xs", [3 * D, T])
    wfs = din("wfs", [3 * D, DH])
    ctx2 = din("ctx2", [D, 512])
    prm = din("prm", [128, NP])
    cst = din("cst", [128, 384])
    w_mod = din("w_mod", [D, 6 * D])
    w_in = din("w_in", [D, 11264])
    w_pw = din("w_pw", [DC, D])
    w_ho = din("w_ho", [DH, D])
    w_o = din("w_o", [D, D])
    w1 = din("w1", [D, DFF])
    w2 = din("w2", [DFF, D])
    out = nc.dram_tensor("out", [D, T], F32, kind="ExternalOutput").ap()

    def dscr(name, shape, dt):
        return nc.dram_tensor(name, shape, dt, kind="Internal").ap()

    hT_d = dscr("hT_d", [D, T], BF16)
    uvh_d = dscr("uvh_d", [512, NHALO], F32)
    cN_d = dscr("cN_d", [DC, T], BF16)
    oN_d = dscr("oN_d", [DH, T], BF16)
    mg_d = dscr("mg_d", [D, T], BF16)
    x1_d = dscr("x1_d", [D, T], F32)
    dbg = {}
    if debug:
        for n, sh in [("d_mod", [128, 192]), ("d_sin", [128, 2048]), ("d_oN", [DH, T]), ("d_cN", [DC, T]),
                      ("d_x1", [D, T]), ("d_hT", [D, T])]:
            dbg[n] = nc.dram_tensor(n, sh, F32, kind="ExternalOutput").ap()

    with contextlib.ExitStack() as top:
        NS = 6
        dstreams = ["slab%d" % i for i in range(NS)] + ["sp%d" % i for i in range(Sched.NSP)]
        sems = {s: top.enter_context(nc.semaphore("s_" + s)) for s in list(Sched.COMPUTE) + dstreams}
        S = Sched(nc, sems, dstreams)

        uniq = [0]

        def sbt(es, name, shape, dt=F32):
            uniq[0] += 1
            return es.enter_context(nc.sbuf_tensor("%s_%d" % (name, uniq[0]), shape, dt))

        ps = [top.enter_context(nc.psum_tensor("ps%d" % i, [128, 512], F32)) for i in range(8)]
        PK = [("ps", i) for i in range(8)]

        Pt = sbt(top, "Pt", [128, NP])
        Ct = sbt(top, "Ct", [128, 384])
        identb = sbt(top, "identb", [128, 128], BF16)
        onesb = sbt(top, "onesb", [128, 128], BF16)
        onesf = sbt(top, "onesf", [128, T])
        cmask = sbt(top, "cmask", [128, T])
        scb = sbt(top, "scb", [128, 32], BF16)
        lbt = sbt(top, "lbt", [128, 40])
        oml = sbt(top, "oml", [128, 40])
        noml = sbt(top, "noml", [128, 40])
        modT = sbt(top, "modT", [128, 96, 2])
        der = sbt(top, "der", [128, 5, 16])
        Sin = sbt(top, "Sin", [128, 2, 8, 128])
        slabs = [sbt(top, "slab%d" % i, [128, 16, 128], BF16) for i in range(NS)]
        slab_ctr = [0]

        def pcol(name, i=0, n=1):
            o, w = PL[name]
            return Pt[:, o + i:o + i + n]

        def end_phase(name):
            S.emit_phase(final=(name == upto))
            if name == upto:
                raise _Stop(nc)

        def DMA(eng, out_, in_, reads=(), writes=(), dstream=None):
            S.op(eng, lambda e: e.dma_start(out=out_, in_=in_), reads, writes, dma=True, dstream=dstream)

        def ACT(out_, in_, func, reads=(), writes=(), **kw):
            S.op("act", lambda e: e.activation(out=out_, in_=in_, func=func, **kw), reads, writes)

        def TT(out_, in0, in1, op, reads=(), writes=(), eng="dve"):
            S.op(eng, lambda e: e.tensor_tensor(out=out_, in0=in0, in1=in1, op=op), reads, writes)

        def TS(out_, in0, s1, s2, op0, op1=None, reads=(), writes=(), eng="dve"):
            if op1 is None:
                S.op(eng, lambda e: e.tensor_scalar(out=out_, in0=in0, scalar1=s1, scalar2=None, op0=op0), reads, writes)
            else:
                S.op(eng, lambda e: e.tensor_scalar(out=out_, in0=in0, scalar1=s1, scalar2=s2, op0=op0, op1=op1), reads, writes)

        def STT(out_, in0, scalar, in1, op0, op1, reads=(), writes=()):
            S.op("dve", lambda e: e.scalar_tensor_tensor(out=out_, in0=in0, scalar=scalar, in1=in1, op0=op0, op1=op1), reads, writes)

        def CP(out_, in_, reads=(), writes=(), eng="dve"):
            S.op(eng, lambda e: e.tensor_copy(out=out_, in_=in_), reads, writes)

        def MM(out_, lhsT, rhs, start, stop, reads=(), writes=()):
            S.op("pe", lambda e: e.matmul(out_, lhsT=lhsT, rhs=rhs, start=start, stop=stop), reads, writes)

        def TR(out_, in_, reads=(), writes=()):
            S.op("pe", lambda e: e.transpose(out_, in_, identb[:]), reads, writes)

        def MS(ap, val, writes=()):
            S.op("pool", lambda e: e.memset(ap, val), (), writes)

        def load_slab(W, row0, nk, col0):
            i = slab_ctr[0] % NS
            slab_ctr[0] += 1
            src = W[row0:row0 + nk * 128, col0:col0 + 128].rearrange("(k p) n -> p k n", p=128)
            DMA("pool", slabs[i][:, 0:nk, :], src, writes=[("slab", i)], dstream="slab%d" % i)
            return i

        def mm_slab(out_, si, nk, rhs_fn, rkeys, wkey):
            for k in range(nk):
                MM(out_, slabs[si][:, k, :], rhs_fn(k), k == 0, k == nk - 1,
                   reads=[("slab", si)] + list(rkeys), writes=[wkey])

        def bfview(p):
            return p[:].bitcast(BF16)

        DMA("sp", Pt[:], prm[:, :], writes=["Pt"])
        DMA("sp", Ct[:], cst[:, :], writes=["Ct"])
        CP(identb[:], Ct[:, 0:128], reads=["Ct"], writes=["identb"])
        MS(onesb[:], 1.0, writes=["onesb"])
        MS(onesf[:], 1.0, writes=["onesf"])
        MS(cmask[:], 1.0, writes=["cmask"])
        MS(cmask[:].rearrange("p (c t) -> p c t", t=64)[:, :, 0:1], 0.0, writes=["cmask"])
        MS(Sin[:], 0.0, writes=["Sin"])
        o_cp = PL["cp"][0]
        ACT(scb[:], Pt[:, o_cp:o_cp + 32], AF.Silu, reads=["Pt"], writes=["scb"])
        o_l0, o_l1 = PL["l0"][0], PL["l1"][0]
        TT(lbt[:], Pt[:, o_l0:o_l0 + 40], Pt[:, o_l1:o_l1 + 40], ALU.subtract, reads=["Pt"], writes=["lbt"])
        ACT(lbt[:], lbt[:], AF.Sigmoid, reads=["lbt"], writes=["lbt"])
        TS(oml[:], lbt[:], -1.0, 1.0, ALU.mult, ALU.add, reads=["lbt"], writes=["oml"])
        TS(noml[:], lbt[:], -1.0, None, ALU.add, reads=["lbt"], writes=["noml"])
        scb3 = scb[:].rearrange("p (c r) -> p c r", r=2)
        for m in range(96):
            si = load_slab(w_mod, 0, 16, m * 128)
            mm_slab(ps[0][:, 2 * m:2 * m + 2], si, 16, lambda k: scb3[:, k, :], ["scb"], PK[0])
        o_bm = PL["bmod"][0]
        TT(modT[:], ps[0][:, 0:192].rearrange("p (m r) -> p m r", r=2),
           Pt[:, o_bm:o_bm + 96].unsqueeze(2).to_broadcast([128, 96, 2]), ALU.add,
           reads=[PK[0], "Pt"], writes=["modT"])
        o_npm, o_npo, o_npl, o_npo2 = PL["npm"][0], PL["npo"][0], PL["npl"][0], PL["npo2"][0]
        STT(der[:, 0, :], modT[:, 16:32, 0], 1.0, Pt[:, o_npm:o_npm + 16], ALU.add, ALU.mult, reads=["modT", "Pt"], writes=["der"])
        STT(der[:, 1, :], modT[:, 16:32, 1], 1.0, Pt[:, o_npm:o_npm + 16], ALU.add, ALU.mult, reads=["modT", "Pt"], writes=["der"])
        STT(der[:, 2, :], modT[:, 64:80, 0], 1.0, Pt[:, o_npl:o_npl + 16], ALU.add, ALU.mult, reads=["modT", "Pt"], writes=["der"])
        TT(der[:, 3, :], modT[:, 32:48, 0], Pt[:, o_npo:o_npo + 16], ALU.mult, reads=["modT", "Pt"], writes=["der"])
        TT(der[:, 4, :], modT[:, 80:96, 0], Pt[:, o_npo2:o_npo2 + 16], ALU.mult, reads=["modT", "Pt"], writes=["der"])
        if debug:
            DMA("sp", dbg["d_mod"][:, :], modT[:].rearrange("p m r -> p (m r)"), reads=["modT"], writes=["dd"])
        end_phase("P0")

        def make_h(tmp, src, n, gs_col, sh_col, out_fn, okey):
            xb, sqb, tx, rs = tmp["xb"], tmp["sqb"], tmp["tx"], tmp["rs"]
            DMA("sp", xb[:, :, 0:n], fm(src), writes=["xb"])
            for c in range(16):
                q = c % 2
                ACT(sqb[q][:, 0:n], xb[:, c, 0:n], AF.Square, reads=["xb"], writes=[("sqb", q)])
                MM(ps[7][:, 0:n], onesb[:], sqb[q][:, 0:n], c == 0, c == 15, reads=[("sqb", q)], writes=[PK[7]])
            ACT(rs[:, 0:n], ps[7][:, 0:n], AF.Sqrt, reads=[PK[7]], writes=["rs"], scale=1.0 / D, bias=EPS)
            S.op("dve", lambda e: e.reciprocal(out=rs[:, 0:n], in_=rs[:, 0:n]), ["rs"], ["rs"])
            for c in range(16):
                q = c % 2
                TT(tx[q][:, 0:n], xb[:, c, 0:n], rs[:, 0:n], ALU.mult, reads=["xb", "rs"], writes=[("tx", q)])
                ACT(out_fn(c), tx[q][:, 0:n], AF.Identity, reads=[("tx", q)], writes=[okey],
                    scale=gs_col(c), bias=sh_col(c))

        def h_tmps(es):
            return {"xb": sbt(es, "xb", [128, 16, 512]),
                    "sqb": [sbt(es, "sqb%d" % q, [128, 512], BF16) for q in range(2)],
                    "tx": [sbt(es, "tx%d" % q, [128, 512]) for q in range(2)],
                    "rs": sbt(es, "rs", [128, 512])}

        gs_m = lambda c: der[:, 0, c:c + 1]
        gsc_m = lambda c: der[:, 1, c:c + 1]
        gs_f = lambda c: der[:, 2, c:c + 1]
        sh_m = lambda c: modT[:, c, 0:1]
        shc_m = lambda c: modT[:, c, 1:2]
        sh_f = lambda c: modT[:, 48 + c, 0:1]

        with contextlib.ExitStack() as es:
            tmp = h_tmps(es)
            hR = sbt(es, "hR", [128, 16, T], BF16)
            vR = sbt(es, "vR", [128, 8, 1024], BF16)
            sg = sbt(es, "sg", [128, T])
            lg = sbt(es, "lg", [128, T])
            kk = sbt(es, "kk", [128, T])
            Pc = sbt(es, "Pc", [128, T])
            khat = sbt(es, "khat", [128, T], BF16)
            khT = sbt(es, "khT", [128, T], BF16)
            G = sbt(es, "G", [128, 8])
            Gb = sbt(es, "Gb", [128, 8])
            GA = sbt(es, "GA", [128, 2, 8])
            SA = sbt(es, "SA", [128, 2, 8, 128])
            Sctx = sbt(es, "Sctx", [128, 2, 8, 128])
            MS(GA[:], 0.0, writes=["GA"])
            MS(SA[:], 0.0, writes=["SA"])
            MS(G[:], 0.0, writes=["G"])

            def region(src, nt, gs_col, sh_col, wf, wf_row0, wf_col0, lb0, post):
                bs = min(512, nt)
                for blk in range(nt // bs):
                    make_h(tmp, src[:, blk * bs:(blk + 1) * bs], bs, gs_col, sh_col,
                           lambda c, blk=blk: hR[:, c, blk * bs:(blk + 1) * bs], "hR")
                ntile = nt // 128
                for cs in range(8):
                    si = load_slab(w_in, 0, 16, C_I + cs * 128)
                    for tl in range(ntile):
                        pb = ps[tl // 4]
                        for k in range(16):
                            MM(pb[:, (tl % 4) * 128:(tl % 4 + 1) * 128], hR[:, k, tl * 128:(tl + 1) * 128],
                               slabs[si][:, k, :], k == 0, k == 15, reads=["hR", ("slab", si)], writes=[PK[tl // 4]])
                    for hf in range((ntile + 3) // 4):
                        n4 = min(4, ntile - hf * 4)
                        ACT(vR[:, hf * 4:hf * 4 + n4, cs * 128:(cs + 1) * 128],
                            ps[hf][:, 0:n4 * 128].rearrange("p (a b) -> p a b", b=128), AF.Copy,
                            reads=[PK[hf]], writes=["vR"])
                for hd in range(8):
                    si = load_slab(wf, wf_row0, 16, wf_col0 + hd * 128)
                    for blk in range(nt // bs):
                        pb = ps[2 + blk]
                        mm_slab(pb[:, 0:bs], si, 16, lambda k, blk=blk: hR[:, k, blk * bs:(blk + 1) * bs], ["hR"], PK[2 + blk])
                        ACT(sg[:, blk * bs:(blk + 1) * bs], pb[:, 0:bs], AF.Sigmoid, reads=[PK[2 + blk]], writes=["sg"])
                    li = lb0 + hd
                    ACT(lg[:, 0:nt], sg[:, 0:nt], AF.Ln, reads=["sg"], writes=["lg"], scale=oml[:, li:li + 1], bias=lbt[:, li:li + 1])
                    TS(kk[:, 0:nt], sg[:, 0:nt], noml[:, li:li + 1], oml[:, li:li + 1], ALU.mult, ALU.add, reads=["sg"], writes=["kk"])
                    S.op("dve", lambda e, hd=hd: e.tensor_tensor_scan(out=Pc[:, 0:nt], data0=onesf[:, 0:nt], data1=lg[:, 0:nt],
                                                                     initial=G[:, hd:hd + 1], op0=ALU.mult, op1=ALU.add),
                         ["lg", "G"], ["Pc"])
                    CP(G[:, hd:hd + 1], Pc[:, nt - 1:nt], reads=["Pc"], writes=["G"])
                    TT(sg[:, 0:nt], Pc[:, 0:nt], lg[:, 0:nt], ALU.subtract, reads=["Pc", "lg"], writes=["sg"])
                    ACT(lg[:, 0:nt], sg[:, 0:nt], AF.Exp, reads=["sg"], writes=["lg"])
                    TT(khat[:, 0:nt], kk[:, 0:nt], lg[:, 0:nt], ALU.mult, reads=["kk", "lg"], writes=["khat"])
                    pT = bfview(ps[4])
                    for tl in range(ntile):
                        TR(pT[:, tl * 128:(tl + 1) * 128], khat[:, tl * 128:(tl + 1) * 128], reads=["khat"], writes=[PK[4]])
                    ACT(khT[:, 0:nt], pT[:, 0:nt], AF.Copy, reads=[PK[4]], writes=["khT"])
                    for tl in range(ntile):
                        MM(ps[5][:, 0:128], khT[:, tl * 128:(tl + 1) * 128], vR[:, tl, hd * 128:(hd + 1) * 128],
                           tl == 0, tl == ntile - 1, reads=["khT", "vR"], writes=[PK[5]])
                    post(hd, ps[5][:, 0:128])

            for d in range(2):
                MS(G[:], 0.0, writes=["G"])
                region(ctx2[:, d * 256:(d + 1) * 256], 256, gsc_m, shc_m, w_in, 0, C_FF if d == 0 else C_FB, 8 * d,
                       lambda hd, acc, d=d: CP(Sctx[:, d, hd, :], acc, reads=[PK[5]], writes=["Sctx"]))

            for s in range(3):
                if s == 0:
                    MS(G[:], 0.0, writes=["G"])
                else:
                    TS(G[:], G[:], pcol("slk", s), None, ALU.mult, reads=["G"], writes=["G"])
                CP(Gb[:], G[:], reads=["G"], writes=["Gb"])

                def post_slot(hd, acc, s=s):
                    for d in range(2):
                        STT(SA[:, d, hd, :], acc, pcol("sla" if d == 0 else "slb", s), SA[:, d, hd, :], ALU.mult, ALU.add,
                            reads=[PK[5], "SA"], writes=["SA"])
                region(xs[s * D:(s + 1) * D, :], T, gs_m, sh_m, wfs, s * D, 0, 16 + 8 * s, post_slot)
                TT(Gb[:], G[:], Gb[:], ALU.subtract, reads=["G", "Gb"], writes=["Gb"])
                for d in range(2):
                    STT(GA[:, d, :], Gb[:], pcol("sla" if d == 0 else "slb", s), GA[:, d, :], ALU.mult, ALU.add,
                        reads=["Gb", "GA"], writes=["GA"])
            ACT(GA[:], GA[:], AF.Exp, reads=["GA"], writes=["GA"])
            for d in range(2):
                for hd in range(8):
                    STT(Sin[:, d, hd, :], Sctx[:, d, hd, :], GA[:, d, hd:hd + 1], SA[:, d, hd, :], ALU.mult, ALU.add,
                        reads=["Sctx", "GA", "SA"], writes=["Sin"])
            if debug:
                DMA("sp", dbg["d_sin"][:, :], Sin[:].rearrange("p d h v -> p (d h v)"), reads=["Sin"], writes=["dd"])
            end_phase("R")

        with contextlib.ExitStack() as es:
            tmp = h_tmps(es)
            hT = sbt(es, "hT", [128, 16, T], BF16)
            for blk in range(2):
                make_h(tmp, xo[:, blk * 512:(blk + 1) * 512], 512, gs_m, sh_m,
                       lambda c, blk=blk: hT[:, c, blk * 512:(blk + 1) * 512], ("hT", blk))
                DMA("sp", fm(hT_d)[:, :, blk * 512:(blk + 1) * 512], hT[:, :, blk * 512:(blk + 1) * 512],
                    reads=[("hT", blk)], writes=["hT_d"])
            end_phase("H")

        with contextlib.ExitStack() as es:
            tmp = h_tmps(es)
            hB = sbt(es, "hB", [128, 16, 512], BF16)
            sgt = [sbt(es, "sgt%d" % q, [128, 512]) for q in range(2)]
            ub = [sbt(es, "ub%d" % q, [128, 512]) for q in range(2)]
            hoff = 0
            for hb, n in enumerate([512, 448, 512, 448]):
                side = 0 if hb < 2 else 1
                make_h(tmp, xh[:, hoff:hoff + n], n, gs_m, sh_m, lambda c: hB[:, c, 0:n], "hB")
                for cc in range(4, 8):
                    q = cc % 2
                    sv = load_slab(w_in, 0, 16, C_CV + cc * 128)
                    sgi = load_slab(w_in, 0, 16, C_CG + cc * 128)
                    mm_slab(ps[0 + 2 * q][:, 0:n], sv, 16, lambda k: hB[:, k, 0:n], ["hB"], PK[0 + 2 * q])
                    mm_slab(ps[1 + 2 * q][:, 0:n], sgi, 16, lambda k: hB[:, k, 0:n], ["hB"], PK[1 + 2 * q])
                    ACT(sgt[q][:, 0:n], ps[1 + 2 * q][:, 0:n], AF.Sigmoid, reads=[PK[1 + 2 * q]], writes=[("sgt", q)])
                    STT(ub[q][:, 0:n], ps[0 + 2 * q][:, 0:n], pcol("hm", side), sgt[q][:, 0:n], ALU.mult, ALU.mult,
                        reads=[("sgt", q), PK[0 + 2 * q]], writes=[("ub", q)])
                    DMA("sp", uvh_d[(cc - 4) * 128:(cc - 3) * 128, hoff:hoff + n], ub[q][:, 0:n],
                        reads=[("ub", q)], writes=["uvh_d"])
                hoff += n
            end_phase("C1")

        with contextlib.ExitStack() as es:
            hT = sbt(es, "hT", [128, 16, T], BF16)
            uv = sbt(es, "uv", [128, 46 * 64])
            uh = sbt(es, "uh", [128, 16, 94])
            conv = sbt(es, "conv", [128, 8, T])
            sgt = [sbt(es, "sgt%d" % q, [128, 512]) for q in range(2)]
            cb16 = [sbt(es, "cb16%d" % q, [128, 512], BF16) for q in range(2)]
            cs16 = [sbt(es, "cs16%d" % q, [128, 512], BF16) for q in range(2)]
            mean = sbt(es, "mean", [128, 512])
            rstd = sbt(es, "rstd", [128, 512])
            t1 = [sbt(es, "t1%d" % q, [128, 512]) for q in range(2)]
            cN = sbt(es, "cN", [128, 8, T], BF16)
            DMA("sp", hT[:], fm(hT_d), writes=["hT"])
            o_cw, o_cb = PL["cw"][0], PL["cb"][0]
            for cc in range(8):
                vert = cc >= 4
                if vert:
                    DMA("sp", uv[:, 0:960], uvh_d[(cc - 4) * 128:(cc - 3) * 128, 0:960], writes=["uv"])
                    DMA("sp", uv[:, 1984:2944], uvh_d[(cc - 4) * 128:(cc - 3) * 128, 960:1920], writes=["uv"])
                else:
                    MS(uh[:], 0.0, writes=["uh"])
                sv = load_slab(w_in, 0, 16, C_CV + cc * 128)
                sgi = load_slab(w_in, 0, 16, C_CG + cc * 128)
                for blk in range(2):
                    q = blk
                    mm_slab(ps[0 + 2 * q][:], sv, 16, lambda k, blk=blk: hT[:, k, blk * 512:(blk + 1) * 512], ["hT"], PK[0 + 2 * q])
                    mm_slab(ps[1 + 2 * q][:], sgi, 16, lambda k, blk=blk: hT[:, k, blk * 512:(blk + 1) * 512], ["hT"], PK[1 + 2 * q])
                    ACT(sgt[q][:], ps[1 + 2 * q][:], AF.Sigmoid, reads=[PK[1 + 2 * q]], writes=[("sgt", q)])
                    if vert:
                        TT(uv[:, 960 + blk * 512:960 + (blk + 1) * 512], ps[0 + 2 * q][:], sgt[q][:], ALU.mult,
                           reads=[PK[0 + 2 * q], ("sgt", q)], writes=["uv"])
                    else:
                        TT(uh[:, blk * 8:(blk + 1) * 8, 15:79], ps[0 + 2 * q][:].rearrange("p (r w) -> p r w", w=64),
                           sgt[q][:].rearrange("p (r w) -> p r w", w=64), ALU.mult,
                           reads=[PK[0 + 2 * q], ("sgt", q)], writes=["uh"])
                acc = conv[:, cc, :]
                for dd in range(31):
                    wcol = Pt[:, o_cw + cc * 31 + dd:o_cw + cc * 31 + dd + 1]
                    if vert:
                        src = uv[:, dd * 64:dd * 64 + 1024]
                        dst = acc
                        accin = acc
                    else:
                        src = uh[:, :, dd:dd + 64]
                        dst = acc.rearrange("p (r w) -> p r w", w=64)
                        accin = dst
                    if dd == 0:
                        TS(dst, src, wcol, Pt[:, o_cb + cc:o_cb + cc + 1], ALU.mult, ALU.add,
                           reads=["uv" if vert else "uh"], writes=[("conv", cc)])
                    else:
                        STT(dst, src, wcol, accin, ALU.mult, ALU.add,
                            reads=["uv" if vert else "uh", ("conv", cc)], writes=[("conv", cc)])
            o_lg, o_lb = PL["lng"][0], PL["lnb"][0]
            for blk in range(2):
                bsl = slice(blk * 512, (blk + 1) * 512)
                for cc in range(8):
                    q = cc % 2
                    ACT(cb16[q][:], conv[:, cc, bsl], AF.Copy, reads=[("conv", cc)], writes=[("cb16", q)])
                    ACT(cs16[q][:], conv[:, cc, bsl], AF.Square, reads=[("conv", cc)], writes=[("cs16", q)])
                    MM(ps[4][:], onesb[:], cb16[q][:], cc == 0, cc == 7, reads=[("cb16", q)], writes=[PK[4]])
                    MM(ps[5][:], onesb[:], cs16[q][:], cc == 0, cc == 7, reads=[("cs16", q)], writes=[PK[5]])
                ACT(mean[:], ps[4][:], AF.Copy, reads=[PK[4]], writes=["mean"], scale=1.0 / DC)
                TT(rstd[:], mean[:], mean[:], ALU.mult, reads=["mean"], writes=["rstd"])
                STT(rstd[:], ps[5][:], 1.0 / DC, rstd[:], ALU.mult, ALU.subtract, reads=[PK[5], "rstd"], writes=["rstd"])
                ACT(rstd[:], rstd[:], AF.Sqrt, reads=["rstd"], writes=["rstd"], bias=EPS)
                S.op("dve", lambda e: e.reciprocal(out=rstd[:], in_=rstd[:]), ["rstd"], ["rstd"])
                for cc in range(8):
                    q = cc % 2
                    TT(t1[q][:], conv[:, cc, bsl], mean[:], ALU.subtract, reads=[("conv", cc), "mean"], writes=[("t1", q)])
                    TT(t1[q][:], t1[q][:], rstd[:], ALU.mult, reads=[("t1", q), "rstd"], writes=[("t1", q)])
                    ACT(cN[:, cc, bsl], t1[q][:], AF.Silu, reads=[("t1", q)], writes=["cN"],
                        scale=Pt[:, o_lg + cc:o_lg + cc + 1], bias=Pt[:, o_lb + cc:o_lb + cc + 1])
            DMA("sp", fm(cN_d), cN[:], reads=["cN"], writes=["cN_d"])
            if debug:
                cNf = sbt(es, "cNf", [128, 8, T])
                CP(cNf[:], cN[:], reads=["cN"], writes=["cNf"])
                DMA("sp", fm(dbg["d_cN"]), cNf[:], reads=["cNf"], writes=["dd"])
            end_phase("C2")

        with contextlib.ExitStack() as es:
            hT = sbt(es, "hT", [128, 16, T], BF16)
            vtok = sbt(es, "vtok", [128, 8, 1024], BF16)
            sg = sbt(es, "sg", [128, T])
            lgd = [sbt(es, "lg%d" % d, [128, T]) for d in range(2)]
            kkd = [sbt(es, "kk%d" % d, [128, T]) for d in range(2)]
            qs = sbt(es, "qs", [128, T])
            gsil = sbt(es, "gsil", [128, T])
            aa = sbt(es, "aa", [128, T])
            ab = sbt(es, "ab", [128, T])
            tA = sbt(es, "tA", [128, T])
            tB = sbt(es, "tB", [128, T])
            qt = [sbt(es, "qt%d" % d, [128, T], BF16) for d in range(2)]
            kt = [sbt(es, "kt%d" % d, [128, T], BF16) for d in range(2)]
            kh = [sbt(es, "kh%d" % d, [128, T], BF16) for d in range(2)]
            khT = [sbt(es, "khT%d" % d, [128, T], BF16) for d in range(2)]
            dec = [sbt(es, "dec%d" % d, [128, 16]) for d in range(2)]
            Srun = [sbt(es, "Srun%d" % d, [128, 128]) for d in range(2)]
            Sb = [sbt(es, "Sb%d" % d, [128, 16, 128], BF16) for d in range(2)]
            scm = [sbt(es, "scm%d" % d, [128, T], BF16) for d in range(2)]
            osq = sbt(es, "osq", [128, 512], BF16)
            rs = sbt(es, "rsH", [128, 512])
            t1 = sbt(es, "t1H", [128, 512])
            oN = sbt(es, "oN", [128, T], BF16)
            pmask = [sbt(es, "pmask%d" % q, [128, T], BF16) for q in range(2)]
            khp = [sbt(es, "khp%d" % q, [128, T], BF16) for q in range(2)]
            khTp = [sbt(es, "khTp%d" % q, [128, T], BF16) for q in range(2)]
            for q in range(2):
                pv = pmask[q][:].rearrange("p (a h t) -> p a h t", h=2, t=64)
                MS(pv[:, :, q, :], 1.0, writes=[("pmask", q)])
                MS(pv[:, :, 1 - q, :], 0.0, writes=[("pmask", q)])
            if debug:
                oNf = sbt(es, "oNf", [128, T])
            DMA("sp", hT[:], fm(hT_d), writes=["hT"])
            for cs in range(8):
                si = load_slab(w_in, 0, 16, C_I + cs * 128)
                for tl in range(8):
                    pb = ps[tl // 4]
                    for k in range(16):
                        MM(pb[:, (tl % 4) * 128:(tl % 4 + 1) * 128], hT[:, k, tl * 128:(tl + 1) * 128],
                           slabs[si][:, k, :], k == 0, k == 15, reads=["hT", ("slab", si)], writes=[PK[tl // 4]])
                for hf in range(2):
                    ACT(vtok[:, hf * 4:hf * 4 + 4, cs * 128:(cs + 1) * 128],
                        ps[hf][:].rearrange("p (a b) -> p a b", b=128), AF.Copy, reads=[PK[hf]], writes=["vtok"])
            o_hng = PL["hng"][0]
            a3 = lambda t: t[:].rearrange("p (c t) -> p c t", t=64)
            HGS = int(os.environ.get('HGS', '9'))
            for hd in range(int(os.environ.get('HGNH', '8'))):
                def proj(colbase, pa):
                    si = load_slab(w_in, 0, 16, colbase + hd * 128)
                    for blk in range(2):
                        mm_slab(ps[pa + blk][:], si, 16, lambda k, blk=blk: hT[:, k, blk * 512:(blk + 1) * 512], ["hT"], PK[pa + blk])
                for d, cbase in ((0, C_FF), (1, C_FB)):
                    proj(cbase, 2 * d)
                    for blk in range(2):
                        ACT(sg[:, blk * 512:(blk + 1) * 512], ps[2 * d + blk][:], AF.Sigmoid, reads=[PK[2 * d + blk]], writes=["sg"])
                    li = 8 * d + hd
                    ACT(lgd[d][:], sg[:], AF.Ln, reads=["sg"], writes=[("lg", d)], scale=oml[:, li:li + 1], bias=lbt[:, li:li + 1])
                    TS(kkd[d][:], sg[:], noml[:, li:li + 1], oml[:, li:li + 1], ALU.mult, ALU.add, reads=["sg"], writes=[("kk", d)])
                proj(C_Q, 0)
                for blk in range(2):
                    ACT(qs[:, blk * 512:(blk + 1) * 512], ps[blk][:], AF.Silu, reads=[PK[blk]], writes=["qs"])
                proj(C_G, 2)
                for blk in range(2):
                    ACT(gsil[:, blk * 512:(blk + 1) * 512], ps[2 + blk][:], AF.Silu, reads=[PK[2 + blk]], writes=["gsil"])
                for d in range(2 if HGS >= 2 else 0):
                    S.op("dve", lambda e, d=d: e.tensor_tensor_scan(out=aa[:], data0=cmask[:], data1=lgd[d][:], initial=0.0,
                                                                   op0=ALU.mult, op1=ALU.add), [("lg", d)], ["aa"])
                    if d == 1:
                        TT(tA[:], lgd[1][:], aa[:], ALU.subtract, reads=[("lg", 1), "aa"], writes=["tA"])
                        TT(a3(ab), a3(tA), a3(aa)[:, :, 63:64].to_broadcast([128, 16, 64]), ALU.add, reads=["tA", "aa"], writes=["ab"])
                    edge = 63 if d == 0 else 0
                    av, ak = (aa, "aa") if d == 0 else (ab, "ab")
                    ACT(tA[:], av[:], AF.Exp, reads=[ak], writes=["tA"])
                    STT(qt[d][:], tA[:], 128.0 ** -0.5, qs[:], ALU.mult, ALU.mult, reads=["tA", "qs"], writes=[("qt", d)])
                    CP(dec[d][:], a3(tA)[:, :, edge], reads=["tA"], writes=[("dec", d)])
                    ACT(tB[:], av[:], AF.Exp, reads=[ak], writes=["tB"], scale=-1.0)
                    TT(kt[d][:], kkd[d][:], tB[:], ALU.mult, reads=[("kk", d), "tB"], writes=[("kt", d)])
                    TT(a3(tA), a3(av)[:, :, edge:edge + 1].to_broadcast([128, 16, 64]), a3(av), ALU.subtract, reads=[ak], writes=["tA"])
                    ACT(tB[:], tA[:], AF.Exp, reads=["tA"], writes=["tB"])
                    TT(kh[d][:], kkd[d][:], tB[:], ALU.mult, reads=[("kk", d), "tB"], writes=[("kh", d)])
                    if HGS < 3:
                        continue
                    pT = bfview(ps[4])
                    for q in range(2):
                        TT(khp[q][:], kh[d][:], pmask[q][:], ALU.mult, reads=[("kh", d), ("pmask", q)], writes=[("khp", q)])
                        for tl in range(8):
                            TR(pT[:, tl * 128:(tl + 1) * 128], khp[q][:, tl * 128:(tl + 1) * 128], reads=[("khp", q)], writes=[PK[4]])
                        ACT(khTp[q][:], pT[:, :], AF.Copy, reads=[PK[4]], writes=[("khTp", q)])
                    if HGS < 4:
                        continue
                    c0 = 0 if d == 0 else 15
                    CP(Srun[d][:], Sin[:, d, hd, :], writes=[("Srun", d)])
                    ACT(Sb[d][:, c0, :], Sin[:, d, hd, :], AF.Copy, writes=[("Sb", d)])
                    order = range(0, 15) if d == 0 else range(15, 0, -1)
                    for n, c in enumerate(list(order)[:int(os.environ.get('HGC', '15'))]):
                        tl, hf = c // 2, c % 2
                        slot = ps[n % 4][:, 0:128]
                        MM(slot, khTp[hf][:, tl * 128:(tl + 1) * 128], vtok[:, tl, hd * 128:(hd + 1) * 128], True, True,
                           reads=[("khTp", hf), "vtok"], writes=[PK[n % 4]])
                        TS(Srun[d][:], Srun[d][:], dec[d][:, c:c + 1], None, ALU.mult,
                           reads=[("Srun", d), ("dec", d)], writes=[("Srun", d)])
                        TT(Srun[d][:], slot, Srun[d][:], ALU.add,
                           reads=[("Srun", d), PK[n % 4]], writes=[("Srun", d)])
                        cn = c + 1 if d == 0 else c - 1
                        ACT(Sb[d][:, cn, :], Srun[d][:], AF.Copy, reads=[("Srun", d)], writes=[("Sb", d)])
                    if HGS < 5:
                        continue
                    mk = Ct[:, 128:256] if d == 0 else Ct[:, 256:384]
                    for g in range(2):
                        pb = ps[6 + d]
                        for tl in range(4):
                            tile = g * 4 + tl
                            MM(pb[:, tl * 128:(tl + 1) * 128], kt[d][:, tile * 128:(tile + 1) * 128],
                               qt[d][:, tile * 128:(tile + 1) * 128], True, True, reads=[("kt", d), ("qt", d)], writes=[PK[6 + d]])
                        TT(scm[d][:, g * 512:(g + 1) * 512].rearrange("p (a b) -> p a b", b=128),
                           pb[:].rearrange("p (a b) -> p a b", b=128),
                           mk.unsqueeze(1).to_broadcast([128, 4, 128]), ALU.mult, reads=[PK[6 + d]], writes=[("scm", d)])
                for tile in range(8 if HGS >= 6 else 0):
                    pb = ps[tile // 4]
                    po = pb[:, (tile % 4) * 128:(tile % 4 + 1) * 128]
                    vt = vtok[:, tile, hd * 128:(hd + 1) * 128]
                    MM(po, vt, scm[0][:, tile * 128:(tile + 1) * 128], True, False, reads=["vtok", ("scm", 0)], writes=[PK[tile // 4]])
                    MM(po, vt, scm[1][:, tile * 128:(tile + 1) * 128], False, False, reads=["vtok", ("scm", 1)], writes=[PK[tile // 4]])
                    for hf in range(2):
                        c = tile * 2 + hf
                        pc = pb[:, (tile % 4) * 128 + hf * 64:(tile % 4) * 128 + hf * 64 + 64]
                        MM(pc, Sb[0][:, c, :], qt[0][:, c * 64:(c + 1) * 64], False, False, reads=[("Sb", 0), ("qt", 0)], writes=[PK[tile // 4]])
                        MM(pc, Sb[1][:, c, :], qt[1][:, c * 64:(c + 1) * 64], False, hf == 1, reads=[("Sb", 1), ("qt", 1)], writes=[PK[tile // 4]])
                for blk in range(2):
                    ACT(osq[:], ps[blk][:], AF.Square, reads=[PK[blk]], writes=["osq"])
                    MM(ps[6][:], onesb[:], osq[:], True, True, reads=["osq"], writes=[PK[6]])
                    ACT(rs[:], ps[6][:], AF.Sqrt, reads=[PK[6]], writes=["rsH"], scale=1.0 / 128, bias=EPS)
                    S.op("dve", lambda e: e.reciprocal(out=rs[:], in_=rs[:]), ["rsH"], ["rsH"])
                    TT(t1[:], ps[blk][:], rs[:], ALU.mult, reads=[PK[blk], "rsH"], writes=["t1H"])
                    STT(oN[:, blk * 512:(blk + 1) * 512], t1[:], Pt[:, o_hng + hd:o_hng + hd + 1], gsil[:, blk * 512:(blk + 1) * 512],
                        ALU.mult, ALU.mult, reads=["t1H", "gsil"], writes=["oN"])
                DMA("sp", oN_d[hd * 128:(hd + 1) * 128, :], oN[:], reads=["oN"], writes=["oN_d"])
                if debug:
                    CP(oNf[:], oN[:], reads=["oN"], writes=["oNf"])
                    DMA("sp", dbg["d_oN"][hd * 128:(hd + 1) * 128, :], oNf[:], reads=["oNf"], writes=["dd"])
            end_phase("HG")

        with contextlib.ExitStack() as es:
            hT = sbt(es, "hT", [128, 16, T], BF16)
            cN = sbt(es, "cN", [128, 8, T], BF16)
            oN = sbt(es, "oNa", [128, 8, T], BF16)
            sgc = [sbt(es, "sgc%d" % q, [128, 512]) for q in range(2)]
            sgh = [sbt(es, "sgh%d" % q, [128, 512]) for q in range(2)]
            m1 = [sbt(es, "m1%d" % q, [128, 512]) for q in range(2)]
            mg = sbt(es, "mg", [128, 16, T], BF16)
            DMA("sp", hT[:], fm(hT_d), writes=["hT"])
            DMA("sp", cN[:], fm(cN_d), writes=["cN"])
            DMA("sp", oN[:], fm(oN_d), writes=["oNa"])
            for m in range(16):
                s_pw = load_slab(w_pw, 0, 8, m * 128)
                s_ho = load_slab(w_ho, 0, 8, m * 128)
                s_gc = load_slab(w_in, 0, 16, C_GC + m * 128)
                s_gh = load_slab(w_in, 0, 16, C_GH + m * 128)
                for blk in range(2):
                    q = blk
                    bsl = slice(blk * 512, (blk + 1) * 512)
                    b0 = 4 * q
                    mm_slab(ps[b0][:], s_pw, 8, lambda k: cN[:, k, bsl], ["cN"], PK[b0])
                    mm_slab(ps[b0 + 1][:], s_ho, 8, lambda k: oN[:, k, bsl], ["oNa"], PK[b0 + 1])
                    mm_slab(ps[b0 + 2][:], s_gc, 16, lambda k: hT[:, k, bsl], ["hT"], PK[b0 + 2])
                    mm_slab(ps[b0 + 3][:], s_gh, 16, lambda k: hT[:, k, bsl], ["hT"], PK[b0 + 3])
                    ACT(sgc[q][:], ps[b0 + 2][:], AF.Sigmoid, reads=[PK[b0 + 2]], writes=[("sgc", q)])
                    ACT(sgh[q][:], ps[b0 + 3][:], AF.Sigmoid, reads=[PK[b0 + 3]], writes=[("sgh", q)])
                    TT(m1[q][:], ps[b0][:], sgc[q][:], ALU.mult, reads=[PK[b0], ("sgc", q)], writes=[("m1", q)])
                    TT(sgh[q][:], ps[b0 + 1][:], sgh[q][:], ALU.mult, reads=[PK[b0 + 1], ("sgh", q)], writes=[("sgh", q)])
                    TT(mg[:, m, bsl], m1[q][:], sgh[q][:], ALU.add, reads=[("m1", q), ("sgh", q)], writes=["mg"])
            DMA("sp", fm(mg_d), mg[:], reads=["mg"], writes=["mg_d"])
            end_phase("M1")

        def post_norm_residual(yT, xb, gg_idx, blk, dst_dram, sqb, tx, rs, dkey):
            bsl = slice(blk * 512, (blk + 1) * 512)
            ACT(rs[:], ps[7][:], AF.Sqrt, reads=[PK[7]], writes=["rs"], scale=1.0 / D, bias=EPS)
            S.op("dve", lambda e: e.reciprocal(out=rs[:], in_=rs[:]), ["rs"], ["rs"])
            for c in range(16):
                q = c % 2
                TT(tx[q][:], yT[:, c, :], rs[:], ALU.mult, reads=["yT", "rs"], writes=[("tx", q)])
                STT(xb[:, c, :], tx[q][:], der[:, gg_idx, c:c + 1], xb[:, c, :], ALU.mult, ALU.add,
                    reads=[("tx", q), "xb"], writes=["xb"])
            DMA("sp", fm(dst_dram)[:, :, bsl], xb[:], reads=["xb"], writes=[dkey])

        with contextlib.ExitStack() as es:
            mg = sbt(es, "mg", [128, 16, T], BF16)
            yT = sbt(es, "yT", [128, 16, 512])
            xb = sbt(es, "xb", [128, 16, 512])
            sqb = [sbt(es, "sqb%d" % q, [128, 512], BF16) for q in range(2)]
            tx = [sbt(es, "tx%d" % q, [128, 512]) for q in range(2)]
            rs = sbt(es, "rs", [128, 512])
            DMA("sp", mg[:], fm(mg_d), writes=["mg"])
            for blk in range(2):
                bsl = slice(blk * 512, (blk + 1) * 512)
                DMA("sp", xb[:], fm(xo)[:, :, bsl], writes=["xb"])
                for m in range(16):
                    si = load_slab(w_o, 0, 16, m * 128)
                    q = m % 2
                    mm_slab(ps[q][:], si, 16, lambda k: mg[:, k, bsl], ["mg"], PK[q])
                    ACT(yT[:, m, :], ps[q][:], AF.Copy, reads=[PK[q]], writes=["yT"])
                    ACT(sqb[q][:], ps[q][:], AF.Square, reads=[PK[q]], writes=[("sqb", q)])
                    MM(ps[7][:], onesb[:], sqb[q][:], m == 0, m == 15, reads=[("sqb", q)], writes=[PK[7]])
                post_norm_residual(yT, xb, 3, blk, x1_d, sqb, tx, rs, "x1_d")
                if debug:
                    DMA("sp", fm(dbg["d_x1"])[:, :, bsl], xb[:], reads=["xb"], writes=["dd"])
            end_phase("M2")

        with contextlib.ExitStack() as es:
            yT = sbt(es, "yT", [128, 16, 512])
            xb = sbt(es, "xb", [128, 16, 512])
            h2 = sbt(es, "h2", [128, 16, 512], BF16)
            z = sbt(es, "z", [128, 64, 512], BF16)
            sqb = [sbt(es, "sqb%d" % q, [128, 512], BF16) for q in range(2)]
            tx = [sbt(es, "tx%d" % q, [128, 512]) for q in range(2)]
            rs = sbt(es, "rs", [128, 512])
            for blk in range(2):
                bsl = slice(blk * 512, (blk + 1) * 512)
                DMA("sp", xb[:], fm(x1_d)[:, :, bsl], writes=["xb"])
                for c in range(16):
                    q = c % 2
                    ACT(sqb[q][:], xb[:, c, :], AF.Square, reads=["xb"], writes=[("sqb", q)])
                    MM(ps[7][:], onesb[:], sqb[q][:], c == 0, c == 15, reads=[("sqb", q)], writes=[PK[7]])
                ACT(rs[:], ps[7][:], AF.Sqrt, reads=[PK[7]], writes=["rs"], scale=1.0 / D, bias=EPS)
                S.op("dve", lambda e: e.reciprocal(out=rs[:], in_=rs[:]), ["rs"], ["rs"])
                for c in range(16):
                    q = c % 2
                    TT(tx[q][:], xb[:, c, :], rs[:], ALU.mult, reads=["xb", "rs"], writes=[("tx", q)])
                    ACT(h2[:, c, :], tx[q][:], AF.Identity, reads=[("tx", q)], writes=["h2"], scale=gs_f(c), bias=sh_f(c))
                for m in range(64):
                    si = load_slab(w1, 0, 16, m * 128)
                    q = m % 2
                    mm_slab(ps[q][:], si, 16, lambda k: h2[:, k, :], ["h2"], PK[q])
                    ACT(tx[q][:], ps[q][:], AF.Relu, reads=[PK[q]], writes=[("tx", q)])
                    TT(z[:, m, :], tx[q][:], tx[q][:], ALU.mult, reads=[("tx", q)], writes=["z"])
                for m in range(16):
                    q = m % 2
                    for kq in range(4):
                        si = load_slab(w2, kq * 2048, 16, m * 128)
                        for k in range(16):
                            MM(ps[2 + q][:], slabs[si][:, k, :], z[:, kq * 16 + k, :], kq == 0 and k == 0, kq == 3 and k == 15,
                               reads=[("slab", si), "z"], writes=[PK[2 + q]])
                    ACT(yT[:, m, :], ps[2 + q][:], AF.Copy, reads=[PK[2 + q]], writes=["yT"])
                    ACT(sqb[q][:], ps[2 + q][:], AF.Square, reads=[PK[2 + q]], writes=[("sqb", q)])
                    MM(ps[7][:], onesb[:], sqb[q][:], m == 0, m == 15, reads=[("sqb", q)], writes=[PK[7]])
                post_norm_residual(yT, xb, 4, blk, out, sqb, tx, rs, "out")
            S.emit_phase(final=True)
    return nc


def _cols(v):
    v = np.asarray(v, np.float32)
    return np.ascontiguousarray(v.reshape(-1, 128).T)


def _consts():
    c = np.zeros((128, 384), np.float32)
    c[:, 0:128] = np.eye(128, dtype=np.float32)
    s = np.arange(128)[:, None]
    t = np.arange(128)[None, :]
    same = (s // 64) == (t // 64)
    c[:, 128:256] = (same & (s <= t)).astype(np.float32)
    c[:, 256:384] = (same & (s >= t)).astype(np.float32)
    return c


def _prep(inp):
    f = lambda k: np.asarray(inp[k], np.float32)
    x, c, ctx, c_ctx = f("x"), f("c"), f("ctx"), f("c_ctx")
    w_in = np.ascontiguousarray(f("w_in")[0])
    shared = {
        "w_mod": np.ascontiguousarray(f("w_mod")[0]), "w_in": w_in,
        "w_pw": np.ascontiguousarray(f("conv_pw_w")[0]), "w_ho": np.ascontiguousarray(f("hgrn_out_w")[0]),
        "w_o": np.ascontiguousarray(f("w_out")[0]), "w1": np.ascontiguousarray(f("mlp_w1")[0]),
        "w2": np.ascontiguousarray(f("mlp_w2")[0]), "cst": _consts(),
    }
    wf = [np.ascontiguousarray(w_in[:, C_FF:C_FF + DH]), np.ascontiguousarray(w_in[:, C_FB:C_FB + DH])]
    lbl = f("hgrn_lb_logits")
    cw = f("conv_dw_w")[0]
    in_maps = []
    for core in range(8):
        b, j = divmod(core, 4)
        P = np.zeros((128, NP), np.float32)

        def put(name, arr, i=0):
            o, w = PL[name]
            arr = np.asarray(arr, np.float32)
            P[:, o + i:o + i + arr.shape[1]] = arr
        put("npm", _cols(f("norm_pre_mix")[0]))
        put("npo", _cols(f("norm_post_mix")[0]))
        put("npl", _cols(f("norm_pre_mlp")[0]))
        put("npo2", _cols(f("norm_post_mlp")[0]))
        put("bmod", _cols(f("b_mod")[0]))
        put("cw", np.ascontiguousarray(cw.T.reshape(8, 128, 31).transpose(1, 0, 2).reshape(128, 8 * 31)))
        put("cb", _cols(f("conv_dw_b")[0]))
        put("lng", _cols(f("conv_ln_g")[0]))
        put("lnb", _cols(f("conv_ln_b")[0]))
        put("hng", _cols(f("hgrn_norm_g")[0]))
        slots = [("f", s) for s in range(j - 1, -1, -1)] + [("b", s) for s in range(j + 1, 4)]
        dirs = [0 if d == "f" else 1 for d, _ in slots]
        l0 = [lbl[0, 0], lbl[1, 0]] + [lbl[d, 0] for d in dirs]
        l1 = [lbl[0, 1], lbl[1, 1]] + [lbl[d, 1] for d in dirs]
        put("l0", np.concatenate([_cols(v) for v in l0], 1))
        put("l1", np.concatenate([_cols(v) for v in l1], 1))
        keep = [0.0] + [1.0 if dirs[s] == dirs[s - 1] else 0.0 for s in (1, 2)]
        put("slk", np.tile(np.array(keep, np.float32)[None], (128, 1)))
        put("sla", np.tile(np.array([1.0 - d for d in dirs], np.float32)[None], (128, 1)))
        put("slb", np.tile(np.array([float(d) for d in dirs], np.float32)[None], (128, 1)))
        put("hm", np.tile(np.array([1.0 if j > 0 else 0.0, 1.0 if j < 3 else 0.0], np.float32)[None], (128, 1)))
        cp = np.stack([_cols(c[b]), _cols(c_ctx)], 2).reshape(128, 32)
        put("cp", cp)
        xb_ = x[b]
        xo = np.ascontiguousarray(xb_[1024 * j:1024 * (j + 1)].T)
        xh = np.zeros((D, NHALO), np.float32)
        if j > 0:
            xh[:, 0:960] = xb_[1024 * j - 960:1024 * j].T
        if j < 3:
            xh[:, 960:1920] = xb_[1024 * (j + 1):1024 * (j + 1) + 960].T
        xs = np.zeros((3 * D, T), np.float32)
        wfs = np.zeros((3 * D, DH), np.float32)
        for s, (d, seg) in enumerate(slots):
            blk = xb_[1024 * seg:1024 * (seg + 1)]
            if d == "f":
                blk = blk[::-1]
            xs[s * D:(s + 1) * D] = blk.T
            wfs[s * D:(s + 1) * D] = wf[0 if d == "f" else 1]
        ctx2 = np.concatenate([ctx[b][::-1].T, ctx[b].T], 1)
        m = dict(shared)
        m.update({"xo": xo, "xh": xh, "xs": xs, "wfs": wfs, "ctx2": np.ascontiguousarray(ctx2), "prm": P})
        in_maps.append(m)
    return in_maps


_NC_CACHE = {}


def kernel(**inputs):
    in_maps = _prep(inputs)
    if "nc" not in _NC_CACHE:
        _NC_CACHE["nc"] = build()
    res = run_bass_kernel_spmd(_NC_CACHE["nc"], in_maps, core_ids=list(range(8)))
    out = np.zeros((2, 4096, D), np.float32)
    for core in range(8):
        b, j = divmod(core, 4)
        out[b, 1024 * j:1024 * (j + 1)] = np.asarray(res.results[core]["out"], np.float32).T
    return out
```

```python
import contextlib
import os
import numpy as np
import concourse.bass as bass
import concourse.mybir as mybir
from concourse.bass_utils import run_bass_kernel_spmd

F32 = mybir.dt.float32
BF16 = mybir.dt.bfloat16
AF = mybir.ActivationFunctionType
ALU = mybir.AluOpType

D = 2048
T = 1024
DH = 1024
DC = 1024
DFF = 8192
EPS = 1e-6
NHALO = 1920
C_I, C_FF, C_FB, C_Q, C_G, C_CV, C_CG, C_GC, C_GH = 0, 1024, 2048, 3072, 4096, 5120, 6144, 7168, 9216

PL = {}
_off = 0
for _n, _w in [("npm", 16), ("npo", 16), ("npl", 16), ("npo2", 16), ("bmod", 96), ("cw", 8 * 31),
               ("cb", 8), ("lng", 8), ("lnb", 8), ("hng", 8), ("l0", 40), ("l1", 40),
               ("slk", 3), ("sla", 3), ("slb", 3), ("hm", 2), ("cp", 32)]:
    PL[_n] = (_off, _w)
    _off += _w
NP = _off


class Sched:
    COMPUTE = ("pe", "act", "dve", "pool")
    NSP = 6

    def __init__(self, nc, sems, dma_streams):
        self.nc = nc
        self.sems = sems
        self.issuers = ("pe", "act", "dve", "pool", "sp")
        self.dma_streams = tuple(dma_streams)
        self.streams = self.COMPUTE + self.dma_streams
        self.ops = {e: [] for e in self.issuers}
        self.cnt = {s: 0 for s in self.streams}
        self.seen = {e: {s: 0 for s in self.streams} for e in self.issuers}
        self.state = {}
        self.phase_base = {s: 0 for s in self.streams}
        self.sp_rr = 0

    def _st(self, b):
        s = self.state.get(b)
        if s is None:
            s = {"w": None, "r": {}}
            self.state[b] = s
        return s

    def op(self, eng, fn, reads=(), writes=(), dma=False, dstream=None):
        deps = {}
        if dma:
            if dstream is None:
                dstream = "sp%d" % (self.sp_rr % self.NSP)
                self.sp_rr += 1
            stream = dstream
            inc = 16
            if self.cnt[stream] > 0:
                deps[stream] = self.cnt[stream]
        else:
            stream = eng
            inc = 1
        for b in reads:
            w = self._st(b)["w"]
            if w is not None:
                deps[w[0]] = max(deps.get(w[0], 0), w[1])
        for b in writes:
            st = self._st(b)
            if st["w"] is not None:
                w = st["w"]
                deps[w[0]] = max(deps.get(w[0], 0), w[1])
            for s2, c in st["r"].items():
                deps[s2] = max(deps.get(s2, 0), c)
        waits = []
        for s2, c in deps.items():
            if s2 == "pe" and eng == "pe":
                continue
            if c > self.seen[eng][s2]:
                waits.append((s2, c))
                self.seen[eng][s2] = c
        self.cnt[stream] += inc
        my = self.cnt[stream]
        self.ops[eng].append((waits, fn, stream))
        for b in reads:
            st = self._st(b)
            st["r"][stream] = max(st["r"].get(stream, 0), my)
        for b in writes:
            st = self._st(b)
            st["w"] = (stream, my)
            st["r"] = {}
        return my

    def emit_phase(self, final=False):
        nc = self.nc
        sems = self.sems
        base = dict(self.phase_base)
        finals = [(s, c) for s, c in self.cnt.items() if c > 0]
        ops = self.ops
        dmas = set(self.dma_streams)

        def make(engname):
            def body(eng):
                for s2, c in base.items():
                    if c > 0:
                        eng.wait_ge(sems[s2], c)
                for waits, fn, stream in ops[engname]:
                    for s2, c in waits:
                        eng.wait_ge(sems[s2], c)
                    inst = fn(eng)
                    inst.then_inc(sems[stream], 16 if stream in dmas else 1)
                if final and engname == "sp":
                    for s2, c in finals:
                        eng.wait_ge(sems[s2], c)
            return body

        with nc.Block() as block:
            block.tensor(make("pe"))
            block.scalar(make("act"))
            block.vector(make("dve"))
            block.gpsimd(make("pool"))
            block.sync(make("sp"))
        self.ops = {e: [] for e in self.issuers}
        self.phase_base = dict(self.cnt)
        for e in self.issuers:
            for s2 in self.streams:
                self.seen[e][s2] = self.cnt[s2]
        self.state = {}


def fm(ap):
    return ap.rearrange("(c p) t -> p c t", p=128)


class _Stop(Exception):
    pass


def build(debug=False, upto=None):
    try:
        return _build(debug, upto)
    except _Stop as e:
        return e.args[0]


def _build(debug, upto):
    nc = bass.Bass("TRN2", target_bir_lowering=False)

    def din(name, shape, dt=F32):
        return nc.dram_tensor(name, shape, dt, kind="ExternalInput").ap()

    xo = din("xo", [D, T])
    xh = din("xh", [D, NHALO])
    xs = din("xs", [3 * D, T])
    wfs = din("wfs", [3 * D, DH])
    ctx2 = din("ctx2", [D, 512])
    prm = din("prm", [128, NP])
    cst = din("cst", [128, 384])
    w_mod = din("w_mod", [D, 6 * D])
    w_in = din("w_in", [D, 11264])
    w_pw = din("w_pw", [DC, D])
    w_ho = din("w_ho", [DH, D])
    w_o = din("w_o", [D, D])
    w1 = din("w1", [D, DFF])
    w2 = din("w2", [DFF, D])
    out = nc.dram_tensor("out", [D, T], F32, kind="ExternalOutput").ap()

    def dscr(name, shape, dt):
        return nc.dram_tensor(name, shape, dt, kind="Internal").ap()

    hT_d = dscr("hT_d", [D, T], BF16)
    uvh_d = dscr("uvh_d", [512, NHALO], F32)
    cN_d = dscr("cN_d", [DC, T], BF16)
    oN_d = dscr("oN_d", [DH, T], BF16)
    mg_d = dscr("mg_d", [D, T], BF16)
    x1_d = dscr("x1_d", [D, T], F32)
    dbg = {}
    if debug:
        for n, sh in [("d_mod", [128, 192]), ("d_sin", [128, 2048]), ("d_oN", [DH, T]), ("d_cN", [DC, T]),
                      ("d_x1", [D, T]), ("d_hT", [D, T])]:
            dbg[n] = nc.dram_tensor(n, sh, F32, kind="ExternalOutput").ap()

    with contextlib.ExitStack() as top:
        NS = 6
        dstreams = ["slab%d" % i for i in range(NS)] + ["sp%d" % i for i in range(Sched.NSP)]
        sems = {s: top.enter_context(nc.semaphore("s_" + s)) for s in list(Sched.COMPUTE) + dstreams}
        S = Sched(nc, sems, dstreams)

        uniq = [0]

        def sbt(es, name, shape, dt=F32):
            uniq[0] += 1
            return es.enter_context(nc.sbuf_tensor("%s_%d" % (name, uniq[0]), shape, dt))

        ps = [top.enter_context(nc.psum_tensor("ps%d" % i, [128, 512], F32)) for i in range(8)]
        PK = [("ps", i) for i in range(8)]

        Pt = sbt(top, "Pt", [128, NP])
        Ct = sbt(top, "Ct", [128, 384])
        identb = sbt(top, "identb", [128, 128], BF16)
        onesb = sbt(top, "onesb", [128, 128], BF16)
        onesf = sbt(top, "onesf", [128, T])
        cmask = sbt(top, "cmask", [128, T])
        scb = sbt(top, "scb", [128, 32], BF16)
        lbt = sbt(top, "lbt", [128, 40])
        oml = sbt(top, "oml", [128, 40])
        noml = sbt(top, "noml", [128, 40])
        modT = sbt(top, "modT", [128, 96, 2])
        der = sbt(top, "der", [128, 5, 16])
        Sin = sbt(top, "Sin", [128, 2, 8, 128])
        slabs = [sbt(top, "slab%d" % i, [128, 16, 128], BF16) for i in range(NS)]
        slab_ctr = [0]

        def pcol(name, i=0, n=1):
            o, w = PL[name]
            return Pt[:, o + i:o + i + n]

        def end_phase(name):
            S.emit_phase(final=(name == upto))
            if name == upto:
                raise _Stop(nc)

        def DMA(eng, out_, in_, reads=(), writes=(), dstream=None):
            S.op(eng, lambda e: e.dma_start(out=out_, in_=in_), reads, writes, dma=True, dstream=dstream)

        def ACT(out_, in_, func, reads=(), writes=(), **kw):
            S.op("act", lambda e: e.activation(out=out_, in_=in_, func=func, **kw), reads, writes)

        def TT(out_, in0, in1, op, reads=(), writes=(), eng="dve"):
            S.op(eng, lambda e: e.tensor_tensor(out=out_, in0=in0, in1=in1, op=op), reads, writes)

        def TS(out_, in0, s1, s2, op0, op1=None, reads=(), writes=(), eng="dve"):
            if op1 is None:
                S.op(eng, lambda e: e.tensor_scalar(out=out_, in0=in0, scalar1=s1, scalar2=None, op0=op0), reads, writes)
            else:
                S.op(eng, lambda e: e.tensor_scalar(out=out_, in0=in0, scalar1=s1, scalar2=s2, op0=op0, op1=op1), reads, writes)

        def STT(out_, in0, scalar, in1, op0, op1, reads=(), writes=()):
            S.op("dve", lambda e: e.scalar_tensor_tensor(out=out_, in0=in0, scalar=scalar, in1=in1, op0=op0, op1=op1), reads, writes)

        def CP(out_, in_, reads=(), writes=(), eng="dve"):
            S.op(eng, lambda e: e.tensor_copy(out=out_, in_=in_), reads, writes)

        def MM(out_, lhsT, rhs, start, stop, reads=(), writes=()):
            S.op("pe", lambda e: e.matmul(out_, lhsT=lhsT, rhs=rhs, start=start, stop=stop), reads, writes)

        def TR(out_, in_, reads=(), writes=()):
            S.op("pe", lambda e: e.transpose(out_, in_, identb[:]), reads, writes)

        def MS(ap, val, writes=()):
            S.op("pool", lambda e: e.memset(ap, val), (), writes)

        def load_slab(W, row0, nk, col0):
            i = slab_ctr[0] % NS
            slab_ctr[0] += 1
            src = W[row0:row0 + nk * 128, col0:col0 + 128].rearrange("(k p) n -> p k n", p=128)
            DMA("pool", slabs[i][:, 0:nk, :], src, writes=[("slab", i)], dstream="slab%d" % i)
            return i

        def mm_slab(out_, si, nk, rhs_fn, rkeys, wkey):
            for k in range(nk):
                MM(out_, slabs[si][:, k, :], rhs_fn(k), k == 0, k == nk - 1,
                   reads=[("slab", si)] + list(rkeys), writes=[wkey])

        def bfview(p):
            return p[:].bitcast(BF16)

        DMA("sp", Pt[:], prm[:, :], writes=["Pt"])
        DMA("sp", Ct[:], cst[:, :], writes=["Ct"])
        CP(identb[:], Ct[:, 0:128], reads=["Ct"], writes=["identb"])
        MS(onesb[:], 1.0, writes=["onesb"])
        MS(onesf[:], 1.0, writes=["onesf"])
        MS(cmask[:], 1.0, writes=["cmask"])
        MS(cmask[:].rearrange("p (c t) -> p c t", t=64)[:, :, 0:1], 0.0, writes=["cmask"])
        MS(Sin[:], 0.0, writes=["Sin"])
        o_cp = PL["cp"][0]
        ACT(scb[:], Pt[:, o_cp:o_cp + 32], AF.Silu, reads=["Pt"], writes=["scb"])
        o_l0, o_l1 = PL["l0"][0], PL["l1"][0]
        TT(lbt[:], Pt[:, o_l0:o_l0 + 40], Pt[:, o_l1:o_l1 + 40], ALU.subtract, reads=["Pt"], writes=["lbt"])
        ACT(lbt[:], lbt[:], AF.Sigmoid, reads=["lbt"], writes=["lbt"])
        TS(oml[:], lbt[:], -1.0, 1.0, ALU.mult, ALU.add, reads=["lbt"], writes=["oml"])
        TS(noml[:], lbt[:], -1.0, None, ALU.add, reads=["lbt"], writes=["noml"])
        scb3 = scb[:].rearrange("p (c r) -> p c r", r=2)
        for m in range(32):
            si = load_slab(w_mod, 0, 16, m * 128)
            mm_slab(ps[0][:, 2 * m:2 * m + 2], si, 16, lambda k: scb3[:, k, :], ["scb"], PK[0])
        o_bm = PL["bmod"][0]
        TT(modT[:, 0:32, :], ps[0][:, 0:64].rearrange("p (m r) -> p m r", r=2),
           Pt[:, o_bm:o_bm + 32].unsqueeze(2).to_broadcast([128, 32, 2]), ALU.add,
           reads=[PK[0], "Pt"], writes=["modT"])
        mod_next = [32]

        def mod_tail_step(n=1):
            for _ in range(n):
                m = mod_next[0]
                if m >= 96:
                    return
                mod_next[0] += 1
                si = load_slab(w_mod, 0, 16, m * 128)
                mm_slab(ps[7][:, 2 * (m - 32):2 * (m - 32) + 2], si, 16, lambda k: scb3[:, k, :], (), PK[7])
        o_npm, o_npo, o_npl, o_npo2 = PL["npm"][0], PL["npo"][0], PL["npl"][0], PL["npo2"][0]
        STT(der[:, 0, :], modT[:, 16:32, 0], 1.0, Pt[:, o_npm:o_npm + 16], ALU.add, ALU.mult, reads=["modT", "Pt"], writes=["der"])
        STT(der[:, 1, :], modT[:, 16:32, 1], 1.0, Pt[:, o_npm:o_npm + 16], ALU.add, ALU.mult, reads=["modT", "Pt"], writes=["der"])
        end_phase("P0")

        def make_h(tmp, src, n, gs_col, sh_col, out_fn, okey, sb=7):
            xb, sqb, tx, rs = tmp["xb"], tmp["sqb"], tmp["tx"], tmp["rs"]
            DMA("sp", xb[:, :, 0:n], fm(src), writes=["xb"])
            for c in range(16):
                q = c % 2
                ACT(sqb[q][:, 0:n], xb[:, c, 0:n], AF.Square, reads=["xb"], writes=[("sqb", q)])
                MM(ps[sb][:, 0:n], onesb[:], sqb[q][:, 0:n], c == 0, c == 15, reads=[("sqb", q)], writes=[PK[sb]])
            ACT(rs[:, 0:n], ps[sb][:, 0:n], AF.Sqrt, reads=[PK[sb]], writes=["rs"], scale=1.0 / D, bias=EPS)
            S.op("dve", lambda e: e.reciprocal(out=rs[:, 0:n], in_=rs[:, 0:n]), ["rs"], ["rs"])
            for c in range(16):
                q = c % 2
                TT(tx[q][:, 0:n], xb[:, c, 0:n], rs[:, 0:n], ALU.mult, reads=["xb", "rs"], writes=[("tx", q)])
                ACT(out_fn(c), tx[q][:, 0:n], AF.Identity, reads=[("tx", q)], writes=[okey],
                    scale=gs_col(c), bias=sh_col(c))

        def h_tmps(es):
            return {"xb": sbt(es, "xb", [128, 16, 512]),
                    "sqb": [sbt(es, "sqb%d" % q, [128, 512], BF16) for q in range(2)],
                    "tx": [sbt(es, "tx%d" % q, [128, 512]) for q in range(2)],
                    "rs": sbt(es, "rs", [128, 512])}

        gs_m = lambda c: der[:, 0, c:c + 1]
        gsc_m = lambda c: der[:, 1, c:c + 1]
        gs_f = lambda c: der[:, 2, c:c + 1]
        sh_m = lambda c: modT[:, c, 0:1]
        shc_m = lambda c: modT[:, c, 1:2]
        sh_f = lambda c: modT[:, 48 + c, 0:1]

        with contextlib.ExitStack() as es:
            tmp = h_tmps(es)
            hR = sbt(es, "hR", [128, 16, T], BF16)
            vR = sbt(es, "vR", [128, 8, 1024], BF16)
            sg2 = [sbt(es, "sg%d" % p, [128, T]) for p in range(2)]
            lg2 = [sbt(es, "lg%d" % p, [128, T]) for p in range(2)]
            kk2 = [sbt(es, "kk%d" % p, [128, T]) for p in range(2)]
            Pc2 = [sbt(es, "Pc%d" % p, [128, T]) for p in range(2)]
            khat2 = [sbt(es, "khat%d" % p, [128, T], BF16) for p in range(2)]
            khT2 = [sbt(es, "khT%d" % p, [128, T], BF16) for p in range(2)]
            G = sbt(es, "G", [128, 8])
            Gb = sbt(es, "Gb", [128, 8])
            GA = sbt(es, "GA", [128, 2, 8])
            SA = sbt(es, "SA", [128, 2, 8, 128])
            Sctx = sbt(es, "Sctx", [128, 2, 8, 128])
            MS(GA[:], 0.0, writes=["GA"])
            MS(SA[:], 0.0, writes=["SA"])
            MS(G[:], 0.0, writes=["G"])

            def region(src, nt, gs_col, sh_col, wf, wf_row0, wf_col0, lb0, post):
                bs = min(512, nt)
                for blk in range(nt // bs):
                    make_h(tmp, src[:, blk * bs:(blk + 1) * bs], bs, gs_col, sh_col,
                           lambda c, blk=blk: hR[:, c, blk * bs:(blk + 1) * bs], "hR", sb=5)
                ntile = nt // 128
                def stageA(hd):
                    p = hd % 2
                    sg, lg, kk, Pc, khat = sg2[p], lg2[p], kk2[p], Pc2[p], khat2[p]
                    fbank = (0, 1) if p == 0 else (2, 3)
                    K = lambda n: (n, p)
                    mod_tail_step(2)
                    sv_ = load_slab(w_in, 0, 16, C_I + hd * 128)
                    for hf in range((ntile + 3) // 4):
                        n4 = min(4, ntile - hf * 4)
                        for t4 in range(n4):
                            tl = hf * 4 + t4
                            for k in range(16):
                                MM(ps[4][:, t4 * 128:(t4 + 1) * 128], hR[:, k, tl * 128:(tl + 1) * 128],
                                   slabs[sv_][:, k, :], k == 0, k == 15, reads=["hR", ("slab", sv_)], writes=[PK[4]])
                        ACT(vR[:, hf * 4:hf * 4 + n4, hd * 128:(hd + 1) * 128],
                            ps[4][:, 0:n4 * 128].rearrange("p (a b) -> p a b", b=128), AF.Copy,
                            reads=[PK[4]], writes=[("vR", hd)])
                    si = load_slab(wf, wf_row0, 16, wf_col0 + hd * 128)
                    for blk in range(nt // bs):
                        pb = ps[fbank[blk]]
                        mm_slab(pb[:, 0:bs], si, 16, lambda k, blk=blk: hR[:, k, blk * bs:(blk + 1) * bs], ["hR"], PK[fbank[blk]])
                        ACT(sg[:, blk * bs:(blk + 1) * bs], pb[:, 0:bs], AF.Sigmoid, reads=[PK[fbank[blk]]], writes=[K("sg")])
                    li = lb0 + hd
                    ACT(lg[:, 0:nt], sg[:, 0:nt], AF.Ln, reads=[K("sg")], writes=[K("lg")], scale=oml[:, li:li + 1], bias=lbt[:, li:li + 1])
                    TS(kk[:, 0:nt], sg[:, 0:nt], noml[:, li:li + 1], oml[:, li:li + 1], ALU.mult, ALU.add, reads=[K("sg")], writes=[K("kk")])
                    S.op("dve", lambda e, hd=hd, Pc=Pc, lg=lg: e.tensor_tensor_scan(out=Pc[:, 0:nt], data0=onesf[:, 0:nt], data1=lg[:, 0:nt],
                                                                     initial=G[:, hd:hd + 1], op0=ALU.mult, op1=ALU.add),
                         [K("lg"), "G"], [K("Pc")])
                    CP(G[:, hd:hd + 1], Pc[:, nt - 1:nt], reads=[K("Pc")], writes=["G"])
                    TT(sg[:, 0:nt], Pc[:, 0:nt], lg[:, 0:nt], ALU.subtract, reads=[K("Pc"), K("lg")], writes=[K("sg")])
                    ACT(lg[:, 0:nt], sg[:, 0:nt], AF.Exp, reads=[K("sg")], writes=[K("lg")])
                    TT(khat[:, 0:nt], kk[:, 0:nt], lg[:, 0:nt], ALU.mult, reads=[K("kk"), K("lg")], writes=[K("khat")])

                def stageB(hd):
                    p = hd % 2
                    khat, khT = khat2[p], khT2[p]
                    K = lambda n: (n, p)
                    sbank = 6
                    pT = bfview(ps[5])
                    for tl in range(ntile):
                        TR(pT[:, tl * 128:(tl + 1) * 128], khat[:, tl * 128:(tl + 1) * 128], reads=[K("khat")], writes=[PK[5]])
                    ACT(khT[:, 0:nt], pT[:, 0:nt], AF.Copy, reads=[PK[5]], writes=[K("khT")])
                    for tl in range(ntile):
                        MM(ps[sbank][:, 0:128], khT[:, tl * 128:(tl + 1) * 128], vR[:, tl, hd * 128:(hd + 1) * 128],
                           tl == 0, tl == ntile - 1, reads=[K("khT"), ("vR", hd)], writes=[PK[sbank]])
                    post(hd, ps[sbank][:, 0:128], PK[sbank])

                stageA(0)
                for hd in range(8):
                    if hd + 1 < 8:
                        stageA(hd + 1)
                    stageB(hd)

            for d in range(2):
                MS(G[:], 0.0, writes=["G"])
                region(ctx2[:, d * 256:(d + 1) * 256], 256, gsc_m, shc_m, w_in, 0, C_FF if d == 0 else C_FB, 8 * d,
                       lambda hd, acc, pk, d=d: CP(Sctx[:, d, hd, :], acc, reads=[pk], writes=["Sctx"]))

            for s in range(3):
                if s == 0:
                    MS(G[:], 0.0, writes=["G"])
                else:
                    TS(G[:], G[:], pcol("slk", s), None, ALU.mult, reads=["G"], writes=["G"])
                CP(Gb[:], G[:], reads=["G"], writes=["Gb"])

                def post_slot(hd, acc, pk, s=s):
                    for d in range(2):
                        STT(SA[:, d, hd, :], acc, pcol("sla" if d == 0 else "slb", s), SA[:, d, hd, :], ALU.mult, ALU.add,
                            reads=[pk, "SA"], writes=["SA"])
                region(xs[s * D:(s + 1) * D, :], T, gs_m, sh_m, wfs, s * D, 0, 16 + 8 * s, post_slot)
                TT(Gb[:], G[:], Gb[:], ALU.subtract, reads=["G", "Gb"], writes=["Gb"])
                for d in range(2):
                    STT(GA[:, d, :], Gb[:], pcol("sla" if d == 0 else "slb", s), GA[:, d, :], ALU.mult, ALU.add,
                        reads=["Gb", "GA"], writes=["GA"])
            mod_tail_step(96)
            TT(modT[:, 32:96, :], ps[7][:, 0:128].rearrange("p (m r) -> p m r", r=2),
               Pt[:, o_bm + 32:o_bm + 96].unsqueeze(2).to_broadcast([128, 64, 2]), ALU.add,
               reads=[PK[7]], writes=["modT"])
            STT(der[:, 2, :], modT[:, 64:80, 0], 1.0, Pt[:, o_npl:o_npl + 16], ALU.add, ALU.mult, reads=["modT"], writes=["der"])
            TT(der[:, 3, :], modT[:, 32:48, 0], Pt[:, o_npo:o_npo + 16], ALU.mult, reads=["modT"], writes=["der"])
            TT(der[:, 4, :], modT[:, 80:96, 0], Pt[:, o_npo2:o_npo2 + 16], ALU.mult, reads=["modT"], writes=["der"])
            if debug:
                DMA("sp", dbg["d_mod"][:, :], modT[:].rearrange("p m r -> p (m r)"), reads=["modT"], writes=["dd"])
            ACT(GA[:], GA[:], AF.Exp, reads=["GA"], writes=["GA"])
            for d in range(2):
                for hd in range(8):
                    STT(Sin[:, d, hd, :], Sctx[:, d, hd, :], GA[:, d, hd:hd + 1], SA[:, d, hd, :], ALU.mult, ALU.add,
                        reads=["Sctx", "GA", "SA"], writes=["Sin"])
            if debug:
                DMA("sp", dbg["d_sin"][:, :], Sin[:].rearrange("p d h v -> p (d h v)"), reads=["Sin"], writes=["dd"])
            end_phase("R")

        with contextlib.ExitStack() as es:
            tmp = h_tmps(es)
            hT = sbt(es, "hT", [128, 16, T], BF16)
            for blk in range(2):
                make_h(tmp, xo[:, blk * 512:(blk + 1) * 512], 512, gs_m, sh_m,
                       lambda c, blk=blk: hT[:, c, blk * 512:(blk + 1) * 512], ("hT", blk))
                DMA("sp", fm(hT_d)[:, :, blk * 512:(blk + 1) * 512], hT[:, :, blk * 512:(blk + 1) * 512],
                    reads=[("hT", blk)], writes=["hT_d"])
            end_phase("H")

        with contextlib.ExitStack() as es:
            tmp = h_tmps(es)
            hB = sbt(es, "hB", [128, 16, 512], BF16)
            sgt = [sbt(es, "sgt%d" % q, [128, 512]) for q in range(2)]
            ub = [sbt(es, "ub%d" % q, [128, 512]) for q in range(2)]
            hoff = 0
            for hb, n in enumerate([512, 448, 512, 448]):
                side = 0 if hb < 2 else 1
                make_h(tmp, xh[:, hoff:hoff + n], n, gs_m, sh_m, lambda c: hB[:, c, 0:n], "hB")
                for cc in range(4, 8):
                    q = cc % 2
                    sv = load_slab(w_in, 0, 16, C_CV + cc * 128)
                    sgi = load_slab(w_in, 0, 16, C_CG + cc * 128)
                    mm_slab(ps[0 + 2 * q][:, 0:n], sv, 16, lambda k: hB[:, k, 0:n], ["hB"], PK[0 + 2 * q])
                    mm_slab(ps[1 + 2 * q][:, 0:n], sgi, 16, lambda k: hB[:, k, 0:n], ["hB"], PK[1 + 2 * q])
                    ACT(sgt[q][:, 0:n], ps[1 + 2 * q][:, 0:n], AF.Sigmoid, reads=[PK[1 + 2 * q]], writes=[("sgt", q)])
                    STT(ub[q][:, 0:n], ps[0 + 2 * q][:, 0:n], pcol("hm", side), sgt[q][:, 0:n], ALU.mult, ALU.mult,
                        reads=[("sgt", q), PK[0 + 2 * q]], writes=[("ub", q)])
                    DMA("sp", uvh_d[(cc - 4) * 128:(cc - 3) * 128, hoff:hoff + n], ub[q][:, 0:n],
                        reads=[("ub", q)], writes=["uvh_d"])
                hoff += n
            end_phase("C1")

        with contextlib.ExitStack() as es:
            hT = sbt(es, "hT", [128, 16, T], BF16)
            uv = sbt(es, "uv", [128, 46 * 64])
            uh = sbt(es, "uh", [128, 16, 94])
            conv = sbt(es, "conv", [128, 8, T])
            sgt = [sbt(es, "sgt%d" % q, [128, 512]) for q in range(2)]
            cb16 = [sbt(es, "cb16%d" % q, [128, 512], BF16) for q in range(2)]
            cs16 = [sbt(es, "cs16%d" % q, [128, 512], BF16) for q in range(2)]
            mean = sbt(es, "mean", [128, 512])
            rstd = sbt(es, "rstd", [128, 512])
            t1 = [sbt(es, "t1%d" % q, [128, 512]) for q in range(2)]
            cN = sbt(es, "cN", [128, 8, T], BF16)
            DMA("sp", hT[:], fm(hT_d), writes=["hT"])
            o_cw, o_cb = PL["cw"][0], PL["cb"][0]
            for cc in range(8):
                vert = cc >= 4
                if vert:
                    DMA("sp", uv[:, 0:960], uvh_d[(cc - 4) * 128:(cc - 3) * 128, 0:960], writes=["uv"])
                    DMA("sp", uv[:, 1984:2944], uvh_d[(cc - 4) * 128:(cc - 3) * 128, 960:1920], writes=["uv"])
                else:
                    MS(uh[:], 0.0, writes=["uh"])
                sv = load_slab(w_in, 0, 16, C_CV + cc * 128)
                sgi = load_slab(w_in, 0, 16, C_CG + cc * 128)
                for blk in range(2):
                    q = blk
                    mm_slab(ps[0 + 2 * q][:], sv, 16, lambda k, blk=blk: hT[:, k, blk * 512:(blk + 1) * 512], ["hT"], PK[0 + 2 * q])
                    mm_slab(ps[1 + 2 * q][:], sgi, 16, lambda k, blk=blk: hT[:, k, blk * 512:(blk + 1) * 512], ["hT"], PK[1 + 2 * q])
                    ACT(sgt[q][:], ps[1 + 2 * q][:], AF.Sigmoid, reads=[PK[1 + 2 * q]], writes=[("sgt", q)])
                    if vert:
                        TT(uv[:, 960 + blk * 512:960 + (blk + 1) * 512], ps[0 + 2 * q][:], sgt[q][:], ALU.mult,
                           reads=[PK[0 + 2 * q], ("sgt", q)], writes=["uv"])
                    else:
                        TT(uh[:, blk * 8:(blk + 1) * 8, 15:79], ps[0 + 2 * q][:].rearrange("p (r w) -> p r w", w=64),
                           sgt[q][:].rearrange("p (r w) -> p r w", w=64), ALU.mult,
                           reads=[PK[0 + 2 * q], ("sgt", q)], writes=["uh"])
                acc = conv[:, cc, :]
                for dd in range(31):
                    wcol = Pt[:, o_cw + cc * 31 + dd:o_cw + cc * 31 + dd + 1]
                    if vert:
                        src = uv[:, dd * 64:dd * 64 + 1024]
                        dst = acc
                        accin = acc
                    else:
                        src = uh[:, :, dd:dd + 64]
                        dst = acc.rearrange("p (r w) -> p r w", w=64)
                        accin = dst
                    if dd == 0:
                        TS(dst, src, wcol, Pt[:, o_cb + cc:o_cb + cc + 1], ALU.mult, ALU.add,
                           reads=["uv" if vert else "uh"], writes=[("conv", cc)])
                    else:
                        STT(dst, src, wcol, accin, ALU.mult, ALU.add,
                            reads=["uv" if vert else "uh", ("conv", cc)], writes=[("conv", cc)])
            o_lg, o_lb = PL["lng"][0], PL["lnb"][0]
            for blk in range(2):
                bsl = slice(blk * 512, (blk + 1) * 512)
                for cc in range(8):
                    q = cc % 2
                    ACT(cb16[q][:], conv[:, cc, bsl], AF.Copy, reads=[("conv", cc)], writes=[("cb16", q)])
                    ACT(cs16[q][:], conv[:, cc, bsl], AF.Square, reads=[("conv", cc)], writes=[("cs16", q)])
                    MM(ps[4][:], onesb[:], cb16[q][:], cc == 0, cc == 7, reads=[("cb16", q)], writes=[PK[4]])
                    MM(ps[5][:], onesb[:], cs16[q][:], cc == 0, cc == 7, reads=[("cs16", q)], writes=[PK[5]])
                ACT(mean[:], ps[4][:], AF.Copy, reads=[PK[4]], writes=["mean"], scale=1.0 / DC)
                TT(rstd[:], mean[:], mean[:], ALU.mult, reads=["mean"], writes=["rstd"])
                STT(rstd[:], ps[5][:], 1.0 / DC, rstd[:], ALU.mult, ALU.subtract, reads=[PK[5], "rstd"], writes=["rstd"])
                ACT(rstd[:], rstd[:], AF.Sqrt, reads=["rstd"], writes=["rstd"], bias=EPS)
                S.op("dve", lambda e: e.reciprocal(out=rstd[:], in_=rstd[:]), ["rstd"], ["rstd"])
                for cc in range(8):
                    q = cc % 2
                    TT(t1[q][:], conv[:, cc, bsl], mean[:], ALU.subtract, reads=[("conv", cc), "mean"], writes=[("t1", q)])
                    TT(t1[q][:], t1[q][:], rstd[:], ALU.mult, reads=[("t1", q), "rstd"], writes=[("t1", q)])
                    ACT(cN[:, cc, bsl], t1[q][:], AF.Silu, reads=[("t1", q)], writes=["cN"],
                        scale=Pt[:, o_lg + cc:o_lg + cc + 1], bias=Pt[:, o_lb + cc:o_lb + cc + 1])
            DMA("sp", fm(cN_d), cN[:], reads=["cN"], writes=["cN_d"])
            if debug:
                cNf = sbt(es, "cNf", [128, 8, T])
                CP(cNf[:], cN[:], reads=["cN"], writes=["cNf"])
                DMA("sp", fm(dbg["d_cN"]), cNf[:], reads=["cNf"], writes=["dd"])
            end_phase("C2")

        with contextlib.ExitStack() as es:
            hT = sbt(es, "hT", [128, 16, T], BF16)
            vtok = sbt(es, "vtok", [128, 8, 1024], BF16)
            sg = sbt(es, "sg", [128, T])
            lgd = [sbt(es, "lg%d" % d, [128, T]) for d in range(2)]
            kkd = [sbt(es, "kk%d" % d, [128, T]) for d in range(2)]
            qs = sbt(es, "qs", [128, T])
            gsil = sbt(es, "gsil", [128, T])
            aa = sbt(es, "aa", [128, T])
            ab = sbt(es, "ab", [128, T])
            tA = sbt(es, "tA", [128, T])
            tB = sbt(es, "tB", [128, T])
            qt = [sbt(es, "qt%d" % d, [128, T], BF16) for d in range(2)]
            kt = [sbt(es, "kt%d" % d, [128, T], BF16) for d in range(2)]
            kh = [sbt(es, "kh%d" % d, [128, T], BF16) for d in range(2)]
            khT = [sbt(es, "khT%d" % d, [128, T], BF16) for d in range(2)]
            dec = [sbt(es, "dec%d" % d, [128, 16]) for d in range(2)]
            Srun = [sbt(es, "Srun%d" % d, [128, 128]) for d in range(2)]
            Sb = [sbt(es, "Sb%d" % d, [128, 16, 128], BF16) for d in range(2)]
            scm = [sbt(es, "scm%d" % d, [128, T], BF16) for d in range(2)]
            osq = sbt(es, "osq", [128, 512], BF16)
            rs = sbt(es, "rsH", [128, 512])
            t1 = sbt(es, "t1H", [128, 512])
            oN = sbt(es, "oN", [128, T], BF16)
            pmask = [sbt(es, "pmask%d" % q, [128, T], BF16) for q in range(2)]
            khp = [sbt(es, "khp%d" % q, [128, T], BF16) for q in range(2)]
            khTp = [sbt(es, "khTp%d" % q, [128, T], BF16) for q in range(2)]
            for q in range(2):
                pv = pmask[q][:].rearrange("p (a h t) -> p a h t", h=2, t=64)
                MS(pv[:, :, q, :], 1.0, writes=[("pmask", q)])
                MS(pv[:, :, 1 - q, :], 0.0, writes=[("pmask", q)])
            if debug:
                oNf = sbt(es, "oNf", [128, T])
            DMA("sp", hT[:], fm(hT_d), writes=["hT"])
            for cs in range(8):
                si = load_slab(w_in, 0, 16, C_I + cs * 128)
                for tl in range(8):
                    pb = ps[tl // 4]
                    for k in range(16):
                        MM(pb[:, (tl % 4) * 128:(tl % 4 + 1) * 128], hT[:, k, tl * 128:(tl + 1) * 128],
                           slabs[si][:, k, :], k == 0, k == 15, reads=["hT", ("slab", si)], writes=[PK[tl // 4]])
                for hf in range(2):
                    ACT(vtok[:, hf * 4:hf * 4 + 4, cs * 128:(cs + 1) * 128],
                        ps[hf][:].rearrange("p (a b) -> p a b", b=128), AF.Copy, reads=[PK[hf]], writes=["vtok"])
            o_hng = PL["hng"][0]
            a3 = lambda t: t[:].rearrange("p (c t) -> p c t", t=64)
            HGS = int(os.environ.get('HGS', '9'))
            for hd in range(int(os.environ.get('HGNH', '8'))):
                def proj(colbase, pa):
                    si = load_slab(w_in, 0, 16, colbase + hd * 128)
                    for blk in range(2):
                        mm_slab(ps[pa + blk][:], si, 16, lambda k, blk=blk: hT[:, k, blk * 512:(blk + 1) * 512], ["hT"], PK[pa + blk])
                for d, cbase in ((0, C_FF), (1, C_FB)):
                    proj(cbase, 2 * d)
                    for blk in range(2):
                        ACT(sg[:, blk * 512:(blk + 1) * 512], ps[2 * d + blk][:], AF.Sigmoid, reads=[PK[2 * d + blk]], writes=["sg"])
                    li = 8 * d + hd
                    ACT(lgd[d][:], sg[:], AF.Ln, reads=["sg"], writes=[("lg", d)], scale=oml[:, li:li + 1], bias=lbt[:, li:li + 1])
                    TS(kkd[d][:], sg[:], noml[:, li:li + 1], oml[:, li:li + 1], ALU.mult, ALU.add, reads=["sg"], writes=[("kk", d)])
                proj(C_Q, 0)
                for blk in range(2):
                    ACT(qs[:, blk * 512:(blk + 1) * 512], ps[blk][:], AF.Silu, reads=[PK[blk]], writes=["qs"])
                proj(C_G, 2)
                for blk in range(2):
                    ACT(gsil[:, blk * 512:(blk + 1) * 512], ps[2 + blk][:], AF.Silu, reads=[PK[2 + blk]], writes=["gsil"])
                for d in range(2 if HGS >= 2 else 0):
                    S.op("dve", lambda e, d=d: e.tensor_tensor_scan(out=aa[:], data0=cmask[:], data1=lgd[d][:], initial=0.0,
                                                                   op0=ALU.mult, op1=ALU.add), [("lg", d)], ["aa"])
                    if d == 1:
                        TT(tA[:], lgd[1][:], aa[:], ALU.subtract, reads=[("lg", 1), "aa"], writes=["tA"])
                        TT(a3(ab), a3(tA), a3(aa)[:, :, 63:64].to_broadcast([128, 16, 64]), ALU.add, reads=["tA", "aa"], writes=["ab"])
                    edge = 63 if d == 0 else 0
                    av, ak = (aa, "aa") if d == 0 else (ab, "ab")
                    ACT(tA[:], av[:], AF.Exp, reads=[ak], writes=["tA"])
                    STT(qt[d][:], tA[:], 128.0 ** -0.5, qs[:], ALU.mult, ALU.mult, reads=["tA", "qs"], writes=[("qt", d)])
                    CP(dec[d][:], a3(tA)[:, :, edge], reads=["tA"], writes=[("dec", d)])
                    ACT(tB[:], av[:], AF.Exp, reads=[ak], writes=["tB"], scale=-1.0)
                    TT(kt[d][:], kkd[d][:], tB[:], ALU.mult, reads=[("kk", d), "tB"], writes=[("kt", d)])
                    TT(a3(tA), a3(av)[:, :, edge:edge + 1].to_broadcast([128, 16, 64]), a3(av), ALU.subtract, reads=[ak], writes=["tA"])
                    ACT(tB[:], tA[:], AF.Exp, reads=["tA"], writes=["tB"])
                    TT(kh[d][:], kkd[d][:], tB[:], ALU.mult, reads=[("kk", d), "tB"], writes=[("kh", d)])
                    if HGS < 3:
                        continue
                    pT = bfview(ps[4])
                    for q in range(2):
                        TT(khp[q][:], kh[d][:], pmask[q][:], ALU.mult, reads=[("kh", d), ("pmask", q)], writes=[("khp", q)])
                        for tl in range(8):
                            TR(pT[:, tl * 128:(tl + 1) * 128], khp[q][:, tl * 128:(tl + 1) * 128], reads=[("khp", q)], writes=[PK[4]])
                        ACT(khTp[q][:], pT[:, :], AF.Copy, reads=[PK[4]], writes=[("khTp", q)])
                    if HGS < 4:
                        continue
                    c0 = 0 if d == 0 else 15
                    CP(Srun[d][:], Sin[:, d, hd, :], writes=[("Srun", d)])
                    ACT(Sb[d][:, c0, :], Sin[:, d, hd, :], AF.Copy, writes=[("Sb", d)])
                    order = range(0, 15) if d == 0 else range(15, 0, -1)
                    for n, c in enumerate(list(order)[:int(os.environ.get('HGC', '15'))]):
                        tl, hf = c // 2, c % 2
                        slot = ps[n % 4][:, 0:128]
                        MM(slot, khTp[hf][:, tl * 128:(tl + 1) * 128], vtok[:, tl, hd * 128:(hd + 1) * 128], True, True,
                           reads=[("khTp", hf), "vtok"], writes=[PK[n % 4]])
                        TS(Srun[d][:], Srun[d][:], dec[d][:, c:c + 1], None, ALU.mult,
                           reads=[("Srun", d), ("dec", d)], writes=[("Srun", d)])
                        TT(Srun[d][:], slot, Srun[d][:], ALU.add,
                           reads=[("Srun", d), PK[n % 4]], writes=[("Srun", d)])
                        cn = c + 1 if d == 0 else c - 1
                        ACT(Sb[d][:, cn, :], Srun[d][:], AF.Copy, reads=[("Srun", d)], writes=[("Sb", d)])
                    if HGS < 5:
                        continue
                    mk = Ct[:, 128:256] if d == 0 else Ct[:, 256:384]
                    for g in range(2):
                        pb = ps[6 + d]
                        for tl in range(4):
                            tile = g * 4 + tl
                            MM(pb[:, tl * 128:(tl + 1) * 128], kt[d][:, tile * 128:(tile + 1) * 128],
                               qt[d][:, tile * 128:(tile + 1) * 128], True, True, reads=[("kt", d), ("qt", d)], writes=[PK[6 + d]])
                        TT(scm[d][:, g * 512:(g + 1) * 512].rearrange("p (a b) -> p a b", b=128),
                           pb[:].rearrange("p (a b) -> p a b", b=128),
                           mk.unsqueeze(1).to_broadcast([128, 4, 128]), ALU.mult, reads=[PK[6 + d]], writes=[("scm", d)])
                for tile in range(8 if HGS >= 6 else 0):
                    pb = ps[tile // 4]
                    po = pb[:, (tile % 4) * 128:(tile % 4 + 1) * 128]
                    vt = vtok[:, tile, hd * 128:(hd + 1) * 128]
                    MM(po, vt, scm[0][:, tile * 128:(tile + 1) * 128], True, False, reads=["vtok", ("scm", 0)], writes=[PK[tile // 4]])
                    MM(po, vt, scm[1][:, tile * 128:(tile + 1) * 128], False, False, reads=["vtok", ("scm", 1)], writes=[PK[tile // 4]])
                    for hf in range(2):
                        c = tile * 2 + hf
                        pc = pb[:, (tile % 4) * 128 + hf * 64:(tile % 4) * 128 + hf * 64 + 64]
                        MM(pc, Sb[0][:, c, :], qt[0][:, c * 64:(c + 1) * 64], False, False, reads=[("Sb", 0), ("qt", 0)], writes=[PK[tile // 4]])
                        MM(pc, Sb[1][:, c, :], qt[1][:, c * 64:(c + 1) * 64], False, hf == 1, reads=[("Sb", 1), ("qt", 1)], writes=[PK[tile // 4]])
                for blk in range(2):
                    ACT(osq[:], ps[blk][:], AF.Square, reads=[PK[blk]], writes=["osq"])
                    MM(ps[6][:], onesb[:], osq[:], True, True, reads=["osq"], writes=[PK[6]])
                    ACT(rs[:], ps[6][:], AF.Sqrt, reads=[PK[6]], writes=["rsH"], scale=1.0 / 128, bias=EPS)
                    S.op("dve", lambda e: e.reciprocal(out=rs[:], in_=rs[:]), ["rsH"], ["rsH"])
                    TT(t1[:], ps[blk][:], rs[:], ALU.mult, reads=[PK[blk], "rsH"], writes=["t1H"])
                    STT(oN[:, blk * 512:(blk + 1) * 512], t1[:], Pt[:, o_hng + hd:o_hng + hd + 1], gsil[:, blk * 512:(blk + 1) * 512],
                        ALU.mult, ALU.mult, reads=["t1H", "gsil"], writes=["oN"])
                DMA("sp", oN_d[hd * 128:(hd + 1) * 128, :], oN[:], reads=["oN"], writes=["oN_d"])
                if debug:
                    CP(oNf[:], oN[:], reads=["oN"], writes=["oNf"])
                    DMA("sp", dbg["d_oN"][hd * 128:(hd + 1) * 128, :], oNf[:], reads=["oNf"], writes=["dd"])
            end_phase("HG")

        with contextlib.ExitStack() as es:
            hT = sbt(es, "hT", [128, 16, T], BF16)
            cN = sbt(es, "cN", [128, 8, T], BF16)
            oN = sbt(es, "oNa", [128, 8, T], BF16)
            sgc = [sbt(es, "sgc%d" % q, [128, 512]) for q in range(2)]
            sgh = [sbt(es, "sgh%d" % q, [128, 512]) for q in range(2)]
            m1 = [sbt(es, "m1%d" % q, [128, 512]) for q in range(2)]
            mg = sbt(es, "mg", [128, 16, T], BF16)
            DMA("sp", hT[:], fm(hT_d), writes=["hT"])
            DMA("sp", cN[:], fm(cN_d), writes=["cN"])
            DMA("sp", oN[:], fm(oN_d), writes=["oNa"])
            for m in range(16):
                s_pw = load_slab(w_pw, 0, 8, m * 128)
                s_ho = load_slab(w_ho, 0, 8, m * 128)
                s_gc = load_slab(w_in, 0, 16, C_GC + m * 128)
                s_gh = load_slab(w_in, 0, 16, C_GH + m * 128)
                for blk in range(2):
                    q = blk
                    bsl = slice(blk * 512, (blk + 1) * 512)
                    b0 = 4 * q
                    mm_slab(ps[b0][:], s_pw, 8, lambda k: cN[:, k, bsl], ["cN"], PK[b0])
                    mm_slab(ps[b0 + 1][:], s_ho, 8, lambda k: oN[:, k, bsl], ["oNa"], PK[b0 + 1])
                    mm_slab(ps[b0 + 2][:], s_gc, 16, lambda k: hT[:, k, bsl], ["hT"], PK[b0 + 2])
                    mm_slab(ps[b0 + 3][:], s_gh, 16, lambda k: hT[:, k, bsl], ["hT"], PK[b0 + 3])
                    ACT(sgc[q][:], ps[b0 + 2][:], AF.Sigmoid, reads=[PK[b0 + 2]], writes=[("sgc", q)])
                    ACT(sgh[q][:], ps[b0 + 3][:], AF.Sigmoid, reads=[PK[b0 + 3]], writes=[("sgh", q)])
                    TT(m1[q][:], ps[b0][:], sgc[q][:], ALU.mult, reads=[PK[b0], ("sgc", q)], writes=[("m1", q)])
                    TT(sgh[q][:], ps[b0 + 1][:], sgh[q][:], ALU.mult, reads=[PK[b0 + 1], ("sgh", q)], writes=[("sgh", q)])
                    TT(mg[:, m, bsl], m1[q][:], sgh[q][:], ALU.add, reads=[("m1", q), ("sgh", q)], writes=["mg"])
            DMA("sp", fm(mg_d), mg[:], reads=["mg"], writes=["mg_d"])
            end_phase("M1")

        def post_norm_residual(yT, xb, gg_idx, blk, dst_dram, sqb, tx, rs, dkey):
            bsl = slice(blk * 512, (blk + 1) * 512)
            ACT(rs[:], ps[7][:], AF.Sqrt, reads=[PK[7]], writes=["rs"], scale=1.0 / D, bias=EPS)
            S.op("dve", lambda e: e.reciprocal(out=rs[:], in_=rs[:]), ["rs"], ["rs"])
            for c in range(16):
                q = c % 2
                TT(tx[q][:], yT[:, c, :], rs[:], ALU.mult, reads=["yT", "rs"], writes=[("tx", q)])
                STT(xb[:, c, :], tx[q][:], der[:, gg_idx, c:c + 1], xb[:, c, :], ALU.mult, ALU.add,
                    reads=[("tx", q), "xb"], writes=["xb"])
            DMA("sp", fm(dst_dram)[:, :, bsl], xb[:], reads=["xb"], writes=[dkey])

        with contextlib.ExitStack() as es:
            mg = sbt(es, "mg", [128, 16, T], BF16)
            yT = sbt(es, "yT", [128, 16, 512])
            xb = sbt(es, "xb", [128, 16, 512])
            sqb = [sbt(es, "sqb%d" % q, [128, 512], BF16) for q in range(2)]
            tx = [sbt(es, "tx%d" % q, [128, 512]) for q in range(2)]
            rs = sbt(es, "rs", [128, 512])
            DMA("sp", mg[:], fm(mg_d), writes=["mg"])
            for blk in range(2):
                bsl = slice(blk * 512, (blk + 1) * 512)
                DMA("sp", xb[:], fm(xo)[:, :, bsl], writes=["xb"])
                for m in range(16):
                    si = load_slab(w_o, 0, 16, m * 128)
                    q = m % 2
                    mm_slab(ps[q][:], si, 16, lambda k: mg[:, k, bsl], ["mg"], PK[q])
                    ACT(yT[:, m, :], ps[q][:], AF.Copy, reads=[PK[q]], writes=["yT"])
                    ACT(sqb[q][:], ps[q][:], AF.Square, reads=[PK[q]], writes=[("sqb", q)])
                    MM(ps[7][:], onesb[:], sqb[q][:], m == 0, m == 15, reads=[("sqb", q)], writes=[PK[7]])
                post_norm_residual(yT, xb, 3, blk, x1_d, sqb, tx, rs, "x1_d")
                if debug:
                    DMA("sp", fm(dbg["d_x1"])[:, :, bsl], xb[:], reads=["xb"], writes=["dd"])
            end_phase("M2")

        with contextlib.ExitStack() as es:
            yT = sbt(es, "yT", [128, 16, 512])
            xb = sbt(es, "xb", [128, 16, 512])
            h2 = sbt(es, "h2", [128, 16, 512], BF16)
            z = sbt(es, "z", [128, 64, 512], BF16)
            sqb = [sbt(es, "sqb%d" % q, [128, 512], BF16) for q in range(2)]
            tx = [sbt(es, "tx%d" % q, [128, 512]) for q in range(2)]
            rs = sbt(es, "rs", [128, 512])
            for blk in range(2):
                bsl = slice(blk * 512, (blk + 1) * 512)
                DMA("sp", xb[:], fm(x1_d)[:, :, bsl], writes=["xb"])
                for c in range(16):
                    q = c % 2
                    ACT(sqb[q][:], xb[:, c, :], AF.Square, reads=["xb"], writes=[("sqb", q)])
                    MM(ps[7][:], onesb[:], sqb[q][:], c == 0, c == 15, reads=[("sqb", q)], writes=[PK[7]])
                ACT(rs[:], ps[7][:], AF.Sqrt, reads=[PK[7]], writes=["rs"], scale=1.0 / D, bias=EPS)
                S.op("dve", lambda e: e.reciprocal(out=rs[:], in_=rs[:]), ["rs"], ["rs"])
                for c in range(16):
                    q = c % 2
                    TT(tx[q][:], xb[:, c, :], rs[:], ALU.mult, reads=["xb", "rs"], writes=[("tx", q)])
                    ACT(h2[:, c, :], tx[q][:], AF.Identity, reads=[("tx", q)], writes=["h2"], scale=gs_f(c), bias=sh_f(c))
                for m in range(64):
                    si = load_slab(w1, 0, 16, m * 128)
                    q = m % 2
                    mm_slab(ps[q][:], si, 16, lambda k: h2[:, k, :], ["h2"], PK[q])
                    ACT(tx[q][:], ps[q][:], AF.Relu, reads=[PK[q]], writes=[("tx", q)])
                    TT(z[:, m, :], tx[q][:], tx[q][:], ALU.mult, reads=[("tx", q)], writes=["z"])
                for m in range(16):
                    q = m % 2
                    for kq in range(4):
                        si = load_slab(w2, kq * 2048, 16, m * 128)
                        for k in range(16):
                            MM(ps[2 + q][:], slabs[si][:, k, :], z[:, kq * 16 + k, :], kq == 0 and k == 0, kq == 3 and k == 15,
                               reads=[("slab", si), "z"], writes=[PK[2 + q]])
                    ACT(yT[:, m, :], ps[2 + q][:], AF.Copy, reads=[PK[2 + q]], writes=["yT"])
                    ACT(sqb[q][:], ps[2 + q][:], AF.Square, reads=[PK[2 + q]], writes=[("sqb", q)])
                    MM(ps[7][:], onesb[:], sqb[q][:], m == 0, m == 15, reads=[("sqb", q)], writes=[PK[7]])
                post_norm_residual(yT, xb, 4, blk, out, sqb, tx, rs, "out")
            S.emit_phase(final=True)
    return nc


def _cols(v):
    v = np.asarray(v, np.float32)
    return np.ascontiguousarray(v.reshape(-1, 128).T)


def _consts():
    c = np.zeros((128, 384), np.float32)
    c[:, 0:128] = np.eye(128, dtype=np.float32)
    s = np.arange(128)[:, None]
    t = np.arange(128)[None, :]
    same = (s // 64) == (t // 64)
    c[:, 128:256] = (same & (s <= t)).astype(np.float32)
    c[:, 256:384] = (same & (s >= t)).astype(np.float32)
    return c


def _prep(inp):
    f = lambda k: np.asarray(inp[k], np.float32)
    x, c, ctx, c_ctx = f("x"), f("c"), f("ctx"), f("c_ctx")
    w_in = np.ascontiguousarray(f("w_in")[0])
    shared = {
        "w_mod": np.ascontiguousarray(f("w_mod")[0]), "w_in": w_in,
        "w_pw": np.ascontiguousarray(f("conv_pw_w")[0]), "w_ho": np.ascontiguousarray(f("hgrn_out_w")[0]),
        "w_o": np.ascontiguousarray(f("w_out")[0]), "w1": np.ascontiguousarray(f("mlp_w1")[0]),
        "w2": np.ascontiguousarray(f("mlp_w2")[0]), "cst": _consts(),
    }
    wf = [np.ascontiguousarray(w_in[:, C_FF:C_FF + DH]), np.ascontiguousarray(w_in[:, C_FB:C_FB + DH])]
    lbl = f("hgrn_lb_logits")
    cw = f("conv_dw_w")[0]
    in_maps = []
    for core in range(8):
        b, j = divmod(core, 4)
        P = np.zeros((128, NP), np.float32)

        def put(name, arr, i=0):
            o, w = PL[name]
            arr = np.asarray(arr, np.float32)
            P[:, o + i:o + i + arr.shape[1]] = arr
        put("npm", _cols(f("norm_pre_mix")[0]))
        put("npo", _cols(f("norm_post_mix")[0]))
        put("npl", _cols(f("norm_pre_mlp")[0]))
        put("npo2", _cols(f("norm_post_mlp")[0]))
        put("bmod", _cols(f("b_mod")[0]))
        put("cw", np.ascontiguousarray(cw.T.reshape(8, 128, 31).transpose(1, 0, 2).reshape(128, 8 * 31)))
        put("cb", _cols(f("conv_dw_b")[0]))
        put("lng", _cols(f("conv_ln_g")[0]))
        put("lnb", _cols(f("conv_ln_b")[0]))
        put("hng", _cols(f("hgrn_norm_g")[0]))
        slots = [("f", s) for s in range(j - 1, -1, -1)] + [("b", s) for s in range(j + 1, 4)]
        dirs = [0 if d == "f" else 1 for d, _ in slots]
        l0 = [lbl[0, 0], lbl[1, 0]] + [lbl[d, 0] for d in dirs]
        l1 = [lbl[0, 1], lbl[1, 1]] + [lbl[d, 1] for d in dirs]
        put("l0", np.concatenate([_cols(v) for v in l0], 1))
        put("l1", np.concatenate([_cols(v) for v in l1], 1))
        keep = [0.0] + [1.0 if dirs[s] == dirs[s - 1] else 0.0 for s in (1, 2)]
        put("slk", np.tile(np.array(keep, np.float32)[None], (128, 1)))
        put("sla", np.tile(np.array([1.0 - d for d in dirs], np.float32)[None], (128, 1)))
        put("slb", np.tile(np.array([float(d) for d in dirs], np.float32)[None], (128, 1)))
        put("hm", np.tile(np.array([1.0 if j > 0 else 0.0, 1.0 if j < 3 else 0.0], np.float32)[None], (128, 1)))
        cp = np.stack([_cols(c[b]), _cols(c_ctx)], 2).reshape(128, 32)
        put("cp", cp)
        xb_ = x[b]
        xo = np.ascontiguousarray(xb_[1024 * j:1024 * (j + 1)].T)
        xh = np.zeros((D, NHALO), np.float32)
        if j > 0:
            xh[:, 0:960] = xb_[1024 * j - 960:1024 * j].T
        if j < 3:
            xh[:, 960:1920] = xb_[1024 * (j + 1):1024 * (j + 1) + 960].T
        xs = np.zeros((3 * D, T), np.float32)
        wfs = np.zeros((3 * D, DH), np.float32)
        for s, (d, seg) in enumerate(slots):
            blk = xb_[1024 * seg:1024 * (seg + 1)]
            if d == "f":
                blk = blk[::-1]
            xs[s * D:(s + 1) * D] = blk.T
            wfs[s * D:(s + 1) * D] = wf[0 if d == "f" else 1]
        ctx2 = np.concatenate([ctx[b][::-1].T, ctx[b].T], 1)
        m = dict(shared)
        m.update({"xo": xo, "xh": xh, "xs": xs, "wfs": wfs, "ctx2": np.ascontiguousarray(ctx2), "prm": P})
        in_maps.append(m)
    return in_maps


_NC_CACHE = {}


def kernel(**inputs):
    in_maps = _prep(inputs)
    if "nc" not in _NC_CACHE:
        _NC_CACHE["nc"] = build()
    res = run_bass_kernel_spmd(_NC_CACHE["nc"], in_maps, core_ids=list(range(8)))
    out = np.zeros((2, 4096, D), np.float32)
    for core in range(8):
        b, j = divmod(core, 4)
        out[b, 1024 * j:1024 * (j + 1)] = np.asarray(res.results[core]["out"], np.float32).T
    return out
```

```python
import contextlib
import os
import numpy as np
import concourse.bass as bass
import concourse.mybir as mybir
from concourse.bass_utils import run_bass_kernel_spmd

F32 = mybir.dt.float32
BF16 = mybir.dt.bfloat16
AF = mybir.ActivationFunctionType
ALU = mybir.AluOpType

D = 2048
T = 1024
DH = 1024
DC = 1024
DFF = 8192
EPS = 1e-6
NHALO = 1920
C_I, C_FF, C_FB, C_Q, C_G, C_CV, C_CG, C_GC, C_GH = 0, 1024, 2048, 3072, 4096, 5120, 6144, 7168, 9216

PL = {}
_off = 0
for _n, _w in [("npm", 16), ("npo", 16), ("npl", 16), ("npo2", 16), ("bmod", 96), ("cw", 8 * 31),
               ("cb", 8), ("lng", 8), ("lnb", 8), ("hng", 8), ("l0", 40), ("l1", 40),
               ("slk", 3), ("sla", 3), ("slb", 3), ("hm", 2), ("cp", 32)]:
    PL[_n] = (_off, _w)
    _off += _w
NP = _off


class Sched:
    COMPUTE = ("pe", "act", "dve", "pool")
    NSP = 6

    def __init__(self, nc, sems, dma_streams):
        self.nc = nc
        self.sems = sems
        self.issuers = ("pe", "act", "dve", "pool", "sp")
        self.dma_streams = tuple(dma_streams)
        self.streams = self.COMPUTE + self.dma_streams
        self.ops = {e: [] for e in self.issuers}
        self.cnt = {s: 0 for s in self.streams}
        self.seen = {e: {s: 0 for s in self.streams} for e in self.issuers}
        self.state = {}
        self.phase_base = {s: 0 for s in self.streams}
        self.sp_rr = 0

    def _st(self, b):
        s = self.state.get(b)
        if s is None:
            s = {"w": None, "r": {}}
            self.state[b] = s
        return s

    def op(self, eng, fn, reads=(), writes=(), dma=False, dstream=None):
        deps = {}
        if dma:
            if dstream is None:
                dstream = "sp%d" % (self.sp_rr % self.NSP)
                self.sp_rr += 1
            stream = dstream
            inc = 16
            if self.cnt[stream] > 0:
                deps[stream] = self.cnt[stream]
        else:
            stream = eng
            inc = 1
        for b in reads:
            w = self._st(b)["w"]
            if w is not None:
                deps[w[0]] = max(deps.get(w[0], 0), w[1])
        for b in writes:
            st = self._st(b)
            if st["w"] is not None:
                w = st["w"]
                deps[w[0]] = max(deps.get(w[0], 0), w[1])
            for s2, c in st["r"].items():
                deps[s2] = max(deps.get(s2, 0), c)
        waits = []
        for s2, c in deps.items():
            if s2 == "pe" and eng == "pe":
                continue
            if c > self.seen[eng][s2]:
                waits.append((s2, c))
                self.seen[eng][s2] = c
        self.cnt[stream] += inc
        my = self.cnt[stream]
        self.ops[eng].append((waits, fn, stream))
        for b in reads:
            st = self._st(b)
            st["r"][stream] = max(st["r"].get(stream, 0), my)
        for b in writes:
            st = self._st(b)
            st["w"] = (stream, my)
            st["r"] = {}
        return my

    def emit_phase(self, final=False):
        nc = self.nc
        sems = self.sems
        base = dict(self.phase_base)
        finals = [(s, c) for s, c in self.cnt.items() if c > 0]
        ops = self.ops
        dmas = set(self.dma_streams)

        def make(engname):
            def body(eng):
                for s2, c in base.items():
                    if c > 0:
                        eng.wait_ge(sems[s2], c)
                for waits, fn, stream in ops[engname]:
                    for s2, c in waits:
                        eng.wait_ge(sems[s2], c)
                    inst = fn(eng)
                    inst.then_inc(sems[stream], 16 if stream in dmas else 1)
                if final and engname == "sp":
                    for s2, c in finals:
                        eng.wait_ge(sems[s2], c)
            return body

        with nc.Block() as block:
            block.tensor(make("pe"))
            block.scalar(make("act"))
            block.vector(make("dve"))
            block.gpsimd(make("pool"))
            block.sync(make("sp"))
        self.ops = {e: [] for e in self.issuers}
        self.phase_base = dict(self.cnt)
        for e in self.issuers:
            for s2 in self.streams:
                self.seen[e][s2] = self.cnt[s2]
        self.state = {}


def fm(ap):
    return ap.rearrange("(c p) t -> p c t", p=128)


class _Stop(Exception):
    pass


def build(debug=False, upto=None):
    try:
        return _build(debug, upto)
    except _Stop as e:
        return e.args[0]


def _build(debug, upto):
    nc = bass.Bass("TRN2", target_bir_lowering=False)

    def din(name, shape, dt=F32):
        return nc.dram_tensor(name, shape, dt, kind="ExternalInput").ap()

    xo = din("xo", [D, T])
    xh = din("xh", [D, NHALO])
    xs = din("xs", [3 * D, T])
    wfs = din("wfs", [3 * D, DH])
    ctx2 = din("ctx2", [D, 512])
    prm = din("prm", [128, NP])
    cst = din("cst", [128, 384])
    w_mod = din("w_mod", [D, 6 * D])
    w_in = din("w_in", [D, 11264])
    w_pw = din("w_pw", [DC, D])
    w_ho = din("w_ho", [DH, D])
    w_o = din("w_o", [D, D])
    w1 = din("w1", [D, DFF])
    w2 = din("w2", [DFF, D])
    out = nc.dram_tensor("out", [D, T], F32, kind="ExternalOutput").ap()

    def dscr(name, shape, dt):
        return nc.dram_tensor(name, shape, dt, kind="Internal").ap()

    hT_d = dscr("hT_d", [D, T], BF16)
    uvh_d = dscr("uvh_d", [512, NHALO], F32)
    cN_d = dscr("cN_d", [DC, T], BF16)
    oN_d = dscr("oN_d", [DH, T], BF16)
    mg_d = dscr("mg_d", [D, T], BF16)
    x1_d = dscr("x1_d", [D, T], F32)
    dbg = {}
    if debug:
        for n, sh in [("d_mod", [128, 192]), ("d_sin", [128, 2048]), ("d_oN", [DH, T]), ("d_cN", [DC, T]),
                      ("d_x1", [D, T]), ("d_hT", [D, T])]:
            dbg[n] = nc.dram_tensor(n, sh, F32, kind="ExternalOutput").ap()

    with contextlib.ExitStack() as top:
        NS = 6
        dstreams = ["slab%d" % i for i in range(NS)] + ["sp%d" % i for i in range(Sched.NSP)]
        sems = {s: top.enter_context(nc.semaphore("s_" + s)) for s in list(Sched.COMPUTE) + dstreams}
        S = Sched(nc, sems, dstreams)

        uniq = [0]

        def sbt(es, name, shape, dt=F32):
            uniq[0] += 1
            return es.enter_context(nc.sbuf_tensor("%s_%d" % (name, uniq[0]), shape, dt))

        ps = [top.enter_context(nc.psum_tensor("ps%d" % i, [128, 512], F32)) for i in range(8)]
        PK = [("ps", i) for i in range(8)]

        Pt = sbt(top, "Pt", [128, NP])
        Ct = sbt(top, "Ct", [128, 384])
        identb = sbt(top, "identb", [128, 128], BF16)
        onesb = sbt(top, "onesb", [128, 128], BF16)
        onesf = sbt(top, "onesf", [128, T])
        cmask = sbt(top, "cmask", [128, T])
        scb = sbt(top, "scb", [128, 32], BF16)
        lbt = sbt(top, "lbt", [128, 40])
        oml = sbt(top, "oml", [128, 40])
        noml = sbt(top, "noml", [128, 40])
        modT = sbt(top, "modT", [128, 96, 2])
        der = sbt(top, "der", [128, 5, 16])
        Sin = sbt(top, "Sin", [128, 2, 8, 128])
        slabs = [sbt(top, "slab%d" % i, [128, 16, 128], BF16) for i in range(NS)]
        slab_ctr = [0]

        def pcol(name, i=0, n=1):
            o, w = PL[name]
            return Pt[:, o + i:o + i + n]

        def end_phase(name):
            S.emit_phase(final=(name == upto))
            if name == upto:
                raise _Stop(nc)

        def DMA(eng, out_, in_, reads=(), writes=(), dstream=None):
            S.op(eng, lambda e: e.dma_start(out=out_, in_=in_), reads, writes, dma=True, dstream=dstream)

        def ACT(out_, in_, func, reads=(), writes=(), **kw):
            S.op("act", lambda e: e.activation(out=out_, in_=in_, func=func, **kw), reads, writes)

        def TT(out_, in0, in1, op, reads=(), writes=(), eng="dve"):
            S.op(eng, lambda e: e.tensor_tensor(out=out_, in0=in0, in1=in1, op=op), reads, writes)

        def TS(out_, in0, s1, s2, op0, op1=None, reads=(), writes=(), eng="dve"):
            if op1 is None:
                S.op(eng, lambda e: e.tensor_scalar(out=out_, in0=in0, scalar1=s1, scalar2=None, op0=op0), reads, writes)
            else:
                S.op(eng, lambda e: e.tensor_scalar(out=out_, in0=in0, scalar1=s1, scalar2=s2, op0=op0, op1=op1), reads, writes)

        def STT(out_, in0, scalar, in1, op0, op1, reads=(), writes=()):
            S.op("dve", lambda e: e.scalar_tensor_tensor(out=out_, in0=in0, scalar=scalar, in1=in1, op0=op0, op1=op1), reads, writes)

        def CP(out_, in_, reads=(), writes=(), eng="dve"):
            S.op(eng, lambda e: e.tensor_copy(out=out_, in_=in_), reads, writes)

        def MM(out_, lhsT, rhs, start, stop, reads=(), writes=()):
            S.op("pe", lambda e: e.matmul(out_, lhsT=lhsT, rhs=rhs, start=start, stop=stop), reads, writes)

        def TR(out_, in_, reads=(), writes=()):
            S.op("pe", lambda e: e.transpose(out_, in_, identb[:]), reads, writes)

        def MS(ap, val, writes=()):
            S.op("pool", lambda e: e.memset(ap, val), (), writes)

        def load_slab(W, row0, nk, col0):
            i = slab_ctr[0] % NS
            slab_ctr[0] += 1
            src = W[row0:row0 + nk * 128, col0:col0 + 128].rearrange("(k p) n -> p k n", p=128)
            DMA("pool", slabs[i][:, 0:nk, :], src, writes=[("slab", i)], dstream="slab%d" % i)
            return i

        def mm_slab(out_, si, nk, rhs_fn, rkeys, wkey):
            for k in range(nk):
                MM(out_, slabs[si][:, k, :], rhs_fn(k), k == 0, k == nk - 1,
                   reads=[("slab", si)] + list(rkeys), writes=[wkey])

        def bfview(p):
            return p[:].bitcast(BF16)

        DMA("sp", Pt[:], prm[:, :], writes=["Pt"])
        DMA("sp", Ct[:], cst[:, :], writes=["Ct"])
        CP(identb[:], Ct[:, 0:128], reads=["Ct"], writes=["identb"])
        MS(onesb[:], 1.0, writes=["onesb"])
        MS(onesf[:], 1.0, writes=["onesf"])
        MS(cmask[:], 1.0, writes=["cmask"])
        MS(cmask[:].rearrange("p (c t) -> p c t", t=64)[:, :, 0:1], 0.0, writes=["cmask"])
        MS(Sin[:], 0.0, writes=["Sin"])
        o_cp = PL["cp"][0]
        ACT(scb[:], Pt[:, o_cp:o_cp + 32], AF.Silu, reads=["Pt"], writes=["scb"])
        o_l0, o_l1 = PL["l0"][0], PL["l1"][0]
        TT(lbt[:], Pt[:, o_l0:o_l0 + 40], Pt[:, o_l1:o_l1 + 40], ALU.subtract, reads=["Pt"], writes=["lbt"])
        ACT(lbt[:], lbt[:], AF.Sigmoid, reads=["lbt"], writes=["lbt"])
        TS(oml[:], lbt[:], -1.0, 1.0, ALU.mult, ALU.add, reads=["lbt"], writes=["oml"])
        TS(noml[:], lbt[:], -1.0, None, ALU.add, reads=["lbt"], writes=["noml"])
        scb3 = scb[:].rearrange("p (c r) -> p c r", r=2)
        for m in range(32):
            si = load_slab(w_mod, 0, 16, m * 128)
            mm_slab(ps[0][:, 2 * m:2 * m + 2], si, 16, lambda k: scb3[:, k, :], ["scb"], PK[0])
        o_bm = PL["bmod"][0]
        TT(modT[:, 0:32, :], ps[0][:, 0:64].rearrange("p (m r) -> p m r", r=2),
           Pt[:, o_bm:o_bm + 32].unsqueeze(2).to_broadcast([128, 32, 2]), ALU.add,
           reads=[PK[0], "Pt"], writes=["modT"])
        mod_next = [32]

        def mod_tail_step(n=1):
            for _ in range(n):
                m = mod_next[0]
                if m >= 96:
                    return
                mod_next[0] += 1
                si = load_slab(w_mod, 0, 16, m * 128)
                mm_slab(ps[7][:, 2 * (m - 32):2 * (m - 32) + 2], si, 16, lambda k: scb3[:, k, :], (), PK[7])
        o_npm, o_npo, o_npl, o_npo2 = PL["npm"][0], PL["npo"][0], PL["npl"][0], PL["npo2"][0]
        STT(der[:, 0, :], modT[:, 16:32, 0], 1.0, Pt[:, o_npm:o_npm + 16], ALU.add, ALU.mult, reads=["modT", "Pt"], writes=["der"])
        STT(der[:, 1, :], modT[:, 16:32, 1], 1.0, Pt[:, o_npm:o_npm + 16], ALU.add, ALU.mult, reads=["modT", "Pt"], writes=["der"])
        end_phase("P0")

        def make_h(tmp, src, n, gs_col, sh_col, out_fn, okey, sb=7):
            xb, sqb, tx, rs = tmp["xb"], tmp["sqb"], tmp["tx"], tmp["rs"]
            DMA("sp", xb[:, :, 0:n], fm(src), writes=["xb"])
            for c in range(16):
                q = c % 2
                ACT(sqb[q][:, 0:n], xb[:, c, 0:n], AF.Square, reads=["xb"], writes=[("sqb", q)])
                MM(ps[sb][:, 0:n], onesb[:], sqb[q][:, 0:n], c == 0, c == 15, reads=[("sqb", q)], writes=[PK[sb]])
            ACT(rs[:, 0:n], ps[sb][:, 0:n], AF.Sqrt, reads=[PK[sb]], writes=["rs"], scale=1.0 / D, bias=EPS)
            S.op("dve", lambda e: e.reciprocal(out=rs[:, 0:n], in_=rs[:, 0:n]), ["rs"], ["rs"])
            for c in range(16):
                q = c % 2
                TT(tx[q][:, 0:n], xb[:, c, 0:n], rs[:, 0:n], ALU.mult, reads=["xb", "rs"], writes=[("tx", q)])
                ACT(out_fn(c), tx[q][:, 0:n], AF.Identity, reads=[("tx", q)], writes=[okey],
                    scale=gs_col(c), bias=sh_col(c))

        def h_tmps(es):
            return {"xb": sbt(es, "xb", [128, 16, 512]),
                    "sqb": [sbt(es, "sqb%d" % q, [128, 512], BF16) for q in range(2)],
                    "tx": [sbt(es, "tx%d" % q, [128, 512]) for q in range(2)],
                    "rs": sbt(es, "rs", [128, 512])}

        gs_m = lambda c: der[:, 0, c:c + 1]
        gsc_m = lambda c: der[:, 1, c:c + 1]
        gs_f = lambda c: der[:, 2, c:c + 1]
        sh_m = lambda c: modT[:, c, 0:1]
        shc_m = lambda c: modT[:, c, 1:2]
        sh_f = lambda c: modT[:, 48 + c, 0:1]

        with contextlib.ExitStack() as es:
            tmp = h_tmps(es)
            hR = sbt(es, "hR", [128, 16, T], BF16)
            vR = sbt(es, "vR", [128, 8, 1024], BF16)
            sg2 = [sbt(es, "sg%d" % p, [128, T]) for p in range(2)]
            lg2 = [sbt(es, "lg%d" % p, [128, T]) for p in range(2)]
            kk2 = [sbt(es, "kk%d" % p, [128, T]) for p in range(2)]
            Pc2 = [sbt(es, "Pc%d" % p, [128, T]) for p in range(2)]
            khat2 = [sbt(es, "khat%d" % p, [128, T], BF16) for p in range(2)]
            khT2 = [sbt(es, "khT%d" % p, [128, T], BF16) for p in range(2)]
            vTf = sbt(es, "vTf", [128, T], BF16)
            G = sbt(es, "G", [128, 8])
            Gb = sbt(es, "Gb", [128, 8])
            GA = sbt(es, "GA", [128, 2, 8])
            SA = sbt(es, "SA", [128, 2, 8, 128])
            Sctx = sbt(es, "Sctx", [128, 2, 8, 128])
            MS(GA[:], 0.0, writes=["GA"])
            MS(SA[:], 0.0, writes=["SA"])
            MS(G[:], 0.0, writes=["G"])

            def region(src, nt, gs_col, sh_col, wf, wf_row0, wf_col0, lb0, post):
                bs = min(512, nt)
                for blk in range(nt // bs):
                    make_h(tmp, src[:, blk * bs:(blk + 1) * bs], bs, gs_col, sh_col,
                           lambda c, blk=blk: hR[:, c, blk * bs:(blk + 1) * bs], "hR", sb=5)
                ntile = nt // 128
                def stageA(hd):
                    p = hd % 2
                    sg, lg, kk, Pc, khat = sg2[p], lg2[p], kk2[p], Pc2[p], khat2[p]
                    fbank = (0, 1) if p == 0 else (2, 3)
                    K = lambda n: (n, p)
                    mod_tail_step(2)
                    sv_ = load_slab(w_in, 0, 16, C_I + hd * 128)
                    for blk in range(nt // bs):
                        mm_slab(ps[4][:, 0:bs], sv_, 16, lambda k, blk=blk: hR[:, k, blk * bs:(blk + 1) * bs], ["hR"], PK[4])
                        ACT(vTf[:, blk * bs:(blk + 1) * bs], ps[4][:, 0:bs], AF.Copy, reads=[PK[4]], writes=["vTf"])
                    pTv = bfview(ps[4])
                    for tl in range(ntile):
                        TR(pTv[:, tl * 128:(tl + 1) * 128], vTf[:, tl * 128:(tl + 1) * 128], reads=["vTf"], writes=[PK[4]])
                    ACT(vR[:, 0:ntile, hd * 128:(hd + 1) * 128],
                        pTv[:, 0:ntile * 128].rearrange("p (a b) -> p a b", b=128), AF.Copy,
                        reads=[PK[4]], writes=[("vR", hd)])
                    si = load_slab(wf, wf_row0, 16, wf_col0 + hd * 128)
                    for blk in range(nt // bs):
                        pb = ps[fbank[blk]]
                        mm_slab(pb[:, 0:bs], si, 16, lambda k, blk=blk: hR[:, k, blk * bs:(blk + 1) * bs], ["hR"], PK[fbank[blk]])
                        ACT(sg[:, blk * bs:(blk + 1) * bs], pb[:, 0:bs], AF.Sigmoid, reads=[PK[fbank[blk]]], writes=[K("sg")])
                    li = lb0 + hd
                    ACT(lg[:, 0:nt], sg[:, 0:nt], AF.Ln, reads=[K("sg")], writes=[K("lg")], scale=oml[:, li:li + 1], bias=lbt[:, li:li + 1])
                    TS(kk[:, 0:nt], sg[:, 0:nt], noml[:, li:li + 1], oml[:, li:li + 1], ALU.mult, ALU.add, reads=[K("sg")], writes=[K("kk")])
                    S.op("dve", lambda e, hd=hd, Pc=Pc, lg=lg: e.tensor_tensor_scan(out=Pc[:, 0:nt], data0=onesf[:, 0:nt], data1=lg[:, 0:nt],
                                                                     initial=G[:, hd:hd + 1], op0=ALU.mult, op1=ALU.add),
                         [K("lg"), "G"], [K("Pc")])
                    CP(G[:, hd:hd + 1], Pc[:, nt - 1:nt], reads=[K("Pc")], writes=["G"])
                    TT(sg[:, 0:nt], Pc[:, 0:nt], lg[:, 0:nt], ALU.subtract, reads=[K("Pc"), K("lg")], writes=[K("sg")])
                    ACT(lg[:, 0:nt], sg[:, 0:nt], AF.Exp, reads=[K("sg")], writes=[K("lg")])
                    TT(khat[:, 0:nt], kk[:, 0:nt], lg[:, 0:nt], ALU.mult, reads=[K("kk"), K("lg")], writes=[K("khat")])

                def stageB(hd):
                    p = hd % 2
                    khat, khT = khat2[p], khT2[p]
                    K = lambda n: (n, p)
                    sbank = 6
                    pT = bfview(ps[5])
                    for tl in range(ntile):
                        TR(pT[:, tl * 128:(tl + 1) * 128], khat[:, tl * 128:(tl + 1) * 128], reads=[K("khat")], writes=[PK[5]])
                    ACT(khT[:, 0:nt], pT[:, 0:nt], AF.Copy, reads=[PK[5]], writes=[K("khT")])
                    for tl in range(ntile):
                        MM(ps[sbank][:, 0:128], khT[:, tl * 128:(tl + 1) * 128], vR[:, tl, hd * 128:(hd + 1) * 128],
                           tl == 0, tl == ntile - 1, reads=[K("khT"), ("vR", hd)], writes=[PK[sbank]])
                    post(hd, ps[sbank][:, 0:128], PK[sbank])

                stageA(0)
                for hd in range(8):
                    if hd + 1 < 8:
                        stageA(hd + 1)
                    stageB(hd)

            for d in range(2):
                MS(G[:], 0.0, writes=["G"])
                region(ctx2[:, d * 256:(d + 1) * 256], 256, gsc_m, shc_m, w_in, 0, C_FF if d == 0 else C_FB, 8 * d,
                       lambda hd, acc, pk, d=d: CP(Sctx[:, d, hd, :], acc, reads=[pk], writes=["Sctx"]))

            for s in range(3):
                if s == 0:
                    MS(G[:], 0.0, writes=["G"])
                else:
                    TS(G[:], G[:], pcol("slk", s), None, ALU.mult, reads=["G"], writes=["G"])
                CP(Gb[:], G[:], reads=["G"], writes=["Gb"])

                def post_slot(hd, acc, pk, s=s):
                    for d in range(2):
                        STT(SA[:, d, hd, :], acc, pcol("sla" if d == 0 else "slb", s), SA[:, d, hd, :], ALU.mult, ALU.add,
                            reads=[pk, "SA"], writes=["SA"])
                region(xs[s * D:(s + 1) * D, :], T, gs_m, sh_m, wfs, s * D, 0, 16 + 8 * s, post_slot)
                TT(Gb[:], G[:], Gb[:], ALU.subtract, reads=["G", "Gb"], writes=["Gb"])
                for d in range(2):
                    STT(GA[:, d, :], Gb[:], pcol("sla" if d == 0 else "slb", s), GA[:, d, :], ALU.mult, ALU.add,
                        reads=["Gb", "GA"], writes=["GA"])
            mod_tail_step(96)
            TT(modT[:, 32:96, :], ps[7][:, 0:128].rearrange("p (m r) -> p m r", r=2),
               Pt[:, o_bm + 32:o_bm + 96].unsqueeze(2).to_broadcast([128, 64, 2]), ALU.add,
               reads=[PK[7]], writes=["modT"])
            STT(der[:, 2, :], modT[:, 64:80, 0], 1.0, Pt[:, o_npl:o_npl + 16], ALU.add, ALU.mult, reads=["modT"], writes=["der"])
            TT(der[:, 3, :], modT[:, 32:48, 0], Pt[:, o_npo:o_npo + 16], ALU.mult, reads=["modT"], writes=["der"])
            TT(der[:, 4, :], modT[:, 80:96, 0], Pt[:, o_npo2:o_npo2 + 16], ALU.mult, reads=["modT"], writes=["der"])
            if debug:
                DMA("sp", dbg["d_mod"][:, :], modT[:].rearrange("p m r -> p (m r)"), reads=["modT"], writes=["dd"])
            ACT(GA[:], GA[:], AF.Exp, reads=["GA"], writes=["GA"])
            for d in range(2):
                for hd in range(8):
                    STT(Sin[:, d, hd, :], Sctx[:, d, hd, :], GA[:, d, hd:hd + 1], SA[:, d, hd, :], ALU.mult, ALU.add,
                        reads=["Sctx", "GA", "SA"], writes=["Sin"])
            if debug:
                DMA("sp", dbg["d_sin"][:, :], Sin[:].rearrange("p d h v -> p (d h v)"), reads=["Sin"], writes=["dd"])
            end_phase("R")

        with contextlib.ExitStack() as es:
            tmp = h_tmps(es)
            hT = sbt(es, "hT", [128, 16, T], BF16)
            for blk in range(2):
                make_h(tmp, xo[:, blk * 512:(blk + 1) * 512], 512, gs_m, sh_m,
                       lambda c, blk=blk: hT[:, c, blk * 512:(blk + 1) * 512], ("hT", blk))
                DMA("sp", fm(hT_d)[:, :, blk * 512:(blk + 1) * 512], hT[:, :, blk * 512:(blk + 1) * 512],
                    reads=[("hT", blk)], writes=["hT_d"])
            end_phase("H")

        with contextlib.ExitStack() as es:
            tmp = h_tmps(es)
            hB = sbt(es, "hB", [128, 16, 512], BF16)
            sgt = [sbt(es, "sgt%d" % q, [128, 512]) for q in range(2)]
            ub = [sbt(es, "ub%d" % q, [128, 512]) for q in range(2)]
            hoff = 0
            for hb, n in enumerate([512, 448, 512, 448]):
                side = 0 if hb < 2 else 1
                make_h(tmp, xh[:, hoff:hoff + n], n, gs_m, sh_m, lambda c: hB[:, c, 0:n], "hB")
                for cc in range(4, 8):
                    q = cc % 2
                    sv = load_slab(w_in, 0, 16, C_CV + cc * 128)
                    sgi = load_slab(w_in, 0, 16, C_CG + cc * 128)
                    mm_slab(ps[0 + 2 * q][:, 0:n], sv, 16, lambda k: hB[:, k, 0:n], ["hB"], PK[0 + 2 * q])
                    mm_slab(ps[1 + 2 * q][:, 0:n], sgi, 16, lambda k: hB[:, k, 0:n], ["hB"], PK[1 + 2 * q])
                    ACT(sgt[q][:, 0:n], ps[1 + 2 * q][:, 0:n], AF.Sigmoid, reads=[PK[1 + 2 * q]], writes=[("sgt", q)])
                    STT(ub[q][:, 0:n], ps[0 + 2 * q][:, 0:n], pcol("hm", side), sgt[q][:, 0:n], ALU.mult, ALU.mult,
                        reads=[("sgt", q), PK[0 + 2 * q]], writes=[("ub", q)])
                    DMA("sp", uvh_d[(cc - 4) * 128:(cc - 3) * 128, hoff:hoff + n], ub[q][:, 0:n],
                        reads=[("ub", q)], writes=["uvh_d"])
                hoff += n
            end_phase("C1")

        with contextlib.ExitStack() as es:
            hT = sbt(es, "hT", [128, 16, T], BF16)
            uv = sbt(es, "uv", [128, 46 * 64])
            uh = sbt(es, "uh", [128, 16, 94])
            conv = sbt(es, "conv", [128, 8, T])
            sgt = [sbt(es, "sgt%d" % q, [128, 512]) for q in range(2)]
            cb16 = [sbt(es, "cb16%d" % q, [128, 512], BF16) for q in range(2)]
            cs16 = [sbt(es, "cs16%d" % q, [128, 512], BF16) for q in range(2)]
            mean = sbt(es, "mean", [128, 512])
            rstd = sbt(es, "rstd", [128, 512])
            t1 = [sbt(es, "t1%d" % q, [128, 512]) for q in range(2)]
            cN = sbt(es, "cN", [128, 8, T], BF16)
            DMA("sp", hT[:], fm(hT_d), writes=["hT"])
            o_cw, o_cb = PL["cw"][0], PL["cb"][0]
            for cc in range(8):
                vert = cc >= 4
                if vert:
                    DMA("sp", uv[:, 0:960], uvh_d[(cc - 4) * 128:(cc - 3) * 128, 0:960], writes=["uv"])
                    DMA("sp", uv[:, 1984:2944], uvh_d[(cc - 4) * 128:(cc - 3) * 128, 960:1920], writes=["uv"])
                else:
                    MS(uh[:], 0.0, writes=["uh"])
                sv = load_slab(w_in, 0, 16, C_CV + cc * 128)
                sgi = load_slab(w_in, 0, 16, C_CG + cc * 128)
                for blk in range(2):
                    q = blk
                    mm_slab(ps[0 + 2 * q][:], sv, 16, lambda k, blk=blk: hT[:, k, blk * 512:(blk + 1) * 512], ["hT"], PK[0 + 2 * q])
                    mm_slab(ps[1 + 2 * q][:], sgi, 16, lambda k, blk=blk: hT[:, k, blk * 512:(blk + 1) * 512], ["hT"], PK[1 + 2 * q])
                    ACT(sgt[q][:], ps[1 + 2 * q][:], AF.Sigmoid, reads=[PK[1 + 2 * q]], writes=[("sgt", q)])
                    if vert:
                        TT(uv[:, 960 + blk * 512:960 + (blk + 1) * 512], ps[0 + 2 * q][:], sgt[q][:], ALU.mult,
                           reads=[PK[0 + 2 * q], ("sgt", q)], writes=["uv"])
                    else:
                        TT(uh[:, blk * 8:(blk + 1) * 8, 15:79], ps[0 + 2 * q][:].rearrange("p (r w) -> p r w", w=64),
                           sgt[q][:].rearrange("p (r w) -> p r w", w=64), ALU.mult,
                           reads=[PK[0 + 2 * q], ("sgt", q)], writes=["uh"])
                acc = conv[:, cc, :]
                for dd in range(31):
                    wcol = Pt[:, o_cw + cc * 31 + dd:o_cw + cc * 31 + dd + 1]
                    if vert:
                        src = uv[:, dd * 64:dd * 64 + 1024]
                        dst = acc
                        accin = acc
                    else:
                        src = uh[:, :, dd:dd + 64]
                        dst = acc.rearrange("p (r w) -> p r w", w=64)
                        accin = dst
                    if dd == 0:
                        TS(dst, src, wcol, Pt[:, o_cb + cc:o_cb + cc + 1], ALU.mult, ALU.add,
                           reads=["uv" if vert else "uh"], writes=[("conv", cc)])
                    else:
                        STT(dst, src, wcol, accin, ALU.mult, ALU.add,
                            reads=["uv" if vert else "uh", ("conv", cc)], writes=[("conv", cc)])
            o_lg, o_lb = PL["lng"][0], PL["lnb"][0]
            for blk in range(2):
                bsl = slice(blk * 512, (blk + 1) * 512)
                for cc in range(8):
                    q = cc % 2
                    ACT(cb16[q][:], conv[:, cc, bsl], AF.Copy, reads=[("conv", cc)], writes=[("cb16", q)])
                    ACT(cs16[q][:], conv[:, cc, bsl], AF.Square, reads=[("conv", cc)], writes=[("cs16", q)])
                    MM(ps[4][:], onesb[:], cb16[q][:], cc == 0, cc == 7, reads=[("cb16", q)], writes=[PK[4]])
                    MM(ps[5][:], onesb[:], cs16[q][:], cc == 0, cc == 7, reads=[("cs16", q)], writes=[PK[5]])
                ACT(mean[:], ps[4][:], AF.Copy, reads=[PK[4]], writes=["mean"], scale=1.0 / DC)
                TT(rstd[:], mean[:], mean[:], ALU.mult, reads=["mean"], writes=["rstd"])
                STT(rstd[:], ps[5][:], 1.0 / DC, rstd[:], ALU.mult, ALU.subtract, reads=[PK[5], "rstd"], writes=["rstd"])
                ACT(rstd[:], rstd[:], AF.Sqrt, reads=["rstd"], writes=["rstd"], bias=EPS)
                S.op("dve", lambda e: e.reciprocal(out=rstd[:], in_=rstd[:]), ["rstd"], ["rstd"])
                for cc in range(8):
                    q = cc % 2
                    TT(t1[q][:], conv[:, cc, bsl], mean[:], ALU.subtract, reads=[("conv", cc), "mean"], writes=[("t1", q)])
                    TT(t1[q][:], t1[q][:], rstd[:], ALU.mult, reads=[("t1", q), "rstd"], writes=[("t1", q)])
                    ACT(cN[:, cc, bsl], t1[q][:], AF.Silu, reads=[("t1", q)], writes=["cN"],
                        scale=Pt[:, o_lg + cc:o_lg + cc + 1], bias=Pt[:, o_lb + cc:o_lb + cc + 1])
            DMA("sp", fm(cN_d), cN[:], reads=["cN"], writes=["cN_d"])
            if debug:
                cNf = sbt(es, "cNf", [128, 8, T])
                CP(cNf[:], cN[:], reads=["cN"], writes=["cNf"])
                DMA("sp", fm(dbg["d_cN"]), cNf[:], reads=["cNf"], writes=["dd"])
            end_phase("C2")

        with contextlib.ExitStack() as es:
            hT = sbt(es, "hT", [128, 16, T], BF16)
            vtok = sbt(es, "vtok", [128, 8, 1024], BF16)
            sg = sbt(es, "sg", [128, T])
            lgd = [sbt(es, "lg%d" % d, [128, T]) for d in range(2)]
            kkd = [sbt(es, "kk%d" % d, [128, T]) for d in range(2)]
            qs = sbt(es, "qs", [128, T])
            gsil = sbt(es, "gsil", [128, T])
            aa = sbt(es, "aa", [128, T])
            ab = sbt(es, "ab", [128, T])
            tA = sbt(es, "tA", [128, T])
            tB = sbt(es, "tB", [128, T])
            qt = [sbt(es, "qt%d" % d, [128, T], BF16) for d in range(2)]
            kt = [sbt(es, "kt%d" % d, [128, T], BF16) for d in range(2)]
            kh = [sbt(es, "kh%d" % d, [128, T], BF16) for d in range(2)]
            dec = [sbt(es, "dec%d" % d, [128, 16]) for d in range(2)]
            Sb = [sbt(es, "Sb%d" % d, [128, 16, 128], BF16) for d in range(2)]
            scm = [sbt(es, "scm%d" % d, [128, T], BF16) for d in range(2)]
            osq = sbt(es, "osq", [128, 512], BF16)
            rs = sbt(es, "rsH", [128, 512])
            t1 = sbt(es, "t1H", [128, 512])
            oN = sbt(es, "oN", [128, T], BF16)
            pmask = [sbt(es, "pmask%d" % q, [128, T], BF16) for q in range(2)]
            khp = [sbt(es, "khp%d" % q, [128, T], BF16) for q in range(2)]
            khTp = [[sbt(es, "khTp%d%d" % (d, q), [128, T], BF16) for q in range(2)] for d in range(2)]
            Srun2 = [[sbt(es, "SrunP%d%d" % (d, q), [128, 128]) for q in range(2)] for d in range(2)]
            for q in range(2):
                pv = pmask[q][:].rearrange("p (a h t) -> p a h t", h=2, t=64)
                MS(pv[:, :, q, :], 1.0, writes=[("pmask", q)])
                MS(pv[:, :, 1 - q, :], 0.0, writes=[("pmask", q)])
            if debug:
                oNf = sbt(es, "oNf", [128, T])
            DMA("sp", hT[:], fm(hT_d), writes=["hT"])
            for cs in range(8):
                si = load_slab(w_in, 0, 16, C_I + cs * 128)
                vb = scm[cs % 2]
                for blk in range(2):
                    b = 2 * (cs % 2) + blk
                    mm_slab(ps[b][:], si, 16, lambda k, blk=blk: hT[:, k, blk * 512:(blk + 1) * 512], ["hT"], PK[b])
                    ACT(vb[:, blk * 512:(blk + 1) * 512], ps[b][:], AF.Copy, reads=[PK[b]], writes=[("scm", cs % 2)])
                pTv = bfview(ps[4 + cs % 2])
                for tl in range(8):
                    TR(pTv[:, tl * 128:(tl + 1) * 128], vb[:, tl * 128:(tl + 1) * 128], reads=[("scm", cs % 2)], writes=[PK[4 + cs % 2]])
                ACT(vtok[:, :, cs * 128:(cs + 1) * 128], pTv[:, :].rearrange("p (a b) -> p a b", b=128), AF.Copy,
                    reads=[PK[4 + cs % 2]], writes=["vtok"])
            o_hng = PL["hng"][0]
            a3 = lambda t: t[:].rearrange("p (c t) -> p c t", t=64)
            HGS = int(os.environ.get('HGS', '9'))
            for hd in range(int(os.environ.get('HGNH', '8'))):
                def proj(colbase, pa):
                    si = load_slab(w_in, 0, 16, colbase + hd * 128)
                    for blk in range(2):
                        mm_slab(ps[pa + blk][:], si, 16, lambda k, blk=blk: hT[:, k, blk * 512:(blk + 1) * 512], ["hT"], PK[pa + blk])
                for d, cbase in ((0, C_FF), (1, C_FB)):
                    proj(cbase, 2 * d)
                    for blk in range(2):
                        ACT(sg[:, blk * 512:(blk + 1) * 512], ps[2 * d + blk][:], AF.Sigmoid, reads=[PK[2 * d + blk]], writes=["sg"])
                    li = 8 * d + hd
                    ACT(lgd[d][:], sg[:], AF.Ln, reads=["sg"], writes=[("lg", d)], scale=oml[:, li:li + 1], bias=lbt[:, li:li + 1])
                    TS(kkd[d][:], sg[:], noml[:, li:li + 1], oml[:, li:li + 1], ALU.mult, ALU.add, reads=["sg"], writes=[("kk", d)])
                proj(C_Q, 0)
                for blk in range(2):
                    ACT(qs[:, blk * 512:(blk + 1) * 512], ps[blk][:], AF.Silu, reads=[PK[blk]], writes=["qs"])
                proj(C_G, 2)
                for blk in range(2):
                    ACT(gsil[:, blk * 512:(blk + 1) * 512], ps[2 + blk][:], AF.Silu, reads=[PK[2 + blk]], writes=["gsil"])
                for d in range(2 if HGS >= 2 else 0):
                    S.op("dve", lambda e, d=d: e.tensor_tensor_scan(out=aa[:], data0=cmask[:], data1=lgd[d][:], initial=0.0,
                                                                   op0=ALU.mult, op1=ALU.add), [("lg", d)], ["aa"])
                    if d == 1:
                        TT(tA[:], lgd[1][:], aa[:], ALU.subtract, reads=[("lg", 1), "aa"], writes=["tA"])
                        TT(a3(ab), a3(tA), a3(aa)[:, :, 63:64].to_broadcast([128, 16, 64]), ALU.add, reads=["tA", "aa"], writes=["ab"])
                    edge = 63 if d == 0 else 0
                    av, ak = (aa, "aa") if d == 0 else (ab, "ab")
                    ACT(tA[:], av[:], AF.Exp, reads=[ak], writes=["tA"])
                    STT(qt[d][:], tA[:], 128.0 ** -0.5, qs[:], ALU.mult, ALU.mult, reads=["tA", "qs"], writes=[("qt", d)])
                    CP(dec[d][:], a3(tA)[:, :, edge], reads=["tA"], writes=[("dec", d)])
                    ACT(tB[:], av[:], AF.Exp, reads=[ak], writes=["tB"], scale=-1.0)
                    TT(kt[d][:], kkd[d][:], tB[:], ALU.mult, reads=[("kk", d), "tB"], writes=[("kt", d)])
                    TT(a3(tA), a3(av)[:, :, edge:edge + 1].to_broadcast([128, 16, 64]), a3(av), ALU.subtract, reads=[ak], writes=["tA"])
                    ACT(tB[:], tA[:], AF.Exp, reads=["tA"], writes=["tB"])
                    TT(kh[d][:], kkd[d][:], tB[:], ALU.mult, reads=[("kk", d), "tB"], writes=[("kh", d)])
                    if HGS < 3:
                        continue
                    pT = bfview(ps[4])
                    for q in range(2):
                        TT(khp[q][:], kh[d][:], pmask[q][:], ALU.mult, reads=[("kh", d), ("pmask", q)], writes=[("khp", q)])
                        for tl in range(8):
                            TR(pT[:, tl * 128:(tl + 1) * 128], khp[q][:, tl * 128:(tl + 1) * 128], reads=[("khp", q)], writes=[PK[4]])
                        ACT(khTp[d][q][:], pT[:, :], AF.Copy, reads=[PK[4]], writes=[("khTp", d, q)])
                    if HGS < 5:
                        continue
                    mk = Ct[:, 128:256] if d == 0 else Ct[:, 256:384]
                    for g in range(2):
                        pb = ps[6 + d]
                        for tl in range(4):
                            tile = g * 4 + tl
                            MM(pb[:, tl * 128:(tl + 1) * 128], kt[d][:, tile * 128:(tile + 1) * 128],
                               qt[d][:, tile * 128:(tile + 1) * 128], True, True, reads=[("kt", d), ("qt", d)], writes=[PK[6 + d]])
                        TT(scm[d][:, g * 512:(g + 1) * 512].rearrange("p (a b) -> p a b", b=128),
                           pb[:].rearrange("p (a b) -> p a b", b=128),
                           mk.unsqueeze(1).to_broadcast([128, 4, 128]), ALU.mult, reads=[PK[6 + d]], writes=[("scm", d)])
                for d in range(2):
                    c0 = 0 if d == 0 else 15
                    CP(Srun2[d][0][:], Sin[:, d, hd, :], writes=[("Srun", d, 0)])
                    ACT(Sb[d][:, c0, :], Sin[:, d, hd, :], AF.Copy, writes=[("Sb", d)])
                orders = [list(range(0, 15)), list(range(15, 0, -1))]
                for n in range(15):
                    for d in range(2):
                        c = orders[d][n]
                        tl, hf = c // 2, c % 2
                        b = 2 * d + (n % 2)
                        slot = ps[b][:, 0:128]
                        MM(slot, khTp[d][hf][:, tl * 128:(tl + 1) * 128], vtok[:, tl, hd * 128:(hd + 1) * 128], True, True,
                           reads=[("khTp", d, hf), "vtok"], writes=[PK[b]])
                        src, dst = Srun2[d][n % 2], Srun2[d][(n + 1) % 2]
                        TS(dst[:], src[:], dec[d][:, c:c + 1], None, ALU.mult,
                           reads=[("Srun", d, n % 2), ("dec", d)], writes=[("Srun", d, (n + 1) % 2)])
                        TT(dst[:], slot, dst[:], ALU.add,
                           reads=[("Srun", d, (n + 1) % 2), PK[b]], writes=[("Srun", d, (n + 1) % 2)])
                        cn = c + 1 if d == 0 else c - 1
                        ACT(Sb[d][:, cn, :], dst[:], AF.Copy, reads=[("Srun", d, (n + 1) % 2)], writes=[("Sb", d)])
                for tile in range(8 if HGS >= 6 else 0):
                    pb = ps[tile // 4]
                    po = pb[:, (tile % 4) * 128:(tile % 4 + 1) * 128]
                    vt = vtok[:, tile, hd * 128:(hd + 1) * 128]
                    MM(po, vt, scm[0][:, tile * 128:(tile + 1) * 128], True, False, reads=["vtok", ("scm", 0)], writes=[PK[tile // 4]])
                    MM(po, vt, scm[1][:, tile * 128:(tile + 1) * 128], False, False, reads=["vtok", ("scm", 1)], writes=[PK[tile // 4]])
                    for hf in range(2):
                        c = tile * 2 + hf
                        pc = pb[:, (tile % 4) * 128 + hf * 64:(tile % 4) * 128 + hf * 64 + 64]
                        MM(pc, Sb[0][:, c, :], qt[0][:, c * 64:(c + 1) * 64], False, False, reads=[("Sb", 0), ("qt", 0)], writes=[PK[tile // 4]])
                        MM(pc, Sb[1][:, c, :], qt[1][:, c * 64:(c + 1) * 64], False, hf == 1, reads=[("Sb", 1), ("qt", 1)], writes=[PK[tile // 4]])
                for blk in range(2):
                    ACT(osq[:], ps[blk][:], AF.Square, reads=[PK[blk]], writes=["osq"])
                    MM(ps[6][:], onesb[:], osq[:], True, True, reads=["osq"], writes=[PK[6]])
                    ACT(rs[:], ps[6][:], AF.Sqrt, reads=[PK[6]], writes=["rsH"], scale=1.0 / 128, bias=EPS)
                    S.op("dve", lambda e: e.reciprocal(out=rs[:], in_=rs[:]), ["rsH"], ["rsH"])
                    TT(t1[:], ps[blk][:], rs[:], ALU.mult, reads=[PK[blk], "rsH"], writes=["t1H"])
                    STT(oN[:, blk * 512:(blk + 1) * 512], t1[:], Pt[:, o_hng + hd:o_hng + hd + 1], gsil[:, blk * 512:(blk + 1) * 512],
                        ALU.mult, ALU.mult, reads=["t1H", "gsil"], writes=["oN"])
                DMA("sp", oN_d[hd * 128:(hd + 1) * 128, :], oN[:], reads=["oN"], writes=["oN_d"])
                if debug:
                    CP(oNf[:], oN[:], reads=["oN"], writes=["oNf"])
                    DMA("sp", dbg["d_oN"][hd * 128:(hd + 1) * 128, :], oNf[:], reads=["oNf"], writes=["dd"])
            end_phase("HG")

        with contextlib.ExitStack() as es:
            hT = sbt(es, "hT", [128, 16, T], BF16)
            cN = sbt(es, "cN", [128, 8, T], BF16)
            oN = sbt(es, "oNa", [128, 8, T], BF16)
            sgc = [sbt(es, "sgc%d" % q, [128, 512]) for q in range(2)]
            sgh = [sbt(es, "sgh%d" % q, [128, 512]) for q in range(2)]
            m1 = [sbt(es, "m1%d" % q, [128, 512]) for q in range(2)]
            mg = sbt(es, "mg", [128, 16, T], BF16)
            DMA("sp", hT[:], fm(hT_d), writes=["hT"])
            DMA("sp", cN[:], fm(cN_d), writes=["cN"])
            DMA("sp", oN[:], fm(oN_d), writes=["oNa"])
            for m in range(16):
                s_pw = load_slab(w_pw, 0, 8, m * 128)
                s_ho = load_slab(w_ho, 0, 8, m * 128)
                s_gc = load_slab(w_in, 0, 16, C_GC + m * 128)
                s_gh = load_slab(w_in, 0, 16, C_GH + m * 128)
                for blk in range(2):
                    q = blk
                    bsl = slice(blk * 512, (blk + 1) * 512)
                    b0 = 4 * q
                    mm_slab(ps[b0][:], s_pw, 8, lambda k: cN[:, k, bsl], ["cN"], PK[b0])
                    mm_slab(ps[b0 + 1][:], s_ho, 8, lambda k: oN[:, k, bsl], ["oNa"], PK[b0 + 1])
                    mm_slab(ps[b0 + 2][:], s_gc, 16, lambda k: hT[:, k, bsl], ["hT"], PK[b0 + 2])
                    mm_slab(ps[b0 + 3][:], s_gh, 16, lambda k: hT[:, k, bsl], ["hT"], PK[b0 + 3])
                    ACT(sgc[q][:], ps[b0 + 2][:], AF.Sigmoid, reads=[PK[b0 + 2]], writes=[("sgc", q)])
                    ACT(sgh[q][:], ps[b0 + 3][:], AF.Sigmoid, reads=[PK[b0 + 3]], writes=[("sgh", q)])
                    TT(m1[q][:], ps[b0][:], sgc[q][:], ALU.mult, reads=[PK[b0], ("sgc", q)], writes=[("m1", q)])
                    TT(sgh[q][:], ps[b0 + 1][:], sgh[q][:], ALU.mult, reads=[PK[b0 + 1], ("sgh", q)], writes=[("sgh", q)])
                    TT(mg[:, m, bsl], m1[q][:], sgh[q][:], ALU.add, reads=[("m1", q), ("sgh", q)], writes=["mg"])
            DMA("sp", fm(mg_d), mg[:], reads=["mg"], writes=["mg_d"])
            end_phase("M1")

        def post_norm_residual(yT, xb, gg_idx, blk, dst_dram, sqb, tx, rs, dkey):
            bsl = slice(blk * 512, (blk + 1) * 512)
            ACT(rs[:], ps[7][:], AF.Sqrt, reads=[PK[7]], writes=["rs"], scale=1.0 / D, bias=EPS)
            S.op("dve", lambda e: e.reciprocal(out=rs[:], in_=rs[:]), ["rs"], ["rs"])
            for c in range(16):
                q = c % 2
                TT(tx[q][:], yT[:, c, :], rs[:], ALU.mult, reads=["yT", "rs"], writes=[("tx", q)])
                STT(xb[:, c, :], tx[q][:], der[:, gg_idx, c:c + 1], xb[:, c, :], ALU.mult, ALU.add,
                    reads=[("tx", q), "xb"], writes=["xb"])
            DMA("sp", fm(dst_dram)[:, :, bsl], xb[:], reads=["xb"], writes=[dkey])

        with contextlib.ExitStack() as es:
            mg = sbt(es, "mg", [128, 16, T], BF16)
            yT = sbt(es, "yT", [128, 16, 512])
            xb = sbt(es, "xb", [128, 16, 512])
            sqb = [sbt(es, "sqb%d" % q, [128, 512], BF16) for q in range(2)]
            tx = [sbt(es, "tx%d" % q, [128, 512]) for q in range(2)]
            rs = sbt(es, "rs", [128, 512])
            DMA("sp", mg[:], fm(mg_d), writes=["mg"])
            for blk in range(2):
                bsl = slice(blk * 512, (blk + 1) * 512)
                DMA("sp", xb[:], fm(xo)[:, :, bsl], writes=["xb"])
                for m in range(16):
                    si = load_slab(w_o, 0, 16, m * 128)
                    q = m % 2
                    mm_slab(ps[q][:], si, 16, lambda k: mg[:, k, bsl], ["mg"], PK[q])
                    ACT(yT[:, m, :], ps[q][:], AF.Copy, reads=[PK[q]], writes=["yT"])
                    ACT(sqb[q][:], ps[q][:], AF.Square, reads=[PK[q]], writes=[("sqb", q)])
                    MM(ps[7][:], onesb[:], sqb[q][:], m == 0, m == 15, reads=[("sqb", q)], writes=[PK[7]])
                post_norm_residual(yT, xb, 3, blk, x1_d, sqb, tx, rs, "x1_d")
                if debug:
                    DMA("sp", fm(dbg["d_x1"])[:, :, bsl], xb[:], reads=["xb"], writes=["dd"])
            end_phase("M2")

        with contextlib.ExitStack() as es:
            yT = sbt(es, "yT", [128, 16, T])
            h2 = sbt(es, "h2", [128, 16, T], BF16)
            zq = sbt(es, "zq", [128, 16, T], BF16)
            xq = sbt(es, "xq", [128, 16, 256])
            sqb = [sbt(es, "sqb%d" % q, [128, 512], BF16) for q in range(2)]
            tx = [sbt(es, "tx%d" % q, [128, 512]) for q in range(2)]
            rs = sbt(es, "rs", [128, 512])
            for blk in range(2):
                bsl = slice(blk * 512, (blk + 1) * 512)
                xs_ = yT[:, :, bsl]
                DMA("sp", xs_, fm(x1_d)[:, :, bsl], writes=[("yT", blk)])
                for c in range(16):
                    q = c % 2
                    ACT(sqb[q][:], yT[:, c, bsl], AF.Square, reads=[("yT", blk)], writes=[("sqb", q)])
                    MM(ps[7][:], onesb[:], sqb[q][:], c == 0, c == 15, reads=[("sqb", q)], writes=[PK[7]])
                ACT(rs[:], ps[7][:], AF.Sqrt, reads=[PK[7]], writes=["rs"], scale=1.0 / D, bias=EPS)
                S.op("dve", lambda e: e.reciprocal(out=rs[:], in_=rs[:]), ["rs"], ["rs"])
                for c in range(16):
                    q = c % 2
                    TT(tx[q][:], yT[:, c, bsl], rs[:], ALU.mult, reads=[("yT", blk), "rs"], writes=[("tx", q)])
                    ACT(h2[:, c, bsl], tx[q][:], AF.Identity, reads=[("tx", q)], writes=[("h2", blk)], scale=gs_f(c), bias=sh_f(c))
            for qp in range(4):
                for m in range(16):
                    si = load_slab(w1, 0, 16, (qp * 16 + m) * 128)
                    for blk in range(2):
                        bsl = slice(blk * 512, (blk + 1) * 512)
                        b = 2 * (m % 2) + blk
                        mm_slab(ps[b][:], si, 16, lambda k: h2[:, k, bsl], [("h2", blk)], PK[b])
                        ACT(tx[blk][:], ps[b][:], AF.Relu, reads=[PK[b]], writes=[("tx", blk)])
                        TT(zq[:, m, bsl], tx[blk][:], tx[blk][:], ALU.mult, reads=[("tx", blk)], writes=[("zq", blk)])
                for mo in range(16):
                    si = load_slab(w2, qp * 2048, 16, mo * 128)
                    for blk in range(2):
                        bsl = slice(blk * 512, (blk + 1) * 512)
                        b = 4 + 2 * (mo % 2) + blk
                        mm_slab(ps[b][:], si, 16, lambda k: zq[:, k, bsl], [("zq", blk)], PK[b])
                        if qp == 0:
                            ACT(yT[:, mo, bsl], ps[b][:], AF.Copy, reads=[PK[b], ("h2", blk)], writes=[("yT", blk)])
                        else:
                            TT(yT[:, mo, bsl], ps[b][:], yT[:, mo, bsl], ALU.add, reads=[PK[b], ("yT", blk)], writes=[("yT", blk)])
            for sbk in range(4):
                ssl = slice(sbk * 256, (sbk + 1) * 256)
                blk = sbk // 2
                DMA("sp", xq[:], fm(x1_d)[:, :, ssl], writes=["xq"])
                for c in range(16):
                    q = c % 2
                    ACT(sqb[q][:, 0:256], yT[:, c, ssl], AF.Square, reads=[("yT", blk)], writes=[("sqb", q)])
                    MM(ps[7][:, 0:256], onesb[:], sqb[q][:, 0:256], c == 0, c == 15, reads=[("sqb", q)], writes=[PK[7]])
                ACT(rs[:, 0:256], ps[7][:, 0:256], AF.Sqrt, reads=[PK[7]], writes=["rs"], scale=1.0 / D, bias=EPS)
                S.op("dve", lambda e: e.reciprocal(out=rs[:, 0:256], in_=rs[:, 0:256]), ["rs"], ["rs"])
                for c in range(16):
                    q = c % 2
                    TT(tx[q][:, 0:256], yT[:, c, ssl], rs[:, 0:256], ALU.mult, reads=[("yT", blk), "rs"], writes=[("tx", q)])
                    STT(xq[:, c, :], tx[q][:, 0:256], der[:, 4, c:c + 1], xq[:, c, :], ALU.mult, ALU.add,
                        reads=[("tx", q), "xq"], writes=["xq"])
                DMA("sp", fm(out)[:, :, ssl], xq[:], reads=["xq"], writes=["out"])
            S.emit_phase(final=True)
    return nc


def _cols(v):
    v = np.asarray(v, np.float32)
    return np.ascontiguousarray(v.reshape(-1, 128).T)


def _consts():
    c = np.zeros((128, 384), np.float32)
    c[:, 0:128] = np.eye(128, dtype=np.float32)
    s = np.arange(128)[:, None]
    t = np.arange(128)[None, :]
    same = (s // 64) == (t // 64)
    c[:, 128:256] = (same & (s <= t)).astype(np.float32)
    c[:, 256:384] = (same & (s >= t)).astype(np.float32)
    return c


def _prep(inp):
    f = lambda k: np.asarray(inp[k], np.float32)
    x, c, ctx, c_ctx = f("x"), f("c"), f("ctx"), f("c_ctx")
    w_in = np.ascontiguousarray(f("w_in")[0])
    shared = {
        "w_mod": np.ascontiguousarray(f("w_mod")[0]), "w_in": w_in,
        "w_pw": np.ascontiguousarray(f("conv_pw_w")[0]), "w_ho": np.ascontiguousarray(f("hgrn_out_w")[0]),
        "w_o": np.ascontiguousarray(f("w_out")[0]), "w1": np.ascontiguousarray(f("mlp_w1")[0]),
        "w2": np.ascontiguousarray(f("mlp_w2")[0]), "cst": _consts(),
    }
    wf = [np.ascontiguousarray(w_in[:, C_FF:C_FF + DH]), np.ascontiguousarray(w_in[:, C_FB:C_FB + DH])]
    lbl = f("hgrn_lb_logits")
    cw = f("conv_dw_w")[0]
    in_maps = []
    for core in range(8):
        b, j = divmod(core, 4)
        P = np.zeros((128, NP), np.float32)

        def put(name, arr, i=0):
            o, w = PL[name]
            arr = np.asarray(arr, np.float32)
            P[:, o + i:o + i + arr.shape[1]] = arr
        put("npm", _cols(f("norm_pre_mix")[0]))
        put("npo", _cols(f("norm_post_mix")[0]))
        put("npl", _cols(f("norm_pre_mlp")[0]))
        put("npo2", _cols(f("norm_post_mlp")[0]))
        put("bmod", _cols(f("b_mod")[0]))
        put("cw", np.ascontiguousarray(cw.T.reshape(8, 128, 31).transpose(1, 0, 2).reshape(128, 8 * 31)))
        put("cb", _cols(f("conv_dw_b")[0]))
        put("lng", _cols(f("conv_ln_g")[0]))
        put("lnb", _cols(f("conv_ln_b")[0]))
        put("hng", _cols(f("hgrn_norm_g")[0]))
        slots = [("f", s) for s in range(j - 1, -1, -1)] + [("b", s) for s in range(j + 1, 4)]
        dirs = [0 if d == "f" else 1 for d, _ in slots]
        l0 = [lbl[0, 0], lbl[1, 0]] + [lbl[d, 0] for d in dirs]
        l1 = [lbl[0, 1], lbl[1, 1]] + [lbl[d, 1] for d in dirs]
        put("l0", np.concatenate([_cols(v) for v in l0], 1))
        put("l1", np.concatenate([_cols(v) for v in l1], 1))
        keep = [0.0] + [1.0 if dirs[s] == dirs[s - 1] else 0.0 for s in (1, 2)]
        put("slk", np.tile(np.array(keep, np.float32)[None], (128, 1)))
        put("sla", np.tile(np.array([1.0 - d for d in dirs], np.float32)[None], (128, 1)))
        put("slb", np.tile(np.array([float(d) for d in dirs], np.float32)[None], (128, 1)))
        put("hm", np.tile(np.array([1.0 if j > 0 else 0.0, 1.0 if j < 3 else 0.0], np.float32)[None], (128, 1)))
        cp = np.stack([_cols(c[b]), _cols(c_ctx)], 2).reshape(128, 32)
        put("cp", cp)
        xb_ = x[b]
        xo = np.ascontiguousarray(xb_[1024 * j:1024 * (j + 1)].T)
        xh = np.zeros((D, NHALO), np.float32)
        if j > 0:
            xh[:, 0:960] = xb_[1024 * j - 960:1024 * j].T
        if j < 3:
            xh[:, 960:1920] = xb_[1024 * (j + 1):1024 * (j + 1) + 960].T
        xs = np.zeros((3 * D, T), np.float32)
        wfs = np.zeros((3 * D, DH), np.float32)
        for s, (d, seg) in enumerate(slots):
            blk = xb_[1024 * seg:1024 * (seg + 1)]
            if d == "f":
                blk = blk[::-1]
            xs[s * D:(s + 1) * D] = blk.T
            wfs[s * D:(s + 1) * D] = wf[0 if d == "f" else 1]
        ctx2 = np.concatenate([ctx[b][::-1].T, ctx[b].T], 1)
        m = dict(shared)
        m.update({"xo": xo, "xh": xh, "xs": xs, "wfs": wfs, "ctx2": np.ascontiguousarray(ctx2), "prm": P})
        in_maps.append(m)
    return in_maps


_NC_CACHE = {}


def kernel(**inputs):
    in_maps = _prep(inputs)
    if "nc" not in _NC_CACHE:
        _NC_CACHE["nc"] = build()
    res = run_bass_kernel_spmd(_NC_CACHE["nc"], in_maps, core_ids=list(range(8)))
    out = np.zeros((2, 4096, D), np.float32)
    for core in range(8):
        b, j = divmod(core, 4)
        out[b, 1024 * j:1024 * (j + 1)] = np.asarray(res.results[core]["out"], np.float32).T
    return out
```

```python
import contextlib
import os
import numpy as np
import concourse.bass as bass
import concourse.mybir as mybir
from concourse.bass_utils import run_bass_kernel_spmd

F32 = mybir.dt.float32
BF16 = mybir.dt.bfloat16
AF = mybir.ActivationFunctionType
ALU = mybir.AluOpType

D = 2048
T = 1024
DH = 1024
DC = 1024
DFF = 8192
EPS = 1e-6
NHALO = 1920
C_I, C_FF, C_FB, C_Q, C_G, C_CV, C_CG, C_GC, C_GH = 0, 1024, 2048, 3072, 4096, 5120, 6144, 7168, 9216

PL = {}
_off = 0
for _n, _w in [("npm", 16), ("npo", 16), ("npl", 16), ("npo2", 16), ("bmod", 96), ("cw", 8 * 31),
               ("cb", 8), ("lng", 8), ("lnb", 8), ("hng", 8), ("l0", 40), ("l1", 40),
               ("slk", 3), ("sla", 3), ("slb", 3), ("hm", 2), ("cp", 32)]:
    PL[_n] = (_off, _w)
    _off += _w
NP = _off


class Sched:
    COMPUTE = ("pe", "act", "dve", "pool")
    NSP = 6

    def __init__(self, nc, sems, dma_streams):
        self.nc = nc
        self.sems = sems
        self.issuers = ("pe", "act", "dve", "pool", "sp")
        self.dma_streams = tuple(dma_streams)
        self.streams = self.COMPUTE + self.dma_streams
        self.ops = {e: [] for e in self.issuers}
        self.cnt = {s: 0 for s in self.streams}
        self.seen = {e: {s: 0 for s in self.streams} for e in self.issuers}
        self.state = {}
        self.phase_base = {s: 0 for s in self.streams}
        self.sp_rr = 0

    def _st(self, b):
        s = self.state.get(b)
        if s is None:
            s = {"w": None, "r": {}}
            self.state[b] = s
        return s

    def op(self, eng, fn, reads=(), writes=(), dma=False, dstream=None):
        deps = {}
        if dma:
            if dstream is None:
                dstream = "sp%d" % (self.sp_rr % self.NSP)
                self.sp_rr += 1
            stream = dstream
            inc = 16
            if self.cnt[stream] > 0:
                deps[stream] = self.cnt[stream]
        else:
            stream = eng
            inc = 1
        for b in reads:
            w = self._st(b)["w"]
            if w is not None:
                deps[w[0]] = max(deps.get(w[0], 0), w[1])
        for b in writes:
            st = self._st(b)
            if st["w"] is not None:
                w = st["w"]
                deps[w[0]] = max(deps.get(w[0], 0), w[1])
            for s2, c in st["r"].items():
                deps[s2] = max(deps.get(s2, 0), c)
        waits = []
        for s2, c in deps.items():
            if s2 == "pe" and eng == "pe":
                continue
            if c > self.seen[eng][s2]:
                waits.append((s2, c))
                self.seen[eng][s2] = c
        self.cnt[stream] += inc
        my = self.cnt[stream]
        self.ops[eng].append((waits, fn, stream))
        for b in reads:
            st = self._st(b)
            st["r"][stream] = max(st["r"].get(stream, 0), my)
        for b in writes:
            st = self._st(b)
            st["w"] = (stream, my)
            st["r"] = {}
        return my

    def emit_phase(self, final=False):
        nc = self.nc
        sems = self.sems
        base = dict(self.phase_base)
        finals = [(s, c) for s, c in self.cnt.items() if c > 0]
        ops = self.ops
        dmas = set(self.dma_streams)

        def make(engname):
            def body(eng):
                for s2, c in base.items():
                    if c > 0:
                        eng.wait_ge(sems[s2], c)
                for waits, fn, stream in ops[engname]:
                    for s2, c in waits:
                        eng.wait_ge(sems[s2], c)
                    inst = fn(eng)
                    inst.then_inc(sems[stream], 16 if stream in dmas else 1)
                if final and engname == "sp":
                    for s2, c in finals:
                        eng.wait_ge(sems[s2], c)
            return body

        with nc.Block() as block:
            block.tensor(make("pe"))
            block.scalar(make("act"))
            block.vector(make("dve"))
            block.gpsimd(make("pool"))
            block.sync(make("sp"))
        self.ops = {e: [] for e in self.issuers}
        self.phase_base = dict(self.cnt)
        for e in self.issuers:
            for s2 in self.streams:
                self.seen[e][s2] = self.cnt[s2]
        self.state = {}


def fm(ap):
    return ap.rearrange("(c p) t -> p c t", p=128)


class _Stop(Exception):
    pass


def build(debug=False, upto=None):
    try:
        return _build(debug, upto)
    except _Stop as e:
        return e.args[0]


def _build(debug, upto):
    nc = bass.Bass("TRN2", target_bir_lowering=False)

    def din(name, shape, dt=F32):
        return nc.dram_tensor(name, shape, dt, kind="ExternalInput").ap()

    xo = din("xo", [D, T])
    xh = din("xh", [D, NHALO])
    xs = din("xs", [3 * D, T])
    wfs = din("wfs", [3 * D, DH])
    ctx2 = din("ctx2", [D, 512])
    prm = din("prm", [128, NP])
    cst = din("cst", [128, 384])
    w_mod = din("w_mod", [D, 6 * D])
    w_in = din("w_in", [D, 11264])
    w_pw = din("w_pw", [DC, D])
    w_ho = din("w_ho", [DH, D])
    w_o = din("w_o", [D, D])
    w1 = din("w1", [D, DFF])
    w2 = din("w2", [DFF, D])
    out = nc.dram_tensor("out", [D, T], F32, kind="ExternalOutput").ap()

    def dscr(name, shape, dt):
        return nc.dram_tensor(name, shape, dt, kind="Internal").ap()

    hT_d = dscr("hT_d", [D, T], BF16)
    uvh_d = dscr("uvh_d", [512, NHALO], F32)
    cN_d = dscr("cN_d", [DC, T], BF16)
    oN_d = dscr("oN_d", [DH, T], BF16)
    mg_d = dscr("mg_d", [D, T], BF16)
    x1_d = dscr("x1_d", [D, T], F32)
    dbg = {}
    if debug:
        for n, sh in [("d_mod", [128, 192]), ("d_sin", [128, 2048]), ("d_oN", [DH, T]), ("d_cN", [DC, T]),
                      ("d_x1", [D, T]), ("d_hT", [D, T])]:
            dbg[n] = nc.dram_tensor(n, sh, F32, kind="ExternalOutput").ap()

    with contextlib.ExitStack() as top:
        NS = 6
        dstreams = ["slab%d" % i for i in range(NS)] + ["sp%d" % i for i in range(Sched.NSP)]
        sems = {s: top.enter_context(nc.semaphore("s_" + s)) for s in list(Sched.COMPUTE) + dstreams}
        S = Sched(nc, sems, dstreams)

        uniq = [0]

        def sbt(es, name, shape, dt=F32):
            uniq[0] += 1
            return es.enter_context(nc.sbuf_tensor("%s_%d" % (name, uniq[0]), shape, dt))

        ps = [top.enter_context(nc.psum_tensor("ps%d" % i, [128, 512], F32)) for i in range(8)]
        PK = [("ps", i) for i in range(8)]

        Pt = sbt(top, "Pt", [128, NP])
        Ct = sbt(top, "Ct", [128, 384])
        identb = sbt(top, "identb", [128, 128], BF16)
        onesb = sbt(top, "onesb", [128, 128], BF16)
        onesf = sbt(top, "onesf", [128, T])
        cmask = sbt(top, "cmask", [128, T])
        scb = sbt(top, "scb", [128, 32], BF16)
        lbt = sbt(top, "lbt", [128, 40])
        oml = sbt(top, "oml", [128, 40])
        noml = sbt(top, "noml", [128, 40])
        modT = sbt(top, "modT", [128, 96, 2])
        der = sbt(top, "der", [128, 5, 16])
        Sin = sbt(top, "Sin", [128, 2, 8, 128])
        slabs = [sbt(top, "slab%d" % i, [128, 16, 128], BF16) for i in range(NS)]
        slab_ctr = [0]

        def pcol(name, i=0, n=1):
            o, w = PL[name]
            return Pt[:, o + i:o + i + n]

        def end_phase(name):
            S.emit_phase(final=(name == upto))
            if name == upto:
                raise _Stop(nc)

        def DMA(eng, out_, in_, reads=(), writes=(), dstream=None):
            S.op(eng, lambda e: e.dma_start(out=out_, in_=in_), reads, writes, dma=True, dstream=dstream)

        def ACT(out_, in_, func, reads=(), writes=(), **kw):
            S.op("act", lambda e: e.activation(out=out_, in_=in_, func=func, **kw), reads, writes)

        def TT(out_, in0, in1, op, reads=(), writes=(), eng="dve"):
            S.op(eng, lambda e: e.tensor_tensor(out=out_, in0=in0, in1=in1, op=op), reads, writes)

        def TS(out_, in0, s1, s2, op0, op1=None, reads=(), writes=(), eng="dve"):
            if op1 is None:
                S.op(eng, lambda e: e.tensor_scalar(out=out_, in0=in0, scalar1=s1, scalar2=None, op0=op0), reads, writes)
            else:
                S.op(eng, lambda e: e.tensor_scalar(out=out_, in0=in0, scalar1=s1, scalar2=s2, op0=op0, op1=op1), reads, writes)

        def STT(out_, in0, scalar, in1, op0, op1, reads=(), writes=()):
            S.op("dve", lambda e: e.scalar_tensor_tensor(out=out_, in0=in0, scalar=scalar, in1=in1, op0=op0, op1=op1), reads, writes)

        def CP(out_, in_, reads=(), writes=(), eng="dve"):
            S.op(eng, lambda e: e.tensor_copy(out=out_, in_=in_), reads, writes)

        def MM(out_, lhsT, rhs, start, stop, reads=(), writes=()):
            S.op("pe", lambda e: e.matmul(out_, lhsT=lhsT, rhs=rhs, start=start, stop=stop), reads, writes)

        def TR(out_, in_, reads=(), writes=()):
            S.op("pe", lambda e: e.transpose(out_, in_, identb[:]), reads, writes)

        def MS(ap, val, writes=()):
            S.op("pool", lambda e: e.memset(ap, val), (), writes)

        def load_slab(W, row0, nk, col0):
            i = slab_ctr[0] % NS
            slab_ctr[0] += 1
            src = W[row0:row0 + nk * 128, col0:col0 + 128].rearrange("(k p) n -> p k n", p=128)
            DMA("pool", slabs[i][:, 0:nk, :], src, writes=[("slab", i)], dstream="slab%d" % i)
            return i

        def mm_slab(out_, si, nk, rhs_fn, rkeys, wkey):
            for k in range(nk):
                MM(out_, slabs[si][:, k, :], rhs_fn(k), k == 0, k == nk - 1,
                   reads=[("slab", si)] + list(rkeys), writes=[wkey])

        def bfview(p):
            return p[:].bitcast(BF16)

        DMA("sp", Pt[:], prm[:, :], writes=["Pt"])
        DMA("sp", Ct[:], cst[:, :], writes=["Ct"])
        CP(identb[:], Ct[:, 0:128], reads=["Ct"], writes=["identb"])
        MS(onesb[:], 1.0, writes=["onesb"])
        MS(onesf[:], 1.0, writes=["onesf"])
        MS(cmask[:], 1.0, writes=["cmask"])
        MS(cmask[:].rearrange("p (c t) -> p c t", t=64)[:, :, 0:1], 0.0, writes=["cmask"])
        MS(Sin[:], 0.0, writes=["Sin"])
        o_cp = PL["cp"][0]
        ACT(scb[:], Pt[:, o_cp:o_cp + 32], AF.Silu, reads=["Pt"], writes=["scb"])
        o_l0, o_l1 = PL["l0"][0], PL["l1"][0]
        TT(lbt[:], Pt[:, o_l0:o_l0 + 40], Pt[:, o_l1:o_l1 + 40], ALU.subtract, reads=["Pt"], writes=["lbt"])
        ACT(lbt[:], lbt[:], AF.Sigmoid, reads=["lbt"], writes=["lbt"])
        TS(oml[:], lbt[:], -1.0, 1.0, ALU.mult, ALU.add, reads=["lbt"], writes=["oml"])
        TS(noml[:], lbt[:], -1.0, None, ALU.add, reads=["lbt"], writes=["noml"])
        scb3 = scb[:].rearrange("p (c r) -> p c r", r=2)
        for m in range(32):
            si = load_slab(w_mod, 0, 16, m * 128)
            mm_slab(ps[0][:, 2 * m:2 * m + 2], si, 16, lambda k: scb3[:, k, :], ["scb"], PK[0])
        o_bm = PL["bmod"][0]
        TT(modT[:, 0:32, :], ps[0][:, 0:64].rearrange("p (m r) -> p m r", r=2),
           Pt[:, o_bm:o_bm + 32].unsqueeze(2).to_broadcast([128, 32, 2]), ALU.add,
           reads=[PK[0], "Pt"], writes=["modT"])
        mod_next = [32]

        def mod_tail_step(n=1):
            for _ in range(n):
                m = mod_next[0]
                if m >= 96:
                    return
                mod_next[0] += 1
                si = load_slab(w_mod, 0, 16, m * 128)
                mm_slab(ps[7][:, 2 * (m - 32):2 * (m - 32) + 2], si, 16, lambda k: scb3[:, k, :], (), PK[7])
        o_npm, o_npo, o_npl, o_npo2 = PL["npm"][0], PL["npo"][0], PL["npl"][0], PL["npo2"][0]
        STT(der[:, 0, :], modT[:, 16:32, 0], 1.0, Pt[:, o_npm:o_npm + 16], ALU.add, ALU.mult, reads=["modT", "Pt"], writes=["der"])
        STT(der[:, 1, :], modT[:, 16:32, 1], 1.0, Pt[:, o_npm:o_npm + 16], ALU.add, ALU.mult, reads=["modT", "Pt"], writes=["der"])
        end_phase("P0")

        def make_h(tmp, src, n, gs_col, sh_col, out_fn, okey, sb=7):
            xb, sqb, tx, rs = tmp["xb"], tmp["sqb"], tmp["tx"], tmp["rs"]
            DMA("sp", xb[:, :, 0:n], fm(src), writes=["xb"])
            for c in range(16):
                q = c % 2
                ACT(sqb[q][:, 0:n], xb[:, c, 0:n], AF.Square, reads=["xb"], writes=[("sqb", q)])
                MM(ps[sb][:, 0:n], onesb[:], sqb[q][:, 0:n], c == 0, c == 15, reads=[("sqb", q)], writes=[PK[sb]])
            ACT(rs[:, 0:n], ps[sb][:, 0:n], AF.Sqrt, reads=[PK[sb]], writes=["rs"], scale=1.0 / D, bias=EPS)
            S.op("dve", lambda e: e.reciprocal(out=rs[:, 0:n], in_=rs[:, 0:n]), ["rs"], ["rs"])
            for c in range(16):
                q = c % 2
                TT(tx[q][:, 0:n], xb[:, c, 0:n], rs[:, 0:n], ALU.mult, reads=["xb", "rs"], writes=[("tx", q)],
                   eng="dve")
                ACT(out_fn(c), tx[q][:, 0:n], AF.Identity, reads=[("tx", q)], writes=[okey],
                    scale=gs_col(c), bias=sh_col(c))

        def h_tmps(es):
            return {"xb": sbt(es, "xb", [128, 16, 512]),
                    "sqb": [sbt(es, "sqb%d" % q, [128, 512], BF16) for q in range(2)],
                    "tx": [sbt(es, "tx%d" % q, [128, 512]) for q in range(2)],
                    "rs": sbt(es, "rs", [128, 512])}

        gs_m = lambda c: der[:, 0, c:c + 1]
        gsc_m = lambda c: der[:, 1, c:c + 1]
        gs_f = lambda c: der[:, 2, c:c + 1]
        sh_m = lambda c: modT[:, c, 0:1]
        shc_m = lambda c: modT[:, c, 1:2]
        sh_f = lambda c: modT[:, 48 + c, 0:1]

        with contextlib.ExitStack() as es:
            tmp = h_tmps(es)
            hR = sbt(es, "hR", [128, 16, T], BF16)
            vR = sbt(es, "vR", [128, 8, 1024], BF16)
            sg2 = [sbt(es, "sg%d" % p, [128, T]) for p in range(2)]
            lg2 = [sbt(es, "lg%d" % p, [128, T]) for p in range(2)]
            kk2 = [sbt(es, "kk%d" % p, [128, T]) for p in range(2)]
            Pc2 = [sbt(es, "Pc%d" % p, [128, T]) for p in range(2)]
            khat2 = [sbt(es, "khat%d" % p, [128, T], BF16) for p in range(2)]
            khT2 = [sbt(es, "khT%d" % p, [128, T], BF16) for p in range(2)]
            vTf = sbt(es, "vTf", [128, T], BF16)
            G = sbt(es, "G", [128, 8])
            Gb = sbt(es, "Gb", [128, 8])
            GA = sbt(es, "GA", [128, 2, 8])
            SA = sbt(es, "SA", [128, 2, 8, 128])
            Sctx = sbt(es, "Sctx", [128, 2, 8, 128])
            MS(GA[:], 0.0, writes=["GA"])
            MS(SA[:], 0.0, writes=["SA"])
            MS(G[:], 0.0, writes=["G"])

            def region(src, nt, gs_col, sh_col, wf, wf_row0, wf_col0, lb0, post):
                bs = min(512, nt)
                for blk in range(nt // bs):
                    make_h(tmp, src[:, blk * bs:(blk + 1) * bs], bs, gs_col, sh_col,
                           lambda c, blk=blk: hR[:, c, blk * bs:(blk + 1) * bs], "hR", sb=5)
                ntile = nt // 128
                def stageA(hd):
                    p = hd % 2
                    sg, lg, kk, Pc, khat = sg2[p], lg2[p], kk2[p], Pc2[p], khat2[p]
                    fbank = (0, 1) if p == 0 else (2, 3)
                    K = lambda n: (n, p)
                    mod_tail_step(2)
                    sv_ = load_slab(w_in, 0, 16, C_I + hd * 128)
                    for blk in range(nt // bs):
                        mm_slab(ps[2 + blk][:, 0:bs], sv_, 16, lambda k, blk=blk: hR[:, k, blk * bs:(blk + 1) * bs], ["hR"], PK[2 + blk])
                        ACT(vTf[:, blk * bs:(blk + 1) * bs], ps[2 + blk][:, 0:bs], AF.Copy, reads=[PK[2 + blk]], writes=["vTf"])
                    si = load_slab(wf, wf_row0, 16, wf_col0 + hd * 128)
                    for blk in range(nt // bs):
                        pb = ps[blk]
                        mm_slab(pb[:, 0:bs], si, 16, lambda k, blk=blk: hR[:, k, blk * bs:(blk + 1) * bs], ["hR"], PK[blk])
                        ACT(sg[:, blk * bs:(blk + 1) * bs], pb[:, 0:bs], AF.Sigmoid, reads=[PK[blk]], writes=[K("sg")])
                    pTv = bfview(ps[4])
                    for tl in range(ntile):
                        TR(pTv[:, tl * 128:(tl + 1) * 128], vTf[:, tl * 128:(tl + 1) * 128], reads=["vTf"], writes=[PK[4]])
                    ACT(vR[:, 0:ntile, hd * 128:(hd + 1) * 128],
                        pTv[:, 0:ntile * 128].rearrange("p (a b) -> p a b", b=128), AF.Copy,
                        reads=[PK[4]], writes=[("vR", hd)])
                    li = lb0 + hd
                    ACT(lg[:, 0:nt], sg[:, 0:nt], AF.Ln, reads=[K("sg")], writes=[K("lg")], scale=oml[:, li:li + 1], bias=lbt[:, li:li + 1])
                    TS(kk[:, 0:nt], sg[:, 0:nt], noml[:, li:li + 1], oml[:, li:li + 1], ALU.mult, ALU.add, reads=[K("sg")], writes=[K("kk")])
                    S.op("dve", lambda e, hd=hd, Pc=Pc, lg=lg: e.tensor_tensor_scan(out=Pc[:, 0:nt], data0=onesf[:, 0:nt], data1=lg[:, 0:nt],
                                                                     initial=G[:, hd:hd + 1], op0=ALU.mult, op1=ALU.add),
                         [K("lg"), "G"], [K("Pc")])
                    CP(G[:, hd:hd + 1], Pc[:, nt - 1:nt], reads=[K("Pc")], writes=["G"])
                    TT(sg[:, 0:nt], Pc[:, 0:nt], lg[:, 0:nt], ALU.subtract, reads=[K("Pc"), K("lg")], writes=[K("sg")])
                    ACT(lg[:, 0:nt], sg[:, 0:nt], AF.Exp, reads=[K("sg")], writes=[K("lg")])
                    TT(khat[:, 0:nt], kk[:, 0:nt], lg[:, 0:nt], ALU.mult, reads=[K("kk"), K("lg")], writes=[K("khat")])

                def stageB(hd):
                    p = hd % 2
                    khat, khT = khat2[p], khT2[p]
                    K = lambda n: (n, p)
                    sbank = 6
                    pT = bfview(ps[5])
                    for tl in range(ntile):
                        TR(pT[:, tl * 128:(tl + 1) * 128], khat[:, tl * 128:(tl + 1) * 128], reads=[K("khat")], writes=[PK[5]])
                    ACT(khT[:, 0:nt], pT[:, 0:nt], AF.Copy, reads=[PK[5]], writes=[K("khT")])
                    for tl in range(ntile):
                        MM(ps[sbank][:, 0:128], khT[:, tl * 128:(tl + 1) * 128], vR[:, tl, hd * 128:(hd + 1) * 128],
                           tl == 0, tl == ntile - 1, reads=[K("khT"), ("vR", hd)], writes=[PK[sbank]])
                    post(hd, ps[sbank][:, 0:128], PK[sbank])

                stageA(0)
                for hd in range(8):
                    if hd + 1 < 8:
                        stageA(hd + 1)
                    stageB(hd)

            for d in range(2):
                MS(G[:], 0.0, writes=["G"])
                region(ctx2[:, d * 256:(d + 1) * 256], 256, gsc_m, shc_m, w_in, 0, C_FF if d == 0 else C_FB, 8 * d,
                       lambda hd, acc, pk, d=d: CP(Sctx[:, d, hd, :], acc, reads=[pk], writes=["Sctx"]))

            for s in range(3):
                if s == 0:
                    MS(G[:], 0.0, writes=["G"])
                else:
                    TS(G[:], G[:], pcol("slk", s), None, ALU.mult, reads=["G"], writes=["G"])
                CP(Gb[:], G[:], reads=["G"], writes=["Gb"])

                def post_slot(hd, acc, pk, s=s):
                    for d in range(2):
                        STT(SA[:, d, hd, :], acc, pcol("sla" if d == 0 else "slb", s), SA[:, d, hd, :], ALU.mult, ALU.add,
                            reads=[pk, "SA"], writes=["SA"])
                region(xs[s * D:(s + 1) * D, :], T, gs_m, sh_m, wfs, s * D, 0, 16 + 8 * s, post_slot)
                TT(Gb[:], G[:], Gb[:], ALU.subtract, reads=["G", "Gb"], writes=["Gb"])
                for d in range(2):
                    STT(GA[:, d, :], Gb[:], pcol("sla" if d == 0 else "slb", s), GA[:, d, :], ALU.mult, ALU.add,
                        reads=["Gb", "GA"], writes=["GA"])
            mod_tail_step(96)
            TT(modT[:, 32:96, :], ps[7][:, 0:128].rearrange("p (m r) -> p m r", r=2),
               Pt[:, o_bm + 32:o_bm + 96].unsqueeze(2).to_broadcast([128, 64, 2]), ALU.add,
               reads=[PK[7]], writes=["modT"])
            STT(der[:, 2, :], modT[:, 64:80, 0], 1.0, Pt[:, o_npl:o_npl + 16], ALU.add, ALU.mult, reads=["modT"], writes=["der"])
            TT(der[:, 3, :], modT[:, 32:48, 0], Pt[:, o_npo:o_npo + 16], ALU.mult, reads=["modT"], writes=["der"])
            TT(der[:, 4, :], modT[:, 80:96, 0], Pt[:, o_npo2:o_npo2 + 16], ALU.mult, reads=["modT"], writes=["der"])
            if debug:
                DMA("sp", dbg["d_mod"][:, :], modT[:].rearrange("p m r -> p (m r)"), reads=["modT"], writes=["dd"])
            ACT(GA[:], GA[:], AF.Exp, reads=["GA"], writes=["GA"])
            for d in range(2):
                for hd in range(8):
                    STT(Sin[:, d, hd, :], Sctx[:, d, hd, :], GA[:, d, hd:hd + 1], SA[:, d, hd, :], ALU.mult, ALU.add,
                        reads=["Sctx", "GA", "SA"], writes=["Sin"])
            if debug:
                DMA("sp", dbg["d_sin"][:, :], Sin[:].rearrange("p d h v -> p (d h v)"), reads=["Sin"], writes=["dd"])
            end_phase("R")

        with contextlib.ExitStack() as es:
            tmp = h_tmps(es)
            hT = sbt(es, "hT", [128, 16, T], BF16)
            for blk in range(2):
                make_h(tmp, xo[:, blk * 512:(blk + 1) * 512], 512, gs_m, sh_m,
                       lambda c, blk=blk: hT[:, c, blk * 512:(blk + 1) * 512], ("hT", blk))
                DMA("sp", fm(hT_d)[:, :, blk * 512:(blk + 1) * 512], hT[:, :, blk * 512:(blk + 1) * 512],
                    reads=[("hT", blk)], writes=["hT_d"])
            end_phase("H")

        with contextlib.ExitStack() as es:
            tmp = h_tmps(es)
            hB = sbt(es, "hB", [128, 16, 512], BF16)
            sgt = [sbt(es, "sgt%d" % q, [128, 512]) for q in range(2)]
            ub = [sbt(es, "ub%d" % q, [128, 512]) for q in range(2)]
            hoff = 0
            for hb, n in enumerate([512, 448, 512, 448]):
                side = 0 if hb < 2 else 1
                make_h(tmp, xh[:, hoff:hoff + n], n, gs_m, sh_m, lambda c: hB[:, c, 0:n], "hB")
                for cc in range(4, 8):
                    q = cc % 2
                    sv = load_slab(w_in, 0, 16, C_CV + cc * 128)
                    sgi = load_slab(w_in, 0, 16, C_CG + cc * 128)
                    mm_slab(ps[0 + 2 * q][:, 0:n], sv, 16, lambda k: hB[:, k, 0:n], ["hB"], PK[0 + 2 * q])
                    mm_slab(ps[1 + 2 * q][:, 0:n], sgi, 16, lambda k: hB[:, k, 0:n], ["hB"], PK[1 + 2 * q])
                    ACT(sgt[q][:, 0:n], ps[1 + 2 * q][:, 0:n], AF.Sigmoid, reads=[PK[1 + 2 * q]], writes=[("sgt", q)])
                    STT(ub[q][:, 0:n], ps[0 + 2 * q][:, 0:n], pcol("hm", side), sgt[q][:, 0:n], ALU.mult, ALU.mult,
                        reads=[("sgt", q), PK[0 + 2 * q]], writes=[("ub", q)])
                    DMA("sp", uvh_d[(cc - 4) * 128:(cc - 3) * 128, hoff:hoff + n], ub[q][:, 0:n],
                        reads=[("ub", q)], writes=["uvh_d"])
                hoff += n
            end_phase("C1")

        with contextlib.ExitStack() as es:
            hT = sbt(es, "hT", [128, 16, T], BF16)
            uv = sbt(es, "uv", [128, 46 * 64])
            uh = sbt(es, "uh", [128, 16, 94])
            conv = sbt(es, "conv", [128, 8, T])
            sgt = [sbt(es, "sgt%d" % q, [128, 512]) for q in range(2)]
            cb16 = [sbt(es, "cb16%d" % q, [128, 512], BF16) for q in range(2)]
            cs16 = [sbt(es, "cs16%d" % q, [128, 512], BF16) for q in range(2)]
            mean = sbt(es, "mean", [128, 512])
            rstd = sbt(es, "rstd", [128, 512])
            t1 = [sbt(es, "t1%d" % q, [128, 512]) for q in range(2)]
            cN = sbt(es, "cN", [128, 8, T], BF16)
            DMA("sp", hT[:], fm(hT_d), writes=["hT"])
            o_cw, o_cb = PL["cw"][0], PL["cb"][0]
            for cc in range(8):
                vert = cc >= 4
                if vert:
                    DMA("sp", uv[:, 0:960], uvh_d[(cc - 4) * 128:(cc - 3) * 128, 0:960], writes=["uv"])
                    DMA("sp", uv[:, 1984:2944], uvh_d[(cc - 4) * 128:(cc - 3) * 128, 960:1920], writes=["uv"])
                else:
                    MS(uh[:], 0.0, writes=["uh"])
                sv = load_slab(w_in, 0, 16, C_CV + cc * 128)
                sgi = load_slab(w_in, 0, 16, C_CG + cc * 128)
                for blk in range(2):
                    q = blk
                    mm_slab(ps[0 + 2 * q][:], sv, 16, lambda k, blk=blk: hT[:, k, blk * 512:(blk + 1) * 512], ["hT"], PK[0 + 2 * q])
                    mm_slab(ps[1 + 2 * q][:], sgi, 16, lambda k, blk=blk: hT[:, k, blk * 512:(blk + 1) * 512], ["hT"], PK[1 + 2 * q])
                    ACT(sgt[q][:], ps[1 + 2 * q][:], AF.Sigmoid, reads=[PK[1 + 2 * q]], writes=[("sgt", q)])
                    if vert:
                        TT(uv[:, 960 + blk * 512:960 + (blk + 1) * 512], ps[0 + 2 * q][:], sgt[q][:], ALU.mult,
                           reads=[PK[0 + 2 * q], ("sgt", q)], writes=["uv"])
                    else:
                        TT(uh[:, blk * 8:(blk + 1) * 8, 15:79], ps[0 + 2 * q][:].rearrange("p (r w) -> p r w", w=64),
                           sgt[q][:].rearrange("p (r w) -> p r w", w=64), ALU.mult,
                           reads=[PK[0 + 2 * q], ("sgt", q)], writes=["uh"])
                acc = conv[:, cc, :]
                for dd in range(31):
                    wcol = Pt[:, o_cw + cc * 31 + dd:o_cw + cc * 31 + dd + 1]
                    if vert:
                        src = uv[:, dd * 64:dd * 64 + 1024]
                        dst = acc
                        accin = acc
                    else:
                        src = uh[:, :, dd:dd + 64]
                        dst = acc.rearrange("p (r w) -> p r w", w=64)
                        accin = dst
                    if dd == 0:
                        TS(dst, src, wcol, Pt[:, o_cb + cc:o_cb + cc + 1], ALU.mult, ALU.add,
                           reads=["uv" if vert else "uh"], writes=[("conv", cc)])
                    else:
                        STT(dst, src, wcol, accin, ALU.mult, ALU.add,
                            reads=["uv" if vert else "uh", ("conv", cc)], writes=[("conv", cc)])
            o_lg, o_lb = PL["lng"][0], PL["lnb"][0]
            for blk in range(2):
                bsl = slice(blk * 512, (blk + 1) * 512)
                for cc in range(8):
                    q = cc % 2
                    ACT(cb16[q][:], conv[:, cc, bsl], AF.Copy, reads=[("conv", cc)], writes=[("cb16", q)])
                    ACT(cs16[q][:], conv[:, cc, bsl], AF.Square, reads=[("conv", cc)], writes=[("cs16", q)])
                    MM(ps[4][:], onesb[:], cb16[q][:], cc == 0, cc == 7, reads=[("cb16", q)], writes=[PK[4]])
                    MM(ps[5][:], onesb[:], cs16[q][:], cc == 0, cc == 7, reads=[("cs16", q)], writes=[PK[5]])
                ACT(mean[:], ps[4][:], AF.Copy, reads=[PK[4]], writes=["mean"], scale=1.0 / DC)
                TT(rstd[:], mean[:], mean[:], ALU.mult, reads=["mean"], writes=["rstd"])
                STT(rstd[:], ps[5][:], 1.0 / DC, rstd[:], ALU.mult, ALU.subtract, reads=[PK[5], "rstd"], writes=["rstd"])
                ACT(rstd[:], rstd[:], AF.Sqrt, reads=["rstd"], writes=["rstd"], bias=EPS)
                S.op("dve", lambda e: e.reciprocal(out=rstd[:], in_=rstd[:]), ["rstd"], ["rstd"])
                for cc in range(8):
                    q = cc % 2
                    TT(t1[q][:], conv[:, cc, bsl], mean[:], ALU.subtract, reads=[("conv", cc), "mean"], writes=[("t1", q)])
                    TT(t1[q][:], t1[q][:], rstd[:], ALU.mult, reads=[("t1", q), "rstd"], writes=[("t1", q)])
                    ACT(cN[:, cc, bsl], t1[q][:], AF.Silu, reads=[("t1", q)], writes=["cN"],
                        scale=Pt[:, o_lg + cc:o_lg + cc + 1], bias=Pt[:, o_lb + cc:o_lb + cc + 1])
            DMA("sp", fm(cN_d), cN[:], reads=["cN"], writes=["cN_d"])
            if debug:
                cNf = sbt(es, "cNf", [128, 8, T])
                CP(cNf[:], cN[:], reads=["cN"], writes=["cNf"])
                DMA("sp", fm(dbg["d_cN"]), cNf[:], reads=["cNf"], writes=["dd"])
            end_phase("C2")

        with contextlib.ExitStack() as es:
            hT = sbt(es, "hT", [128, 16, T], BF16)
            vtok = sbt(es, "vtok", [128, 8, 1024], BF16)
            sg = sbt(es, "sg", [128, T])
            lgd = [sbt(es, "lg%d" % d, [128, T]) for d in range(2)]
            kkd = [sbt(es, "kk%d" % d, [128, T]) for d in range(2)]
            qs = sbt(es, "qs", [128, T])
            gsil = sbt(es, "gsil", [128, T])
            aa = sbt(es, "aa", [128, T])
            ab = sbt(es, "ab", [128, T])
            tA = sbt(es, "tA", [128, T])
            tB = sbt(es, "tB", [128, T])
            qt = [sbt(es, "qt%d" % d, [128, T], BF16) for d in range(2)]
            kt = [sbt(es, "kt%d" % d, [128, T], BF16) for d in range(2)]
            kh = [sbt(es, "kh%d" % d, [128, T], BF16) for d in range(2)]
            dec = [sbt(es, "dec%d" % d, [128, 16]) for d in range(2)]
            Sb = [sbt(es, "Sb%d" % d, [128, 16, 128], BF16) for d in range(2)]
            scm = [sbt(es, "scm%d" % d, [128, T], BF16) for d in range(2)]
            osq = sbt(es, "osq", [128, 512], BF16)
            rs = sbt(es, "rsH", [128, 512])
            t1 = sbt(es, "t1H", [128, 512])
            oN = sbt(es, "oN", [128, T], BF16)
            pmask = [sbt(es, "pmask%d" % q, [128, T], BF16) for q in range(2)]
            khp = [sbt(es, "khp%d" % q, [128, T], BF16) for q in range(2)]
            khTp = [[sbt(es, "khTp%d%d" % (d, q), [128, T], BF16) for q in range(2)] for d in range(2)]
            Srun2 = [[sbt(es, "SrunP%d%d" % (d, q), [128, 128]) for q in range(2)] for d in range(2)]
            for q in range(2):
                pv = pmask[q][:].rearrange("p (a h t) -> p a h t", h=2, t=64)
                MS(pv[:, :, q, :], 1.0, writes=[("pmask", q)])
                MS(pv[:, :, 1 - q, :], 0.0, writes=[("pmask", q)])
            if debug:
                oNf = sbt(es, "oNf", [128, T])
            DMA("sp", hT[:], fm(hT_d), writes=["hT"])
            for cs in range(8):
                si = load_slab(w_in, 0, 16, C_I + cs * 128)
                vb = scm[cs % 2]
                for blk in range(2):
                    b = 2 * (cs % 2) + blk
                    mm_slab(ps[b][:], si, 16, lambda k, blk=blk: hT[:, k, blk * 512:(blk + 1) * 512], ["hT"], PK[b])
                    ACT(vb[:, blk * 512:(blk + 1) * 512], ps[b][:], AF.Copy, reads=[PK[b]], writes=[("scm", cs % 2)])
                pTv = bfview(ps[4 + cs % 2])
                for tl in range(8):
                    TR(pTv[:, tl * 128:(tl + 1) * 128], vb[:, tl * 128:(tl + 1) * 128], reads=[("scm", cs % 2)], writes=[PK[4 + cs % 2]])
                ACT(vtok[:, :, cs * 128:(cs + 1) * 128], pTv[:, :].rearrange("p (a b) -> p a b", b=128), AF.Copy,
                    reads=[PK[4 + cs % 2]], writes=["vtok"])
            o_hng = PL["hng"][0]
            a3 = lambda t: t[:].rearrange("p (c t) -> p c t", t=64)
            HGS = int(os.environ.get('HGS', '9'))
            for hd in range(int(os.environ.get('HGNH', '8'))):
                def proj(colbase, pa):
                    si = load_slab(w_in, 0, 16, colbase + hd * 128)
                    for blk in range(2):
                        mm_slab(ps[pa + blk][:], si, 16, lambda k, blk=blk: hT[:, k, blk * 512:(blk + 1) * 512], ["hT"], PK[pa + blk])
                for d, cbase in ((0, C_FF), (1, C_FB)):
                    proj(cbase, 2 * d)
                    for blk in range(2):
                        ACT(sg[:, blk * 512:(blk + 1) * 512], ps[2 * d + blk][:], AF.Sigmoid, reads=[PK[2 * d + blk]], writes=["sg"])
                    li = 8 * d + hd
                    ACT(lgd[d][:], sg[:], AF.Ln, reads=["sg"], writes=[("lg", d)], scale=oml[:, li:li + 1], bias=lbt[:, li:li + 1])
                    TS(kkd[d][:], sg[:], noml[:, li:li + 1], oml[:, li:li + 1], ALU.mult, ALU.add, reads=["sg"], writes=[("kk", d)])
                proj(C_Q, 0)
                for blk in range(2):
                    ACT(qs[:, blk * 512:(blk + 1) * 512], ps[blk][:], AF.Silu, reads=[PK[blk]], writes=["qs"])
                proj(C_G, 2)
                for blk in range(2):
                    ACT(gsil[:, blk * 512:(blk + 1) * 512], ps[2 + blk][:], AF.Silu, reads=[PK[2 + blk]], writes=["gsil"])
                for d in range(2 if HGS >= 2 else 0):
                    S.op("dve", lambda e, d=d: e.tensor_tensor_scan(out=aa[:], data0=cmask[:], data1=lgd[d][:], initial=0.0,
                                                                   op0=ALU.mult, op1=ALU.add), [("lg", d)], ["aa"])
                    if d == 1:
                        TT(tA[:], lgd[1][:], aa[:], ALU.subtract, reads=[("lg", 1), "aa"], writes=["tA"])
                        TT(a3(ab), a3(tA), a3(aa)[:, :, 63:64].to_broadcast([128, 16, 64]), ALU.add, reads=["tA", "aa"], writes=["ab"])
                    edge = 63 if d == 0 else 0
                    av, ak = (aa, "aa") if d == 0 else (ab, "ab")
                    ACT(tA[:], av[:], AF.Exp, reads=[ak], writes=["tA"])
                    STT(qt[d][:], tA[:], 128.0 ** -0.5, qs[:], ALU.mult, ALU.mult, reads=["tA", "qs"], writes=[("qt", d)])
                    CP(dec[d][:], a3(tA)[:, :, edge], reads=["tA"], writes=[("dec", d)])
                    ACT(tB[:], av[:], AF.Exp, reads=[ak], writes=["tB"], scale=-1.0)
                    TT(kt[d][:], kkd[d][:], tB[:], ALU.mult, reads=[("kk", d), "tB"], writes=[("kt", d)])
                    TT(a3(tA), a3(av)[:, :, edge:edge + 1].to_broadcast([128, 16, 64]), a3(av), ALU.subtract, reads=[ak], writes=["tA"])
                    ACT(tB[:], tA[:], AF.Exp, reads=["tA"], writes=["tB"])
                    TT(kh[d][:], kkd[d][:], tB[:], ALU.mult, reads=[("kk", d), "tB"], writes=[("kh", d)])
                    if HGS < 3:
                        continue
                    pT = bfview(ps[4])
                    for q in range(2):
                        TT(khp[q][:], kh[d][:], pmask[q][:], ALU.mult, reads=[("kh", d), ("pmask", q)], writes=[("khp", q)])
                        for tl in range(8):
                            TR(pT[:, tl * 128:(tl + 1) * 128], khp[q][:, tl * 128:(tl + 1) * 128], reads=[("khp", q)], writes=[PK[4]])
                        ACT(khTp[d][q][:], pT[:, :], AF.Copy, reads=[PK[4]], writes=[("khTp", d, q)])
                    if HGS < 5:
                        continue
                    mk = Ct[:, 128:256] if d == 0 else Ct[:, 256:384]
                    for g in range(2):
                        pb = ps[6 + d]
                        for tl in range(4):
                            tile = g * 4 + tl
                            MM(pb[:, tl * 128:(tl + 1) * 128], kt[d][:, tile * 128:(tile + 1) * 128],
                               qt[d][:, tile * 128:(tile + 1) * 128], True, True, reads=[("kt", d), ("qt", d)], writes=[PK[6 + d]])
                        TT(scm[d][:, g * 512:(g + 1) * 512].rearrange("p (a b) -> p a b", b=128),
                           pb[:].rearrange("p (a b) -> p a b", b=128),
                           mk.unsqueeze(1).to_broadcast([128, 4, 128]), ALU.mult, reads=[PK[6 + d]], writes=[("scm", d)])
                for d in range(2):
                    c0 = 0 if d == 0 else 15
                    CP(Srun2[d][0][:], Sin[:, d, hd, :], writes=[("Srun", d, 0)])
                    ACT(Sb[d][:, c0, :], Sin[:, d, hd, :], AF.Copy, writes=[("Sb", d)])
                orders = [list(range(0, 15)), list(range(15, 0, -1))]
                for n in range(15):
                    for d in range(2):
                        c = orders[d][n]
                        tl, hf = c // 2, c % 2
                        b = 2 * d + (n % 2)
                        slot = ps[b][:, 0:128]
                        MM(slot, khTp[d][hf][:, tl * 128:(tl + 1) * 128], vtok[:, tl, hd * 128:(hd + 1) * 128], True, True,
                           reads=[("khTp", d, hf), "vtok"], writes=[PK[b]])
                        src, dst = Srun2[d][n % 2], Srun2[d][(n + 1) % 2]
                        TS(dst[:], src[:], dec[d][:, c:c + 1], None, ALU.mult,
                           reads=[("Srun", d, n % 2), ("dec", d)], writes=[("Srun", d, (n + 1) % 2)])
                        TT(dst[:], slot, dst[:], ALU.add,
                           reads=[("Srun", d, (n + 1) % 2), PK[b]], writes=[("Srun", d, (n + 1) % 2)])
                        cn = c + 1 if d == 0 else c - 1
                        ACT(Sb[d][:, cn, :], dst[:], AF.Copy, reads=[("Srun", d, (n + 1) % 2)], writes=[("Sb", d)])
                for tile in range(8 if HGS >= 6 else 0):
                    pb = ps[tile // 4]
                    po = pb[:, (tile % 4) * 128:(tile % 4 + 1) * 128]
                    vt = vtok[:, tile, hd * 128:(hd + 1) * 128]
                    MM(po, vt, scm[0][:, tile * 128:(tile + 1) * 128], True, False, reads=["vtok", ("scm", 0)], writes=[PK[tile // 4]])
                    MM(po, vt, scm[1][:, tile * 128:(tile + 1) * 128], False, False, reads=["vtok", ("scm", 1)], writes=[PK[tile // 4]])
                    for hf in range(2):
                        c = tile * 2 + hf
                        pc = pb[:, (tile % 4) * 128 + hf * 64:(tile % 4) * 128 + hf * 64 + 64]
                        MM(pc, Sb[0][:, c, :], qt[0][:, c * 64:(c + 1) * 64], False, False, reads=[("Sb", 0), ("qt", 0)], writes=[PK[tile // 4]])
                        MM(pc, Sb[1][:, c, :], qt[1][:, c * 64:(c + 1) * 64], False, hf == 1, reads=[("Sb", 1), ("qt", 1)], writes=[PK[tile // 4]])
                for blk in range(2):
                    ACT(osq[:], ps[blk][:], AF.Square, reads=[PK[blk]], writes=["osq"])
                    MM(ps[6][:], onesb[:], osq[:], True, True, reads=["osq"], writes=[PK[6]])
                    ACT(rs[:], ps[6][:], AF.Sqrt, reads=[PK[6]], writes=["rsH"], scale=1.0 / 128, bias=EPS)
                    S.op("dve", lambda e: e.reciprocal(out=rs[:], in_=rs[:]), ["rsH"], ["rsH"])
                    TT(t1[:], ps[blk][:], rs[:], ALU.mult, reads=[PK[blk], "rsH"], writes=["t1H"])
                    STT(oN[:, blk * 512:(blk + 1) * 512], t1[:], Pt[:, o_hng + hd:o_hng + hd + 1], gsil[:, blk * 512:(blk + 1) * 512],
                        ALU.mult, ALU.mult, reads=["t1H", "gsil"], writes=["oN"])
                DMA("sp", oN_d[hd * 128:(hd + 1) * 128, :], oN[:], reads=["oN"], writes=["oN_d"])
                if debug:
                    CP(oNf[:], oN[:], reads=["oN"], writes=["oNf"])
                    DMA("sp", dbg["d_oN"][hd * 128:(hd + 1) * 128, :], oNf[:], reads=["oNf"], writes=["dd"])
            end_phase("HG")

        with contextlib.ExitStack() as es:
            hT = sbt(es, "hT", [128, 16, T], BF16)
            cN = sbt(es, "cN", [128, 8, T], BF16)
            oN = sbt(es, "oNa", [128, 8, T], BF16)
            sgc = [sbt(es, "sgc%d" % q, [128, 512]) for q in range(2)]
            sgh = [sbt(es, "sgh%d" % q, [128, 512]) for q in range(2)]
            m1 = [sbt(es, "m1%d" % q, [128, 512]) for q in range(2)]
            mg = sbt(es, "mg", [128, 16, T], BF16)
            DMA("sp", hT[:], fm(hT_d), writes=["hT"])
            DMA("sp", cN[:], fm(cN_d), writes=["cN"])
            DMA("sp", oN[:], fm(oN_d), writes=["oNa"])
            for m in range(16):
                s_pw = load_slab(w_pw, 0, 8, m * 128)
                s_ho = load_slab(w_ho, 0, 8, m * 128)
                s_gc = load_slab(w_in, 0, 16, C_GC + m * 128)
                s_gh = load_slab(w_in, 0, 16, C_GH + m * 128)
                for blk in range(2):
                    q = blk
                    bsl = slice(blk * 512, (blk + 1) * 512)
                    b0 = 4 * q
                    mm_slab(ps[b0][:], s_pw, 8, lambda k: cN[:, k, bsl], ["cN"], PK[b0])
                    mm_slab(ps[b0 + 1][:], s_ho, 8, lambda k: oN[:, k, bsl], ["oNa"], PK[b0 + 1])
                    mm_slab(ps[b0 + 2][:], s_gc, 16, lambda k: hT[:, k, bsl], ["hT"], PK[b0 + 2])
                    mm_slab(ps[b0 + 3][:], s_gh, 16, lambda k: hT[:, k, bsl], ["hT"], PK[b0 + 3])
                    ACT(sgc[q][:], ps[b0 + 2][:], AF.Sigmoid, reads=[PK[b0 + 2]], writes=[("sgc", q)])
                    ACT(sgh[q][:], ps[b0 + 3][:], AF.Sigmoid, reads=[PK[b0 + 3]], writes=[("sgh", q)])
                    TT(m1[q][:], ps[b0][:], sgc[q][:], ALU.mult, reads=[PK[b0], ("sgc", q)], writes=[("m1", q)])
                    TT(sgh[q][:], ps[b0 + 1][:], sgh[q][:], ALU.mult, reads=[PK[b0 + 1], ("sgh", q)], writes=[("sgh", q)])
                    TT(mg[:, m, bsl], m1[q][:], sgh[q][:], ALU.add, reads=[("m1", q), ("sgh", q)], writes=["mg"])
            DMA("sp", fm(mg_d), mg[:], reads=["mg"], writes=["mg_d"])
            end_phase("M1")

        def post_norm_residual(yT, xb, gg_idx, blk, dst_dram, sqb, tx, rs, dkey):
            bsl = slice(blk * 512, (blk + 1) * 512)
            ACT(rs[:], ps[7][:], AF.Sqrt, reads=[PK[7]], writes=["rs"], scale=1.0 / D, bias=EPS)
            S.op("dve", lambda e: e.reciprocal(out=rs[:], in_=rs[:]), ["rs"], ["rs"])
            for c in range(16):
                q = c % 2
                TT(tx[q][:], yT[:, c, :], rs[:], ALU.mult, reads=["yT", "rs"], writes=[("tx", q)])
                STT(xb[:, c, :], tx[q][:], der[:, gg_idx, c:c + 1], xb[:, c, :], ALU.mult, ALU.add,
                    reads=[("tx", q), "xb"], writes=["xb"])
            DMA("sp", fm(dst_dram)[:, :, bsl], xb[:], reads=["xb"], writes=[dkey])

        with contextlib.ExitStack() as es:
            mg = sbt(es, "mg", [128, 16, T], BF16)
            yT = sbt(es, "yT", [128, 16, 512])
            xb = sbt(es, "xb", [128, 16, 512])
            sqb = [sbt(es, "sqb%d" % q, [128, 512], BF16) for q in range(2)]
            tx = [sbt(es, "tx%d" % q, [128, 512]) for q in range(2)]
            rs = sbt(es, "rs", [128, 512])
            DMA("sp", mg[:], fm(mg_d), writes=["mg"])
            for blk in range(2):
                bsl = slice(blk * 512, (blk + 1) * 512)
                DMA("sp", xb[:], fm(xo)[:, :, bsl], writes=["xb"])
                for m in range(16):
                    si = load_slab(w_o, 0, 16, m * 128)
                    q = m % 2
                    mm_slab(ps[q][:], si, 16, lambda k: mg[:, k, bsl], ["mg"], PK[q])
                    ACT(yT[:, m, :], ps[q][:], AF.Copy, reads=[PK[q]], writes=["yT"])
                    ACT(sqb[q][:], ps[q][:], AF.Square, reads=[PK[q]], writes=[("sqb", q)])
                    MM(ps[7][:], onesb[:], sqb[q][:], m == 0, m == 15, reads=[("sqb", q)], writes=[PK[7]])
                post_norm_residual(yT, xb, 3, blk, x1_d, sqb, tx, rs, "x1_d")
                if debug:
                    DMA("sp", fm(dbg["d_x1"])[:, :, bsl], xb[:], reads=["xb"], writes=["dd"])
            end_phase("M2")

        with contextlib.ExitStack() as es:
            yT = sbt(es, "yT", [128, 16, T])
            h2 = sbt(es, "h2", [128, 16, T], BF16)
            zq = sbt(es, "zq", [128, 16, T], BF16)
            xq = sbt(es, "xq", [128, 16, 256])
            sqb = [sbt(es, "sqb%d" % q, [128, 512], BF16) for q in range(2)]
            tx = [sbt(es, "tx%d" % q, [128, 512]) for q in range(2)]
            rs = sbt(es, "rs", [128, 512])
            for blk in range(2):
                bsl = slice(blk * 512, (blk + 1) * 512)
                xs_ = yT[:, :, bsl]
                DMA("sp", xs_, fm(x1_d)[:, :, bsl], writes=[("yT", blk)])
                for c in range(16):
                    q = c % 2
                    ACT(sqb[q][:], yT[:, c, bsl], AF.Square, reads=[("yT", blk)], writes=[("sqb", q)])
                    MM(ps[7][:], onesb[:], sqb[q][:], c == 0, c == 15, reads=[("sqb", q)], writes=[PK[7]])
                ACT(rs[:], ps[7][:], AF.Sqrt, reads=[PK[7]], writes=["rs"], scale=1.0 / D, bias=EPS)
                S.op("dve", lambda e: e.reciprocal(out=rs[:], in_=rs[:]), ["rs"], ["rs"])
                for c in range(16):
                    q = c % 2
                    TT(tx[q][:], yT[:, c, bsl], rs[:], ALU.mult, reads=[("yT", blk), "rs"], writes=[("tx", q)])
                    ACT(h2[:, c, bsl], tx[q][:], AF.Identity, reads=[("tx", q)], writes=[("h2", blk)], scale=gs_f(c), bias=sh_f(c))
            for qp in range(4):
                for m in range(16):
                    si = load_slab(w1, 0, 16, (qp * 16 + m) * 128)
                    for blk in range(2):
                        bsl = slice(blk * 512, (blk + 1) * 512)
                        b = 2 * (m % 2) + blk
                        mm_slab(ps[b][:], si, 16, lambda k: h2[:, k, bsl], [("h2", blk)], PK[b])
                        ACT(tx[blk][:], ps[b][:], AF.Relu, reads=[PK[b]], writes=[("tx", blk)])
                        TT(zq[:, m, bsl], tx[blk][:], tx[blk][:], ALU.mult, reads=[("tx", blk)], writes=[("zq", blk)])
                for mo in range(16):
                    si = load_slab(w2, qp * 2048, 16, mo * 128)
                    for blk in range(2):
                        bsl = slice(blk * 512, (blk + 1) * 512)
                        b = 4 + 2 * (mo % 2) + blk
                        mm_slab(ps[b][:], si, 16, lambda k: zq[:, k, bsl], [("zq", blk)], PK[b])
                        if qp == 0:
                            ACT(yT[:, mo, bsl], ps[b][:], AF.Copy, reads=[PK[b], ("h2", blk)], writes=[("yT", blk)])
                        else:
                            TT(yT[:, mo, bsl], ps[b][:], yT[:, mo, bsl], ALU.add, reads=[PK[b], ("yT", blk)], writes=[("yT", blk)])
            for sbk in range(4):
                ssl = slice(sbk * 256, (sbk + 1) * 256)
                blk = sbk // 2
                DMA("sp", xq[:], fm(x1_d)[:, :, ssl], writes=["xq"])
                for c in range(16):
                    q = c % 2
                    ACT(sqb[q][:, 0:256], yT[:, c, ssl], AF.Square, reads=[("yT", blk)], writes=[("sqb", q)])
                    MM(ps[7][:, 0:256], onesb[:], sqb[q][:, 0:256], c == 0, c == 15, reads=[("sqb", q)], writes=[PK[7]])
                ACT(rs[:, 0:256], ps[7][:, 0:256], AF.Sqrt, reads=[PK[7]], writes=["rs"], scale=1.0 / D, bias=EPS)
                S.op("dve", lambda e: e.reciprocal(out=rs[:, 0:256], in_=rs[:, 0:256]), ["rs"], ["rs"])
                for c in range(16):
                    q = c % 2
                    TT(tx[q][:, 0:256], yT[:, c, ssl], rs[:, 0:256], ALU.mult, reads=[("yT", blk), "rs"], writes=[("tx", q)])
                    STT(xq[:, c, :], tx[q][:, 0:256], der[:, 4, c:c + 1], xq[:, c, :], ALU.mult, ALU.add,
                        reads=[("tx", q), "xq"], writes=["xq"])
                DMA("sp", fm(out)[:, :, ssl], xq[:], reads=["xq"], writes=["out"])
            S.emit_phase(final=True)
    return nc


def _cols(v):
    v = np.asarray(v, np.float32)
    return np.ascontiguousarray(v.reshape(-1, 128).T)


def _consts():
    c = np.zeros((128, 384), np.float32)
    c[:, 0:128] = np.eye(128, dtype=np.float32)
    s = np.arange(128)[:, None]
    t = np.arange(128)[None, :]
    same = (s // 64) == (t // 64)
    c[:, 128:256] = (same & (s <= t)).astype(np.float32)
    c[:, 256:384] = (same & (s >= t)).astype(np.float32)
    return c


def _prep(inp):
    f = lambda k: np.asarray(inp[k], np.float32)
    x, c, ctx, c_ctx = f("x"), f("c"), f("ctx"), f("c_ctx")
    w_in = np.ascontiguousarray(f("w_in")[0])
    shared = {
        "w_mod": np.ascontiguousarray(f("w_mod")[0]), "w_in": w_in,
        "w_pw": np.ascontiguousarray(f("conv_pw_w")[0]), "w_ho": np.ascontiguousarray(f("hgrn_out_w")[0]),
        "w_o": np.ascontiguousarray(f("w_out")[0]), "w1": np.ascontiguousarray(f("mlp_w1")[0]),
        "w2": np.ascontiguousarray(f("mlp_w2")[0]), "cst": _consts(),
    }
    wf = [np.ascontiguousarray(w_in[:, C_FF:C_FF + DH]), np.ascontiguousarray(w_in[:, C_FB:C_FB + DH])]
    lbl = f("hgrn_lb_logits")
    cw = f("conv_dw_w")[0]
    in_maps = []
    for core in range(8):
        b, j = divmod(core, 4)
        P = np.zeros((128, NP), np.float32)

        def put(name, arr, i=0):
            o, w = PL[name]
            arr = np.asarray(arr, np.float32)
            P[:, o + i:o + i + arr.shape[1]] = arr
        put("npm", _cols(f("norm_pre_mix")[0]))
        put("npo", _cols(f("norm_post_mix")[0]))
        put("npl", _cols(f("norm_pre_mlp")[0]))
        put("npo2", _cols(f("norm_post_mlp")[0]))
        put("bmod", _cols(f("b_mod")[0]))
        put("cw", np.ascontiguousarray(cw.T.reshape(8, 128, 31).transpose(1, 0, 2).reshape(128, 8 * 31)))
        put("cb", _cols(f("conv_dw_b")[0]))
        put("lng", _cols(f("conv_ln_g")[0]))
        put("lnb", _cols(f("conv_ln_b")[0]))
        put("hng", _cols(f("hgrn_norm_g")[0]))
        slots = [("f", s) for s in range(j - 1, -1, -1)] + [("b", s) for s in range(j + 1, 4)]
        dirs = [0 if d == "f" else 1 for d, _ in slots]
        l0 = [lbl[0, 0], lbl[1, 0]] + [lbl[d, 0] for d in dirs]
        l1 = [lbl[0, 1], lbl[1, 1]] + [lbl[d, 1] for d in dirs]
        put("l0", np.concatenate([_cols(v) for v in l0], 1))
        put("l1", np.concatenate([_cols(v) for v in l1], 1))
        keep = [0.0] + [1.0 if dirs[s] == dirs[s - 1] else 0.0 for s in (1, 2)]
        put("slk", np.tile(np.array(keep, np.float32)[None], (128, 1)))
        put("sla", np.tile(np.array([1.0 - d for d in dirs], np.float32)[None], (128, 1)))
        put("slb", np.tile(np.array([float(d) for d in dirs], np.float32)[None], (128, 1)))
        put("hm", np.tile(np.array([1.0 if j > 0 else 0.0, 1.0 if j < 3 else 0.0], np.float32)[None], (128, 1)))
        cp = np.stack([_cols(c[b]), _cols(c_ctx)], 2).reshape(128, 32)
        put("cp", cp)
        xb_ = x[b]
        xo = np.ascontiguousarray(xb_[1024 * j:1024 * (j + 1)].T)
        xh = np.zeros((D, NHALO), np.float32)
        if j > 0:
            xh[:, 0:960] = xb_[1024 * j - 960:1024 * j].T
        if j < 3:
            xh[:, 960:1920] = xb_[1024 * (j + 1):1024 * (j + 1) + 960].T
        xs = np.zeros((3 * D, T), np.float32)
        wfs = np.zeros((3 * D, DH), np.float32)
        for s, (d, seg) in enumerate(slots):
            blk = xb_[1024 * seg:1024 * (seg + 1)]
            if d == "f":
                blk = blk[::-1]
            xs[s * D:(s + 1) * D] = blk.T
            wfs[s * D:(s + 1) * D] = wf[0 if d == "f" else 1]
        ctx2 = np.concatenate([ctx[b][::-1].T, ctx[b].T], 1)
        m = dict(shared)
        m.update({"xo": xo, "xh": xh, "xs": xs, "wfs": wfs, "ctx2": np.ascontiguousarray(ctx2), "prm": P})
        in_maps.append(m)
    return in_maps


_NC_CACHE = {}


def kernel(**inputs):
    in_maps = _prep(inputs)
    if "nc" not in _NC_CACHE:
        _NC_CACHE["nc"] = build()
    res = run_bass_kernel_spmd(_NC_CACHE["nc"], in_maps, core_ids=list(range(8)))
    out = np.zeros((2, 4096, D), np.float32)
    for core in range(8):
        b, j = divmod(core, 4)
        out[b, 1024 * j:1024 * (j + 1)] = np.asarray(res.results[core]["out"], np.float32).T
    return out
```
